# Optimizing a Trainium2 kernel written in Bass

```python
import jax, jax.numpy as jnp
from jax import lax
import numpy as np

D_MODEL = 1024
BATCH = 8
SEQ = 8192
DEPTH = 2

N_META = 16
CHUNK = 64
META_PAD = (-N_META) % CHUNK
DN_HEADS = 8
DN_DK = 128
DN_DV = 128
DN_QK = DN_HEADS * DN_DK
DN_VW = DN_HEADS * DN_DV
DN_CONV = 5
SC_WIDTH = 1024
SC_CONV = 3
D_FF = -(-8 * D_MODEL // (3 * 256)) * 256
RMS_EPS = 1e-6
L2_EPS = 1e-6

SPLIT_SIZES = [DN_QK, DN_QK, DN_VW, DN_VW, 2 * DN_HEADS, 2 * DN_HEADS,
               SC_WIDTH, SC_WIDTH, SC_WIDTH, D_MODEL, D_MODEL]
W_IN_COLS = sum(SPLIT_SIZES)
SPLIT_POINTS = list(np.cumsum(SPLIT_SIZES)[:-1].tolist())

kernel_name = "hybrid_deltanet_shortconv_encoder"


def rmsnorm(x, w):
    xf = x.astype(jnp.float32)
    y = xf * lax.rsqrt(jnp.mean(xf * xf, axis=-1, keepdims=True) + RMS_EPS)
    return (y * w.astype(jnp.float32)).astype(x.dtype)


def l2norm(t):
    return t * lax.rsqrt(jnp.sum(t * t, axis=-1, keepdims=True) + L2_EPS)


def depthwise_conv_centred(x, w):
    K, C = w.shape
    r = K // 2
    return lax.conv_general_dilated(
        x, w[:, None, :].astype(x.dtype), window_strides=(1,), padding=[(r, r)],
        dimension_numbers=("NWC", "WIO", "NWC"), feature_group_count=C)


def chunk_gated_delta_rule(q, k, v, g, beta):
    Bsz, T, H, dk = q.shape
    dv = v.shape[-1]
    N = T // CHUNK

    def chunks(t):
        return t.reshape(Bsz, N, CHUNK, H, t.shape[-1]).transpose(0, 1, 3, 2, 4)

    q, k, v = chunks(q), chunks(k), chunks(v)
    g = g.reshape(Bsz, N, CHUNK, H).transpose(0, 1, 3, 2)
    beta = beta.reshape(Bsz, N, CHUNK, H).transpose(0, 1, 3, 2)
    g = jnp.cumsum(g, axis=-1)

    tri = jnp.tril(jnp.ones((CHUNK, CHUNK), dtype=bool))
    strict = jnp.tril(jnp.ones((CHUNK, CHUNK), dtype=bool), -1)
    decay = jnp.exp(jnp.where(tri, g[..., :, None] - g[..., None, :], -jnp.inf))

    kk = jnp.einsum("bnhid,bnhjd->bnhij", k, k)
    lmat = jnp.where(strict, beta[..., :, None] * kk * decay, 0.0)
    a = lmat + jnp.eye(CHUNK, dtype=lmat.dtype)
    rhs = jnp.concatenate([v * beta[..., None], k * (beta * jnp.exp(g))[..., None]], axis=-1)
    sol = lax.linalg.triangular_solve(a, rhs, left_side=True, lower=True, unit_diagonal=True)
    u, w = sol[..., :dv], sol[..., dv:]

    qk = jnp.einsum("bnhid,bnhjd->bnhij", q, k) * decay
    q_dec = q * jnp.exp(g)[..., None]
    g_last = g[..., -1]
    k_dec = k * jnp.exp(g_last[..., None] - g)[..., None]

    xs = tuple(jnp.moveaxis(t, 1, 0) for t in (qk, q_dec, k_dec, u, w, jnp.exp(g_last)))

    def step(S, inp):
        qk_c, qd_c, kd_c, u_c, w_c, gl_c = inp
        v_new = u_c - jnp.einsum("bhcd,bhde->bhce", w_c, S)
        o_c = (jnp.einsum("bhcd,bhde->bhce", qd_c, S)
               + jnp.einsum("bhij,bhje->bhie", qk_c, v_new))
        S = S * gl_c[..., None, None] + jnp.einsum("bhcd,bhce->bhde", kd_c, v_new)
        return S, o_c

    S0 = jnp.zeros((Bsz, H, dk, dv), jnp.float32)
    _, o = lax.scan(step, S0, xs)
    return o.transpose(1, 0, 3, 2, 4).reshape(Bsz, T, H, dv)


def gated_deltanet_bidir(q, k, v, z, b_raw, a_raw, conv_w, A_log, dt_bias, norm_w):
    Bsz, L, _ = q.shape
    f32 = jnp.float32
    qkv = jax.nn.silu(depthwise_conv_centred(jnp.concatenate([q, k, v], axis=-1), conv_w)).astype(f32)
    qc, kc, vc = jnp.split(qkv, [DN_QK, 2 * DN_QK], axis=-1)
    qc = l2norm(qc.reshape(Bsz, L, DN_HEADS, DN_DK)) * (DN_DK ** -0.5)
    kc = l2norm(kc.reshape(Bsz, L, DN_HEADS, DN_DK))
    vc = vc.reshape(Bsz, L, DN_HEADS, DN_DV)
    beta = jax.nn.sigmoid(b_raw.astype(f32)).reshape(Bsz, L, 2, DN_HEADS)
    g = -jnp.exp(A_log.astype(f32)) * jax.nn.softplus(
        a_raw.astype(f32).reshape(Bsz, L, 2, DN_HEADS) + dt_bias.astype(f32))

    def pad(t):
        return jnp.pad(t, [(0, 0), (META_PAD, 0)] + [(0, 0)] * (t.ndim - 2))

    qp, kp, vp, bp, gp = pad(qc), pad(kc), pad(vc), pad(beta), pad(g)
    o_fwd = chunk_gated_delta_rule(qp, kp, vp, gp[:, :, 0], bp[:, :, 0])

    def flip(t):
        return jnp.flip(t, axis=1)

    o_bwd = flip(chunk_gated_delta_rule(flip(qp), flip(kp), flip(vp), flip(gp[:, :, 1]), flip(bp[:, :, 1])))
    o = (o_fwd + o_bwd)[:, META_PAD:]
    o = o * lax.rsqrt(jnp.mean(o * o, axis=-1, keepdims=True) + RMS_EPS) * norm_w.astype(f32)
    o = o * jax.nn.silu(z.reshape(Bsz, L, DN_HEADS, DN_DV).astype(f32))
    return o.reshape(Bsz, L, DN_VW).astype(z.dtype)


def setup_inputs(seed: int = 0) -> dict:
    key = jax.random.key(seed)
    ks = jax.random.split(key, 20)
    f32 = jnp.float32

    def nrm(k, shape, scale):
        return jax.random.normal(k, shape, f32) * scale

    x = jax.random.normal(ks[0], (BATCH, SEQ, D_MODEL), f32)
    meta_tokens = nrm(ks[1], (N_META, D_MODEL), 1.0)
    norm1_w = 1.0 + nrm(ks[2], (DEPTH, D_MODEL), 0.02)
    w_in = nrm(ks[3], (DEPTH, D_MODEL, W_IN_COLS), D_MODEL ** -0.5)
    dn_conv_w = nrm(ks[4], (DEPTH, DN_CONV, DN_QK * 2 + DN_VW), DN_CONV ** -0.5)
    A_log = jnp.log(jax.random.uniform(ks[5], (DEPTH, 2, DN_HEADS), f32, 1.0, 16.0))
    dt = jnp.exp(jax.random.uniform(ks[6], (DEPTH, 2, DN_HEADS), f32, np.log(1e-3), np.log(1e-1)))
    dt_bias = dt + jnp.log(-jnp.expm1(-dt))
    dn_norm_w = 1.0 + nrm(ks[7], (DEPTH, DN_DV), 0.02)
    sc_conv_w = nrm(ks[8], (DEPTH, SC_CONV, SC_WIDTH), SC_CONV ** -0.5)
    w_branch_dn = nrm(ks[9], (DEPTH, DN_VW, D_MODEL), DN_VW ** -0.5)
    w_branch_sc = nrm(ks[10], (DEPTH, SC_WIDTH, D_MODEL), SC_WIDTH ** -0.5)
    w_out = nrm(ks[11], (DEPTH, D_MODEL, D_MODEL), D_MODEL ** -0.5)
    norm2_w = 1.0 + nrm(ks[12], (DEPTH, D_MODEL), 0.02)
    w_gate_up = nrm(ks[13], (DEPTH, D_MODEL, 2 * D_FF), D_MODEL ** -0.5)
    w_down = nrm(ks[14], (DEPTH, D_FF, D_MODEL), D_FF ** -0.5)
    final_norm_w = 1.0 + nrm(ks[15], (D_MODEL,), 0.02)
    return {"x": x, "meta_tokens": meta_tokens, "norm1_w": norm1_w, "w_in": w_in,
            "dn_conv_w": dn_conv_w, "A_log": A_log, "dt_bias": dt_bias, "dn_norm_w": dn_norm_w,
            "sc_conv_w": sc_conv_w, "w_branch_dn": w_branch_dn, "w_branch_sc": w_branch_sc,
            "w_out": w_out, "norm2_w": norm2_w, "w_gate_up": w_gate_up, "w_down": w_down,
            "final_norm_w": final_norm_w}


def reference(x, meta_tokens, norm1_w, w_in, dn_conv_w, A_log, dt_bias, dn_norm_w,
              sc_conv_w, w_branch_dn, w_branch_sc, w_out, norm2_w, w_gate_up, w_down,
              final_norm_w):
    Bsz = x.shape[0]
    meta = jnp.broadcast_to(meta_tokens[None].astype(x.dtype), (Bsz, N_META, x.shape[-1]))
    h_res = jnp.concatenate([meta, x], axis=1)
    for l in range(DEPTH):
        h = rmsnorm(h_res, norm1_w[l])
        proj = h @ w_in[l]
        (q, k, v, z, b_raw, a_raw, sc_b, sc_c, sc_x, gate_a, gate_b) = jnp.split(proj, SPLIT_POINTS, axis=-1)
        o_dn = gated_deltanet_bidir(q, k, v, z, b_raw, a_raw, dn_conv_w[l], A_log[l], dt_bias[l], dn_norm_w[l])
        y_a = o_dn @ w_branch_dn[l]
        y_b = (sc_b * depthwise_conv_centred(sc_c * sc_x, sc_conv_w[l])) @ w_branch_sc[l]
        merged = jax.nn.sigmoid(gate_a) * y_a + jax.nn.sigmoid(gate_b) * y_b
        h_res = h_res + merged @ w_out[l]
        h2 = rmsnorm(h_res, norm2_w[l])
        gu = h2 @ w_gate_up[l]
        h_res = h_res + (jax.nn.silu(gu[..., :D_FF]) * gu[..., D_FF:]) @ w_down[l]
    out = rmsnorm(h_res, final_norm_w)
    return out[:, N_META:]
```

```python
import numpy as np
import ml_dtypes
from contextlib import ExitStack
import concourse.bass as bass
import concourse.mybir as mybir
from concourse.bass_utils import run_bass_kernel_spmd

F32 = mybir.dt.float32
BF16 = mybir.dt.bfloat16
AF = mybir.ActivationFunctionType
ALU = mybir.AluOpType

D = 1024
NCH = 8
H = 8
NMETA = 16
DFF = 2816
NFF = 22
WIN = 9248
RMS_EPS = 1e-6
L2_EPS = 1e-6
TW = 508
SEM_ROLL = 30000
CHAIN_FP32 = True
CHAIN_R = False
STQ = "sp"

C_Q, C_K, C_V, C_Z, C_B, C_A, C_SB, C_SC, C_SX, C_GA, C_GB = (
    0, 1024, 2048, 3072, 4096, 4112, 4128, 5152, 6176, 7200, 8224)


class Tl:
    def __init__(self, name, ap):
        self.name = name
        self.ap = ap
        self.w = None
        self.r = {}

    def __getitem__(self, idx):
        return self.ap[idx]


class Eng:
    def __init__(self, name, sems):
        self.name = name
        self.sems = sems
        self.si = 0
        self.cnt = 0
        self.ops = []
        self.waited = {}


class KB:
    def __init__(self, nc, es):
        self.nc = nc
        self.es = es
        self.semh = []
        self.eng = {}
        for n in ("pe", "act", "dve", "pool", "sp"):
            ids = [self._newsem(f"s_{n}{i}") for i in range(3)]
            self.eng[n] = Eng(n, ids)
        self.dq = {}
        for q, cnt in (("sp", 40), ("pool", 4), ("act", 36)):
            self.dq[q] = dict(ids=[self._newsem(f"d_{q}{i}") for i in range(cnt)],
                              cnt=[0] * cnt, nxt=0)
        self.all_tokens = {}
        self.ntile = 0

    def _newsem(self, name):
        h = self.es.enter_context(self.nc.semaphore(name))
        self.semh.append(h)
        return len(self.semh) - 1

    def sb(self, name, shape, dt):
        t = self.es.enter_context(self.nc.sbuf_tensor(name, list(shape), dt))
        return Tl(name, t)

    def arena_init(self, nbytes):
        self.arena = self.es.enter_context(self.nc.sbuf_tensor("arena", [128, nbytes // 2], BF16))
        self.arena_size = nbytes
        self.arena_off = 0

    def arena_reset(self):
        self.arena_off = 0

    def ar(self, name, shape, dt):
        esz = 4 if dt == F32 else 2
        n = 1
        for d_ in shape[1:]:
            n *= d_
        nb = (n * esz + 31) // 32 * 32
        off = self.arena_off
        assert off + nb <= self.arena_size, f"arena overflow at {name}: {off + nb}"
        self.arena_off += nb
        ap = self.arena[0:shape[0], off // 2:(off + n * esz) // 2]
        if dt == F32:
            ap = ap.bitcast(F32)
        if len(shape) == 3:
            ap = ap.rearrange("p (a b) -> p a b", a=shape[1])
        elif len(shape) == 4:
            ap = ap.rearrange("p (a b c) -> p a b c", a=shape[1], b=shape[2])
        return Tl(name, ap)

    def ps(self, name, shape, dt=F32):
        t = self.es.enter_context(self.nc.psum_tensor(name, list(shape), dt))
        return Tl(name, t)

    def _deps(self, e, r, w, extra=()):
        waits = {}

        def add(tok):
            if tok is None:
                return
            s, v = tok
            if waits.get(s, 0) < v:
                waits[s] = v
        for t in r:
            add(t.w)
        for t in w:
            add(t.w)
            for s, v in t.r.items():
                add((s, v))
        for tok in extra:
            add(tok)
        need = []
        for s, v in waits.items():
            if e.waited.get(s, 0) < v:
                e.waited[s] = v
                need.append((s, v))
        return need

    def _mark(self, tok, r, w):
        s, v = tok
        for t in r:
            if t.r.get(s, 0) < v:
                t.r[s] = v
        for t in w:
            t.w = tok
            t.r = {}
        if self.all_tokens.get(s, 0) < v:
            self.all_tokens[s] = v

    def op(self, engine, fn, r=(), w=(), extra=()):
        e = self.eng[engine]
        need = self._deps(e, r, w, extra)
        if e.cnt >= SEM_ROLL:
            e.si += 1
            e.cnt = 0
        e.cnt += 1
        sid = e.sems[e.si]
        tok = (sid, e.cnt)
        e.ops.append((need, fn, sid, 1))
        self._mark(tok, r, w)
        return tok

    def dma(self, queue, out, in_, r=(), w=(), extra=()):
        e = self.eng[queue]
        dq = self.dq[queue]
        i = dq["nxt"]
        dq["nxt"] = (i + 1) % len(dq["ids"])
        sid = dq["ids"][i]
        prev = (sid, dq["cnt"][i]) if dq["cnt"][i] else None
        need = self._deps(e, r, w, tuple(extra) + ((prev,) if prev else ()))
        dq["cnt"][i] += 16
        tok = (sid, dq["cnt"][i])
        e.ops.append((need, lambda g, o=out, s=in_: g.dma_start(out=o, in_=s), sid, 16))
        self._mark(tok, r, w)
        return tok

    def barrier(self):
        toks = list(self.all_tokens.items())
        for e in self.eng.values():
            need = []
            for s, v in toks:
                if e.waited.get(s, 0) < v:
                    e.waited[s] = v
                    need.append((s, v))
            if need:
                e.ops.append((need, None, None, 0))

    def replay(self, engine, g):
        for need, fn, sid, inc in self.eng[engine].ops:
            for s, v in need:
                g.wait_ge(self.semh[s], v)
            if fn is None:
                continue
            ins = fn(g)
            if inc:
                ins.then_inc(self.semh[sid], inc)

    def mm(self, out_t, out_ap, pairs, r=(), transpose=False):
        n = len(pairs)

        def fn(g, out_ap=out_ap, pairs=pairs, n=n):
            ins = None
            for i, (a, b) in enumerate(pairs):
                ins = g.matmul(out_ap, lhsT=a, rhs=b, start=(i == 0), stop=(i == n - 1))
            return ins
        return self.op("pe", fn, r=r, w=(out_t,))

    def mm_multi(self, out_t, groups, r=()):
        def fn(g, groups=groups):
            ins = None
            for (o, a, b, tr) in groups:
                if tr:
                    ins = g.transpose(o, a, b)
                else:
                    ins = g.matmul(o, lhsT=a, rhs=b, start=True, stop=True)
            return ins
        return self.op("pe", fn, r=r, w=(out_t,))

    def act(self, out, in_, func, r=(), w=(), bias=None, scale=None, eng="act"):
        kw = {}
        if bias is not None:
            kw["bias"] = bias
        if scale is not None:
            kw["scale"] = scale
        return self.op(eng, lambda g, o=out, i=in_, f=func, kw=kw: g.activation(out=o, in_=i, func=f, **kw),
                       r=r, w=w)

    def tt(self, eng, out, in0, in1, op, r=(), w=()):
        return self.op(eng, lambda g, o=out, a=in0, b=in1, p=op: g.tensor_tensor(out=o, in0=a, in1=b, op=p),
                       r=r, w=w)

    def stt(self, eng, out, in0, scalar, in1, op0, op1, r=(), w=()):
        return self.op(eng, lambda g, o=out, a=in0, s=scalar, b=in1, p0=op0, p1=op1:
                       g.scalar_tensor_tensor(out=o, in0=a, scalar=s, in1=b, op0=p0, op1=p1), r=r, w=w)

    def ts(self, eng, out, in0, s1, s2, op0, op1=None, r=(), w=()):
        if op1 is None:
            return self.op(eng, lambda g, o=out, a=in0, s=s1, p0=op0:
                           g.tensor_scalar(out=o, in0=a, scalar1=s, scalar2=None, op0=p0), r=r, w=w)
        return self.op(eng, lambda g, o=out, a=in0, x=s1, y=s2, p0=op0, p1=op1:
                       g.tensor_scalar(out=o, in0=a, scalar1=x, scalar2=y, op0=p0, op1=p1), r=r, w=w)

    def copy(self, eng, out, in_, r=(), w=()):
        if eng == "act":
            return self.op(eng, lambda g, o=out, i=in_: g.copy(out=o, in_=i), r=r, w=w)
        return self.op(eng, lambda g, o=out, i=in_: g.tensor_copy(out=o, in_=i), r=r, w=w)

    def recip(self, out, in_, r=(), w=()):
        return self.op("dve", lambda g, o=out, i=in_: g.reciprocal(out=o, in_=i), r=r, w=w)

    def memset(self, eng, ap, val, w=()):
        return self.op(eng, lambda g, a=ap, v=val: g.memset(a, v), w=w)


def bc(ap, shape):
    return ap.to_broadcast(list(shape))


class Cfg:
    def __init__(self, seq, depth):
        self.seq = seq
        self.depth = depth
        self.L = seq + NMETA
        self.PADF = (-self.L) % 128
        self.T = self.L + self.PADF
        self.NCK = self.T // 128
        self.XOFF = 2
        self.XW = self.L + 4
        self.tiles = []
        t0 = 0
        while t0 < self.L:
            w = min(TW, self.L - t0)
            self.tiles.append((t0, w))
            t0 += w


def build(cfg, debug=False):
    nc = bass.Bass("TRN2", target_bir_lowering=False)
    L, T, DEPTH = cfg.L, cfg.T, cfg.depth
    es = ExitStack()
    k = KB(nc, es)

    def din(name, shape, dt=F32):
        return nc.dram_tensor(name, list(shape), dt, kind="ExternalInput").ap()

    def dscr(name, shape, dt):
        kind = "ExternalOutput" if debug else "Internal"
        return nc.dram_tensor(name, list(shape), dt, kind=kind).ap()

    xin = din("xin", [L, D])
    norm1_w = din("norm1_w", [DEPTH, D])
    w_in = din("w_in", [DEPTH, D, WIN])
    dn_conv_w = din("dn_conv_w", [DEPTH, 5, 3072])
    A_log = din("A_log", [DEPTH, 16])
    dt_bias = din("dt_bias", [DEPTH, 16])
    dn_norm_w = din("dn_norm_w", [DEPTH, 128])
    sc_conv_w = din("sc_conv_w", [DEPTH, 3, 1024])
    w_bdn = din("w_branch_dn", [DEPTH, D, D])
    w_bsc = din("w_branch_sc", [DEPTH, D, D])
    w_out = din("w_out", [DEPTH, D, D])
    norm2_w = din("norm2_w", [DEPTH, D])
    w_gu = din("w_gate_up", [DEPTH, D, 2 * DFF])
    w_down = din("w_down", [DEPTH, DFF, D])
    final_w = din("final_norm_w", [D])
    c_ident_f = din("c_ident_f", [128, 128])
    c_masks = din("c_masks", [8, 128, 128])
    out = nc.dram_tensor("out", [cfg.seq, D], F32, kind="ExternalOutput").ap()

    wb_in = dscr("wb_in", [DEPTH, D, WIN], BF16)
    wb_bdn = dscr("wb_bdn", [DEPTH, D, D], BF16)
    wb_bsc = dscr("wb_bsc", [DEPTH, D, D], BF16)
    wb_out = dscr("wb_out", [DEPTH, D, D], BF16)
    wb_gu = dscr("wb_gu", [DEPTH, D, 2 * DFF], BF16)
    wb_down = dscr("wb_down", [DEPTH, DFF, D], BF16)
    xT = dscr("xT", [D, cfg.XW], F32)
    qT = dscr("qT", [D, T], BF16)
    kT = dscr("kT", [D, T], BF16)
    vT = dscr("vT", [D, T], BF16)
    gbT = dscr("gbT", [32, T], F32)
    zsT = dscr("zsT", [D, T], BF16)
    yscT = dscr("yscT", [D, T], BF16)
    gaT = dscr("gaT", [D, T], BF16)
    gbgT = dscr("gbgT", [D, T], BF16)
    oT = [dscr(f"oT{d}", [D, T], BF16) for d in range(2)]

    ident_f = k.sb("ident_f", [128, 128], F32)
    ident_b = k.sb("ident_b", [128, 128], BF16)
    ones_b = k.sb("ones_b", [128, 128], BF16)
    ones_f = k.sb("ones_f", [128, 128], F32)
    masks = k.sb("masks", [128, 8, 128], F32)
    zero_b = k.sb("zero_b", [128, 1024], BF16)
    zero_f = k.sb("zero_f", [128, 512], F32)
    k.dma("sp", ident_f[:], c_ident_f[:, :], w=(ident_f,))
    k.dma("sp", masks[:], c_masks.rearrange("m p n -> p m n"), w=(masks,))
    k.copy("dve", ident_b[:], ident_f[:], r=(ident_f,), w=(ident_b,))
    k.memset("dve", ones_b[:], 1.0, w=(ones_b,))
    k.memset("dve", ones_f[:], 1.0, w=(ones_f,))
    k.memset("pool", zero_b[:], 0.0, w=(zero_b,))
    k.memset("pool", zero_f[:], 0.0, w=(zero_f,))

    def load_vec(name, src_ap, nchunk):
        t = k.sb(name, [128, nchunk], F32)
        k.dma("sp", t[:], src_ap.rearrange("(c p) -> p c", p=128), w=(t,))
        return t

    nc_allow = nc.allow_non_contiguous_dma(reason="tiny parameter vectors")
    es.enter_context(nc_allow)

    n1w = [load_vec(f"n1w{l}", norm1_w[l], 8) for l in range(DEPTH)]
    n2w = [load_vec(f"n2w{l}", norm2_w[l], 8) for l in range(DEPTH)]
    fw = load_vec("fw", final_w, 8)
    dcw = []
    scw = []
    dnw = []
    nAexp = []
    dtb = []
    for l in range(DEPTH):
        t = k.sb(f"dcw{l}", [128, 5, 24], F32)
        k.dma("sp", t[:], dn_conv_w[l].rearrange("d (c p) -> p d c", p=128), w=(t,))
        dcw.append(t)
        t = k.sb(f"scw{l}", [128, 3, 8], F32)
        k.dma("sp", t[:], sc_conv_w[l].rearrange("d (c p) -> p d c", p=128), w=(t,))
        scw.append(t)
        t = k.sb(f"dnw{l}", [128, 1], F32)
        k.dma("sp", t[:], dn_norm_w[l].rearrange("(p o) -> p o", o=1), w=(t,))
        dnw.append(t)
        ta = k.sb(f"alog{l}", [16, 1], F32)
        k.dma("sp", ta[:], A_log[l].rearrange("(p o) -> p o", o=1), w=(ta,))
        tb = k.sb(f"dtb{l}", [16, 1], F32)
        k.dma("sp", tb[:], dt_bias[l].rearrange("(p o) -> p o", o=1), w=(tb,))
        dtb.append(tb)
        te = k.sb(f"nAexp{l}", [16, 1], F32)
        k.act(te[:], ta[:], AF.Exp, r=(ta,), w=(te,))
        k.ts("dve", te[:], te[:], -1.0, None, ALU.mult, r=(te,), w=(te,))
        nAexp.append(te)
    eps_t = k.sb("eps_t", [128, 1], F32)
    k.memset("dve", eps_t[:], RMS_EPS, w=(eps_t,))
    eps128_t = k.sb("eps128_t", [128, 1], F32)
    k.memset("dve", eps128_t[:], 128.0 * L2_EPS, w=(eps128_t,))
    one_t = k.sb("one_t", [128, 1], F32)
    k.memset("dve", one_t[:], 1.0, w=(one_t,))

    k.arena_init(196 * 1024)
    CW = 4096
    cf = [k.ar(f"castf{i}", [128, CW], F32) for i in range(3)]
    cb = [k.ar(f"castb{i}", [128, CW], BF16) for i in range(3)]
    cidx = [0]

    def cast_w(dst, src, rows, cols):
        for l in range(DEPTH):
            for r0 in range(0, rows, 128):
                for c0 in range(0, cols, CW):
                    cw = min(CW, cols - c0)
                    i = cidx[0] % 3
                    cidx[0] += 1
                    k.dma("sp", cf[i][:, 0:cw], src[l, r0:r0 + 128, c0:c0 + cw], w=(cf[i],))
                    eng = ("act", "dve", "pool")[i]
                    k.copy(eng, cb[i][:, 0:cw], cf[i][:, 0:cw], r=(cf[i],), w=(cb[i],))
                    k.dma(STQ, dst[l, r0:r0 + 128, c0:c0 + cw], cb[i][:, 0:cw], r=(cb[i],))
    cast_w(wb_in, w_in, D, WIN)
    cast_w(wb_bdn, w_bdn, D, D)
    cast_w(wb_bsc, w_bsc, D, D)
    cast_w(wb_out, w_out, D, D)
    cast_w(wb_gu, w_gu, D, 2 * DFF)
    cast_w(wb_down, w_down, DFF, D)
    k.barrier()

    PADF = cfg.PADF
    if PADF:
        for arr in (qT, kT, vT):
            k.dma("sp", arr.rearrange("(c p) n -> p c n", p=128)[:, :, 0:PADF],
                  zero_b[:, 0:8 * PADF].rearrange("p (c n) -> p c n", c=8), r=(zero_b,))
        k.dma("sp", gbT[:, 0:PADF], zero_f[0:32, 0:PADF], r=(zero_f,))
    xTv = xT.rearrange("(c p) n -> p c n", p=128)
    k.dma("sp", xTv[:, :, 0:2], zero_f[:, 0:16].rearrange("p (c n) -> p c n", c=8), r=(zero_f,))
    k.dma("sp", xTv[:, :, L + 2:L + 4], zero_f[:, 0:16].rearrange("p (c n) -> p c n", c=8), r=(zero_f,))

    PS = [k.ps(f"ps{i}", [128, 512], F32) for i in range(8)]
    psi = [0]

    def nextps():
        t = PS[psi[0] % 8]
        psi[0] += 1
        return t

    k.arena_reset()
    p0_in = [k.ar(f"p0in{i}", [128, 4, D], F32) for i in range(2)]
    p0_out = [k.ar(f"p0out{i}", [128, 8, 512], F32) for i in range(2)]
    it = 0
    for t0 in range(0, L, 512):
        n = min(512, L - t0)
        tin = p0_in[it % 2]
        tout = p0_out[it % 2]
        nb = (n + 127) // 128
        for b in range(nb):
            nn = min(128, n - b * 128)
            k.dma("sp", tin[0:nn, b, :], xin[t0 + b * 128:t0 + b * 128 + nn, :], w=(tin,))
        for c in range(8):
            pt = nextps()
            groups = []
            for b in range(nb):
                nn = min(128, n - b * 128)
                groups.append((pt[:, b * 128:b * 128 + nn], tin[0:nn, b, c * 128:(c + 1) * 128],
                               ident_f[0:nn, 0:nn], True))
            k.mm_multi(pt, groups, r=(tin, ident_f))
            k.copy("act" if c % 2 else "dve", tout[:, c, 0:n], pt[:, 0:n], r=(pt,), w=(tout,))
        k.dma("sp", xTv[:, :, 2 + t0:2 + t0 + n], tout[:, :, 0:n], r=(tout,))
        it += 1
    k.barrier()

    NCMAX = TW + 4

    def nxt(lst, ctr):
        t = lst[ctr[0] % len(lst)]
        ctr[0] += 1
        return t

    psfree = []

    def ps_alloc():
        return psfree.pop(0)

    def ps_free(t):
        psfree.append(t)

    def run_pipeline(tasks, depth, budget=7):
        psfree[:] = list(PS)
        live = []
        pending = list(tasks)
        pi = 0
        while True:
            while pi < len(pending) and len(live) < depth:
                tk = pending[pi]
                nb = getattr(tk, "nb", 0)
                if sum(n for _, n in live) + nb > budget:
                    break
                pi += 1
                g_ = tk()
                if g_ is not None:
                    try:
                        next(g_)
                        live.append((g_, nb))
                    except StopIteration:
                        pass
            if not live:
                if pi >= len(pending):
                    break
                continue
            for ent in list(live):
                try:
                    next(ent[0])
                except StopIteration:
                    live.remove(ent)

    def phaseA(l):
        k.arena_reset()
        xts = [k.ar(f"xt{i}", [128, 8, NCMAX], F32) for i in range(1)]
        hTs = [k.ar(f"hT{i}", [128, 8, NCMAX], BF16) for i in range(2)]
        sq = k.ar("sq", [128, 8, NCMAX], BF16)
        rstd = k.ar("rstd", [128, NCMAX], F32)
        wbuf = [k.ar(f"wbuf{i}", [128, 8, 1024], BF16) for i in range(4)]
        wsm = k.ar("wsm", [128, 8, 32], BF16)
        stg = [k.ar(f"stg{i}", [128, 8, TW], BF16) for i in range(3)]
        tmpA = [k.ar(f"tmpA{i}", [128, NCMAX], F32) for i in range(8)]
        ssq8 = [k.ar(f"ssq8{i}", [128, 8, TW], F32) for i in range(2)]
        tmpB = [k.ar(f"tmpB{i}", [128, NCMAX], BF16) for i in range(6)]

        def ta():
            return tmpA.pop(0)

        def tb():
            return tmpB.pop(0)
        gbs = k.ar("gbs", [16, 2, NCMAX], F32)
        print("arena phase A bytes", k.arena_off)
        wl = wb_in[l]
        fm = lambda arr: arr.rearrange("(c p) n -> p c n", p=128)
        BLK = [C_Q, C_K, C_V, C_Z, C_SC, C_SX, C_SB, C_GA, C_GB]
        nblk = len(BLK)
        wslot = {}
        gblk = [0]

        def t_wload(gi):
            def f():
                wt = wbuf[gi % 4]
                k.dma("sp", wt[:, :, :], wl[:, BLK[gi % nblk]:BLK[gi % nblk] + 1024].rearrange("(c p) n -> p c n", p=128),
                      w=(wt,))
                wslot[gi] = wt
            return f

        def t_xload(ti):
            def f():
                t0, W = cfg.tiles[ti]
                NC = W + 4
                k.dma("sp", xts[0][:, :, 0:NC], xTv[:, :, t0:t0 + NC], w=(xts[0],))
            return f

        def t_pro1(ti):
            def f():
                t0, W = cfg.tiles[ti]
                NC = W + 4
                xt = xts[0]
                k.act(sq[:, :, 0:NC], xt[:, :, 0:NC], AF.Square, r=(xt,), w=(sq,))
            return f

        def t_pro2(ti):
            def f():
                t0, W = cfg.tiles[ti]
                NC = W + 4
                xt, hT = xts[0], hTs[ti % 2]
                pt = ps_alloc()
                k.mm(pt, pt[:, 0:NC], [(ones_b[:], sq[:, c, 0:NC]) for c in range(8)], r=(sq, ones_b))
                k.act(rstd[:, 0:NC], pt[:, 0:NC], AF.Sqrt, r=(pt, eps_t), w=(rstd,), bias=eps_t[:], scale=1.0 / D)
                ps_free(pt)
                k.recip(rstd[:, 0:NC], rstd[:, 0:NC], r=(rstd,), w=(rstd,))
                for c in range(8):
                    if c % 2:
                        k.stt("dve", hT[:, c, 0:NC], xt[:, c, 0:NC], n1w[l][:, c:c + 1],
                              rstd[:, 0:NC], ALU.mult, ALU.mult, r=(xt, rstd, n1w[l]), w=(hT,))
                    else:
                        tp = ta()
                        k.ts("pool", tp[:, 0:NC], xt[:, c, 0:NC], n1w[l][:, c:c + 1], None, ALU.mult,
                             r=(xt, n1w[l]), w=(tp,))
                        k.tt("pool", hT[:, c, 0:NC], tp[:, 0:NC], rstd[:, 0:NC], ALU.mult, r=(tp, rstd), w=(hT,))
                        tmpA.append(tp)
            return f

        def proj(hT, NC, wt, j, M=128):
            pt = ps_alloc()
            k.mm(pt, pt[0:M, 0:NC], [(wt[:, c, j:j + M], hT[:, c, 0:NC]) for c in range(8)], r=(wt, hT))
            return pt

        class Grp:
            def __init__(self, st, dst, pcol, W, n=8, norm=None, ssq=None):
                self.st, self.dst, self.pcol, self.W, self.left = st, dst, pcol, W, n
                self.norm, self.ssq = norm, ssq

            def done(self):
                self.left -= 1
                if self.left == 0:
                    W = self.W
                    if self.norm is not None:
                        sq_ = self.ssq
                        if self.norm == 0:
                            k.act(sq_[:, :, 0:W], sq_[:, :, 0:W], AF.Sqrt, r=(sq_, eps128_t), w=(sq_,),
                                  bias=eps128_t[:], scale=128.0)
                        else:
                            k.act(sq_[:, :, 0:W], sq_[:, :, 0:W], AF.Sqrt, r=(sq_, eps_t), w=(sq_,),
                                  bias=eps_t[:], scale=1.0)
                        k.recip(sq_[:, :, 0:W], sq_[:, :, 0:W], r=(sq_,), w=(sq_,))
                        k.tt("pool", self.st[:, :, 0:W], self.st[:, :, 0:W], sq_[:, :, 0:W], ALU.mult,
                             r=(self.st, sq_), w=(self.st,))
                    k.dma(STQ, fm(self.dst)[:, :, self.pcol:self.pcol + W], self.st[:, :, 0:W],
                          r=(self.st,))

        def t_qkv(ti, gi, grp, c, G):
            def gen():
                t0, W = cfg.tiles[ti]
                NC = W + 4
                hT = hTs[ti % 2]
                wt = wslot[gi]
                st = G.st
                pt = proj(hT, NC, wt, c * 128)
                yield
                cc = grp * 8 + c
                acc = ta()
                k.ts("dve", acc[:, 0:W], pt[:, 0:W], dcw[l][:, 0, cc:cc + 1], None, ALU.mult,
                     r=(pt, dcw[l]), w=(acc,))
                for d in range(1, 5):
                    k.stt("dve", acc[:, 0:W], pt[:, d:d + W], dcw[l][:, d, cc:cc + 1], acc[:, 0:W],
                          ALU.mult, ALU.add, r=(pt, dcw[l], acc), w=(acc,))
                ps_free(pt)
                yield
                if grp == 2:
                    k.act(st[:, c, 0:W], acc[:, 0:W], AF.Silu, r=(acc,), w=(st,))
                    tmpA.append(acc)
                    G.done()
                    return
                k.act(st[:, c, 0:W], acc[:, 0:W], AF.Silu, r=(acc,), w=(st,))
                tmpA.append(acc)
                s2 = tb()
                k.act(s2[:, 0:W], st[:, c, 0:W], AF.Square, r=(st,), w=(s2,))
                yield
                p2 = ps_alloc()
                k.mm(p2, p2[:, 0:W], [(ones_b[:], s2[:, 0:W])], r=(s2, ones_b))
                tmpB.append(s2)
                yield
                k.copy("act", G.ssq[:, c, 0:W], p2[:, 0:W], r=(p2,), w=(G.ssq,))
                ps_free(p2)
                G.done()
            gen.nb = 1
            return gen

        def t_simple(ti, gi, c, G, func):
            def gen():
                t0, W = cfg.tiles[ti]
                NC = W + 4
                pt = proj(hTs[ti % 2], NC, wslot[gi], c * 128)
                yield
                k.act(G.st[:, c, 0:W], pt[:, 2:2 + W], func, r=(pt,), w=(G.st,))
                ps_free(pt)
                G.done()
            gen.nb = 1
            return gen

        def t_bg(ti):
            def gen():
                t0, W = cfg.tiles[ti]
                NC = W + 4
                pcol = t0 + PADF
                hT = hTs[ti % 2]
                k.dma("sp", wsm[:, :, :], wl[:, C_B:C_B + 32].rearrange("(c p) n -> p c n", p=128), w=(wsm,))
                pts = []
                for which in range(2):
                    pt = ps_alloc()
                    k.mm(pt, pt[0:16, 0:NC], [(wsm[:, c, which * 16:which * 16 + 16], hT[:, c, 0:NC])
                                              for c in range(8)], r=(wsm, hT))
                    pts.append(pt)
                yield
                k.act(gbs[:, 0, 0:W], pts[0][0:16, 2:2 + W], AF.Sigmoid, r=(pts[0],), w=(gbs,))
                k.act(gbs[:, 1, 0:W], pts[1][0:16, 2:2 + W], AF.Exp, r=(pts[1], dtb[l]), w=(gbs,), bias=dtb[l][:])
                ps_free(pts[0])
                ps_free(pts[1])
                k.act(gbs[:, 1, 0:W], gbs[:, 1, 0:W], AF.Ln, r=(gbs, one_t), w=(gbs,), bias=one_t[0:16, :])
                yield
                k.ts("dve", gbs[:, 1, 0:W], gbs[:, 1, 0:W], nAexp[l][:, 0:1], None, ALU.mult,
                     r=(gbs, nAexp[l]), w=(gbs,))
                k.dma(STQ, gbT.rearrange("(a p) n -> p a n", p=16)[:, :, pcol:pcol + W], gbs[:, :, 0:W], r=(gbs,))
            gen.nb = 2
            return gen

        def t_sc(ti, gi_c, gi_x, gi_b, c, G):
            def gen():
                t0, W = cfg.tiles[ti]
                NC = W + 4
                hT = hTs[ti % 2]
                pc = proj(hT, NC, wslot[gi_c], c * 128)
                px = proj(hT, NC, wslot[gi_x], c * 128)
                pb = proj(hT, NC, wslot[gi_b], c * 128)
                yield
                cx = ta()
                k.copy("act", cx[:, 0:NC], pc[:, 0:NC], r=(pc,), w=(cx,))
                ps_free(pc)
                yield
                pr = ta()
                k.tt("dve", pr[:, 0:NC], px[:, 0:NC], cx[:, 0:NC], ALU.mult, r=(px, cx), w=(pr,))
                ps_free(px)
                tmpA.append(cx)
                yield
                acc = ta()
                k.ts("pool", acc[:, 0:W], pr[:, 1:1 + W], scw[l][:, 0, c:c + 1], None, ALU.mult,
                     r=(pr, scw[l]), w=(acc,))
                for d in range(1, 3):
                    t2 = ta()
                    k.ts("pool", t2[:, 0:W], pr[:, 1 + d:1 + d + W], scw[l][:, d, c:c + 1], None, ALU.mult,
                         r=(pr, scw[l]), w=(t2,))
                    k.tt("pool", acc[:, 0:W], acc[:, 0:W], t2[:, 0:W], ALU.add, r=(acc, t2), w=(acc,))
                    tmpA.append(t2)
                tmpA.append(pr)
                yield
                k.tt("dve", G.st[:, c, 0:W], pb[:, 2:2 + W], acc[:, 0:W], ALU.mult, r=(pb, acc), w=(G.st,))
                ps_free(pb)
                tmpA.append(acc)
                G.done()
            gen.nb = 3
            return gen

        tasks = []
        ntile = len(cfg.tiles)
        stgc = [0]
        total_blocks = ntile * nblk
        tasks.append(t_xload(0))
        for gi in range(4):
            tasks.append(t_wload(gi))
        tasks.append(t_pro1(0))
        tasks.append(t_pro2(0))
        for ti, (t0, W) in enumerate(cfg.tiles):
            pcol = t0 + PADF
            base = ti * nblk

            def post(gi):
                if gi + 4 < total_blocks:
                    tasks.append(t_wload(gi + 4))

            def newG(dst, n=8, norm=None, ssq=None):
                st = stg[stgc[0] % 3]
                stgc[0] += 1
                return Grp(st, dst, pcol, W, n, norm, ssq)
            for grp, dst in enumerate((qT, kT, vT)):
                G = newG(dst, norm=(grp if grp < 2 else None), ssq=(ssq8[grp] if grp < 2 else None))
                for c in range(8):
                    tasks.append(t_qkv(ti, base + grp, grp, c, G))
                post(base + grp)
                if grp == 1 and ti + 1 < ntile:
                    tasks.append(t_xload(ti + 1))
            G = newG(zsT)
            for c in range(8):
                tasks.append(t_simple(ti, base + 3, c, G, AF.Silu))
            post(base + 3)
            tasks.append(t_bg(ti))
            G = newG(yscT)
            for c in range(8):
                tasks.append(t_sc(ti, base + 4, base + 5, base + 6, c, G))
            post(base + 4)
            post(base + 5)
            post(base + 6)
            if ti + 1 < ntile:
                tasks.append(t_pro1(ti + 1))
            for bi, dst in ((7, gaT), (8, gbgT)):
                G = newG(dst)
                for c in range(8):
                    tasks.append(t_simple(ti, base + bi, c, G, AF.Sigmoid))
                post(base + bi)
                if bi == 7 and ti + 1 < ntile:
                    tasks.append(t_pro2(ti + 1))
        run_pipeline(tasks, 5)
        k.barrier()

    def run_threads(gens):
        live = list(gens)
        while live:
            for g_ in list(live):
                try:
                    next(g_)
                except StopIteration:
                    live.remove(g_)

    def v3(ap, a):
        return ap.rearrange("p (a b) -> p a b", a=a)

    def phaseB(l):
        k.arena_reset()
        NCK = cfg.NCK
        qTv = qT.rearrange("(c p) n -> p c n", p=128)
        kTv = kT.rearrange("(c p) n -> p c n", p=128)
        vTv = vT.rearrange("(c p) n -> p c n", p=128)
        B = {}
        CDT = F32 if CHAIN_FP32 else BF16
        identc = ident_f if CHAIN_FP32 else ident_b
        for d in range(2):
            rGa_ = k.ar(f"rGa{d}", [128, 8, 128], F32)
            rGi_ = k.ar(f"rGi{d}", [128, 8, 128], F32)
            for sl in range(2):
                B[d, sl] = dict(
                    kq=k.ar(f"kq{d}{sl}", [128, 8, 2, 128], BF16),
                    vt=k.ar(f"vt{d}{sl}", [128, 8, 128], BF16),
                    gbt=k.ar(f"gbt{d}{sl}", [32, 128], F32),
                    gb=k.ar(f"gb{d}{sl}", [128, 32], F32),
                    E=k.ar(f"E{d}{sl}", [128, 24], F32),
                    bege=k.ar(f"bege{d}{sl}", [128, 8], F32),
                    nbeta=k.ar(f"nbeta{d}{sl}", [128, 8], F32),
                    ost=k.ar(f"ost{d}{sl}", [128, 8, 128], BF16),
                    rGa=rGa_, rGi=rGi_,
                )
                for hh in range(2):
                    B[d, sl, hh] = dict(
                        qkm=k.ar(f"qkm{d}{sl}{hh}", [128, 4, 128], BF16),
                        kdec=k.ar(f"kdec{d}{sl}{hh}", [128, 4, 128], BF16),
                        u=k.ar(f"u{d}{sl}{hh}", [128, 4, 128], F32),
                        wT=k.ar(f"wT{d}{sl}{hh}", [128, 4, 128], BF16),
                        qdT=k.ar(f"qdT{d}{sl}{hh}", [128, 4, 128], BF16),
                    )
            for hh in range(2):
                B["t", d, hh] = dict(
                    Dx=k.ar(f"Dx{d}{hh}", [128, 4, 128], F32),
                    DTx=k.ar(f"DTx{d}{hh}", [128, 4, 128], F32),
                    egcb=k.ar(f"egcb{d}{hh}", [128, 4, 128], F32),
                    U=[k.ar(f"U{d}{hh}{i}", [128, 4, 128], CDT) for i in range(1 if CHAIN_FP32 else 2)],
                    W=[k.ar(f"W{d}{hh}{i}", [128, 4, 128], CDT) for i in range(1 if CHAIN_FP32 else 2)],
                    P=[k.ar(f"P{d}{hh}{i}", [128, 4, 128], CDT) for i in range(1 if CHAIN_FP32 else 2)],
                    Pb=k.ar(f"Pb{d}{hh}", [128, 4, 128], BF16),
                    ktok=k.ar(f"ktok{d}{hh}", [128, 4, 128], BF16),
                    bkg=k.ar(f"bkg{d}{hh}", [128, 4, 128], BF16),
                    bv=k.ar(f"bv{d}{hh}", [128, 4, 128], BF16),
                    vnew=k.ar(f"vnew{d}{hh}", [128, 4, 128], BF16),
                    Ssc=k.ar(f"Ssc{d}{hh}", [128, 4, 128], F32),
                    S=k.ar(f"S{d}{hh}", [128, 4, 128], F32),
                    Sb=k.ar(f"Sb{d}{hh}", [128, 4, 128], BF16),
                )
                if CHAIN_FP32:
                    tt_ = B["t", d, hh]
                    tt_["U"].append(tt_["Dx"])
                    tt_["W"].append(tt_["DTx"])
                    tt_["P"].append(tt_["egcb"])
                k.memset("pool", B["t", d, hh]["S"][:], 0.0, w=(B["t", d, hh]["S"],))
                k.memset("pool", B["t", d, hh]["Sb"][:], 0.0, w=(B["t", d, hh]["Sb"],))
        print("arena phase B bytes", k.arena_off)

        def setup(d, c, sl):
            b = B[d, sl]
            Mincl, Maft = masks[:, 2 * d, :], masks[:, 2 * d + 1, :]
            cs = slice(c * 128, (c + 1) * 128)
            k.dma("sp", b["kq"][:, :, 0, :], kTv[:, :, cs], w=(b["kq"],))
            k.dma("sp", b["kq"][:, :, 1, :], qTv[:, :, cs], w=(b["kq"],))
            k.dma("sp", b["vt"][:], vTv[:, :, cs], w=(b["vt"],))
            k.dma("sp", b["gbt"][:], gbT[:, cs], w=(b["gbt"],))
            yield
            pt = nextps()
            k.mm_multi(pt, [(pt[:, 0:32], b["gbt"][:], ident_f[0:32, 0:32], True)], r=(b["gbt"], ident_f))
            k.copy("dve", b["gb"][:], pt[:, 0:32], r=(pt,), w=(b["gb"],))
            yield
            beta = b["gb"][:, d * 8:d * 8 + 8]
            g = b["gb"][:, 16 + d * 8:16 + d * 8 + 8]
            pt = nextps()
            k.mm_multi(pt, [(pt[:, 0:8], Mincl, g, False), (pt[:, 8:16], Maft, g, False),
                            (pt[:, 16:24], ones_f[:], g, False)], r=(masks, ones_f, b["gb"]))
            k.act(b["E"][:], pt[:, 0:24], AF.Exp, r=(pt,), w=(b["E"],))
            k.tt("pool", b["rGa"][:], bc(masks[:, 2 * d + 1:2 * d + 2, :], [128, 8, 128]),
                 bc(g.unsqueeze(2), [128, 8, 128]), ALU.mult, r=(masks, b["gb"]), w=(b["rGa"],))
            k.tt("pool", b["rGi"][:], bc(masks[:, 2 * d:2 * d + 1, :], [128, 8, 128]),
                 bc(g.unsqueeze(2), [128, 8, 128]), ALU.mult, r=(masks, b["gb"]), w=(b["rGi"],))
            yield
            k.tt("dve", b["bege"][:], beta, b["E"][:, 0:8], ALU.mult, r=(b["gb"], b["E"]), w=(b["bege"],))
            k.ts("dve", b["nbeta"][:], beta, -1.0, None, ALU.mult, r=(b["gb"],), w=(b["nbeta"],))
            yield

        def prep(d, c, sl, hh):
            b = B[d, sl]
            bh = B[d, sl, hh]
            t = B["t", d, hh]
            hs = slice(4 * hh, 4 * hh + 4)
            Mincl, Maft = masks[:, 2 * d, :], masks[:, 2 * d + 1, :]
            Mincl_bc = bc(masks[:, 2 * d:2 * d + 1, :], [128, 4, 128])
            Maft_bc = bc(masks[:, 2 * d + 1:2 * d + 2, :], [128, 4, 128])
            beta = b["gb"][:, d * 8 + 4 * hh:d * 8 + 4 * hh + 4]
            cr = (lambda a: a.bitcast(mybir.dt.float32r)) if (CHAIN_FP32 and CHAIN_R) else (lambda a: a)
            pD = nextps()
            k.mm(pD, pD[:], [(Mincl, b["rGa"][:, hs, :])], r=(masks, b["rGa"]))
            k.act(v3(t["Dx"][:].rearrange("p a b -> p (a b)"), 4), v3(pD[:], 4), AF.Exp, r=(pD,), w=(t["Dx"],))
            pDT = nextps()
            k.mm(pDT, pDT[:], [(Maft, b["rGi"][:, hs, :])], r=(masks, b["rGi"]))
            k.act(t["DTx"][:], v3(pDT[:], 4), AF.Exp, r=(pDT,), w=(t["DTx"],))
            yield
            pG = nextps()
            k.mm(pG, pG[:], [(ones_f[:], b["rGi"][:, hs, :])], r=(ones_f, b["rGi"]))
            k.act(t["egcb"][:], v3(pG[:], 4), AF.Exp, r=(pG,), w=(t["egcb"],))
            k.tt("pool", t["Dx"][:], t["Dx"][:], Maft_bc, ALU.mult, r=(t["Dx"], masks), w=(t["Dx"],))
            k.tt("pool", t["Dx"][:], t["Dx"][:], bc(b["nbeta"][:, hs].unsqueeze(2), [128, 4, 128]), ALU.mult,
                 r=(t["Dx"], b["nbeta"]), w=(t["Dx"],))
            k.tt("pool", t["DTx"][:], t["DTx"][:], Mincl_bc, ALU.mult, r=(t["DTx"], masks), w=(t["DTx"],))
            yield
            W0 = t["W"][0]
            for pair in range(2):
                pk = nextps()
                groups = []
                for hl in range(2):
                    h = 4 * hh + 2 * pair + hl
                    groups.append((pk[:, hl * 256:(hl + 1) * 256], b["kq"][:, h, 0, :],
                                   b["kq"][:, h, :, :].rearrange("p a b -> p (a b)"), False))
                k.mm_multi(pk, groups, r=(b["kq"],))
                pkv = v3(pk[:], 2)
                k.tt("dve", cr(W0[:, 2 * pair:2 * pair + 2, :]), pkv[:, :, 0:128], t["Dx"][:, 2 * pair:2 * pair + 2, :],
                     ALU.mult, r=(pk, t["Dx"]), w=(W0,))
                k.tt("dve", bh["qkm"][:, 2 * pair:2 * pair + 2, :], pkv[:, :, 128:256],
                     t["DTx"][:, 2 * pair:2 * pair + 2, :], ALU.mult, r=(pk, t["DTx"]), w=(bh["qkm"],))
            yield
            U0 = t["U"][0]
            pt = nextps()
            ptb = v3(pt[:], 4) if CHAIN_FP32 else v3(pt[:].bitcast(BF16)[:, 0:512], 4)
            k.mm_multi(pt, [(ptb[:, hl, :], W0[:, hl, :], identc[:], True) for hl in range(4)], r=(W0, identc))
            k.copy("dve" if CHAIN_R else "act", cr(U0[:]), ptb, r=(pt,), w=(U0,))
            pt = nextps()
            ptb = v3(pt[:].bitcast(BF16)[:, 0:512], 4)
            k.mm_multi(pt, [(ptb[:, hl, :], b["kq"][:, 4 * hh + hl, 0, :], ident_b[:], True) for hl in range(4)],
                       r=(b["kq"], ident_b))
            k.copy("act", t["ktok"][:], ptb, r=(pt,), w=(t["ktok"],))
            pt = nextps()
            ptb = v3(pt[:].bitcast(BF16)[:, 0:512], 4)
            k.mm_multi(pt, [(ptb[:, hl, :], b["vt"][:, 4 * hh + hl, :], ident_b[:], True) for hl in range(4)],
                       r=(b["vt"], ident_b))
            k.tt("dve", t["bv"][:], ptb, bc(beta.unsqueeze(2), [128, 4, 128]), ALU.mult, r=(pt, b["gb"]),
                 w=(t["bv"],))
            yield
            k.tt("pool", t["bkg"][:], t["ktok"][:], bc(b["bege"][:, hs].unsqueeze(2), [128, 4, 128]), ALU.mult,
                 r=(t["ktok"], b["bege"]), w=(t["bkg"],))
            k.tt("pool", bh["kdec"][:], t["ktok"][:], bc(b["E"][:, 8 + 4 * hh:12 + 4 * hh].unsqueeze(2), [128, 4, 128]),
                 ALU.mult, r=(t["ktok"], b["E"]), w=(bh["kdec"],))
            k.tt("pool", bh["qdT"][:], b["kq"][:, hs, 1, :], t["egcb"][:], ALU.mult, r=(b["kq"], t["egcb"]),
                 w=(bh["qdT"],))
            k.tt("dve", cr(t["P"][0][:]), U0[:], bc(identc[:].unsqueeze(1), [128, 4, 128]), ALU.add,
                 r=(U0, identc), w=(t["P"][0],))
            yield
            for lev in range(6):
                Uc, Wc = t["U"][lev % 2], t["W"][lev % 2]
                Un, Wn = t["U"][(lev + 1) % 2], t["W"][(lev + 1) % 2]
                Pc, Pn = t["P"][lev % 2], t["P"][(lev + 1) % 2]
                pA = nextps()
                k.mm_multi(pA, [(pA[:, hl * 128:(hl + 1) * 128], cr(Wc[:, hl, :]), cr(Uc[:, hl, :]), False) for hl in range(4)],
                           r=(Wc, Uc))
                pB = nextps()
                k.mm_multi(pB, [(pB[:, hl * 128:(hl + 1) * 128], cr(Uc[:, hl, :]), cr(Wc[:, hl, :]), False) for hl in range(4)],
                           r=(Wc, Uc))
                k.copy("dve" if CHAIN_R else "act", cr(Un[:]), v3(pA[:], 4), r=(pA,), w=(Un,))
                k.copy("dve", cr(Wn[:]), v3(pB[:], 4), r=(pB,), w=(Wn,))
                yield
                pC = nextps()
                k.mm_multi(pC, [(pC[:, hl * 128:(hl + 1) * 128], cr(Wn[:, hl, :]), cr(Pc[:, hl, :]), False) for hl in range(4)],
                           r=(Wn, Pc))
                k.tt("dve", cr(Pn[:]), v3(pC[:], 4), Pc[:], ALU.add, r=(pC, Pc), w=(Pn,))
                yield
            Pf = t["P"][0]
            if CHAIN_FP32:
                k.copy("pool", t["Pb"][:], Pf[:], r=(Pf,), w=(t["Pb"],))
                Pf = t["Pb"]
            pu = nextps()
            k.mm_multi(pu, [(pu[:, hl * 128:(hl + 1) * 128], Pf[:, hl, :], t["bv"][:, hl, :], False) for hl in range(4)],
                       r=(Pf, t["bv"]))
            k.copy("act", bh["u"][:], v3(pu[:], 4), r=(pu,), w=(bh["u"],))
            pw = nextps()
            k.mm_multi(pw, [(pw[:, hl * 128:(hl + 1) * 128], t["bkg"][:, hl, :], Pf[:, hl, :], False) for hl in range(4)],
                       r=(Pf, t["bkg"]))
            k.copy("dve", bh["wT"][:], v3(pw[:], 4), r=(pw,), w=(bh["wT"],))
            yield

        def scan(d, c, sl, hh):
            b = B[d, sl]
            bh = B[d, sl, hh]
            t = B["t", d, hh]
            S, Sb = t["S"], t["Sb"]
            pws = nextps()
            k.mm_multi(pws, [(pws[:, hl * 128:(hl + 1) * 128], bh["wT"][:, hl, :], Sb[:, hl, :], False)
                             for hl in range(4)], r=(bh["wT"], Sb))
            k.tt("dve", t["vnew"][:], bh["u"][:], v3(pws[:], 4), ALU.subtract, r=(bh["u"], pws), w=(t["vnew"],))
            k.tt("pool", t["Ssc"][:], S[:], bc(b["E"][:, 16 + 4 * hh:20 + 4 * hh].unsqueeze(2), [128, 4, 128]),
                 ALU.mult, r=(S, b["E"]), w=(t["Ssc"],))
            yield
            po = nextps()

            def fn(g, po=po, Sb=Sb, bh=bh, t=t):
                ins = None
                for hl in range(4):
                    o_ = po[:, hl * 128:(hl + 1) * 128]
                    g.matmul(o_, lhsT=Sb[:, hl, :], rhs=bh["qdT"][:, hl, :], start=True, stop=False)
                    ins = g.matmul(o_, lhsT=t["vnew"][:, hl, :], rhs=bh["qkm"][:, hl, :], start=False, stop=True)
                return ins
            k.op("pe", fn, r=(Sb, bh["qdT"], t["vnew"], bh["qkm"]), w=(po,))
            k.copy("act", b["ost"][:, 4 * hh:4 * hh + 4, :], v3(po[:], 4), r=(po,), w=(b["ost"],))
            pds = nextps()
            k.mm_multi(pds, [(pds[:, hl * 128:(hl + 1) * 128], bh["kdec"][:, hl, :], t["vnew"][:, hl, :], False)
                             for hl in range(4)], r=(bh["kdec"], t["vnew"]))
            k.tt("dve", S[:], t["Ssc"][:], v3(pds[:], 4), ALU.add, r=(t["Ssc"], pds), w=(S,))
            k.copy("act", Sb[:], S[:], r=(S,), w=(Sb,))
            yield

        def store(d, c, sl):
            b = B[d, sl]
            k.dma(STQ, oT[d].rearrange("(h p) n -> p h n", p=128)[:, :, c * 128:(c + 1) * 128], b["ost"][:],
                  r=(b["ost"],))

        def chunk_of(d, s):
            return s if d == 0 else NCK - 1 - s

        run_threads([setup(d, chunk_of(d, 0), 0) for d in range(2)])
        run_threads([prep(d, chunk_of(d, 0), 0, hh) for d in range(2) for hh in range(2)])
        for s in range(NCK):
            sl = s % 2
            th = []
            if s + 1 < NCK:
                run_threads([setup(d, chunk_of(d, s + 1), 1 - sl) for d in range(2)])
                th += [prep(d, chunk_of(d, s + 1), 1 - sl, hh) for d in range(2) for hh in range(2)]
            th += [scan(d, chunk_of(d, s), sl, hh) for d in range(2) for hh in range(2)]
            run_threads(th)
            for d in range(2):
                store(d, chunk_of(d, s), sl)
        k.barrier()


    def phaseC(l, last):
        k.arena_reset()
        x = k.ar("Cx", [128, 8, TW], F32)
        osum = k.ar("Cosum", [128, 8, TW], F32)
        b_of = k.ar("Cof", [128, 8, TW], BF16)
        b_ob = k.ar("Cob", [128, 8, TW], BF16)
        b_zs = k.ar("Czs", [128, 8, TW], BF16)
        b_ysc = k.ar("Cysc", [128, 8, TW], BF16)
        b_ga = k.ar("Cga", [128, 8, TW], BF16)
        b_gb = k.ar("Cgb", [128, 8, TW], BF16)
        a_t = k.ar("Ca", [128, 22, TW], BF16)
        wb = [k.ar(f"Cw{i}", [128, 8, 1024], BF16) for i in range(4)]
        wi = [0]
        tA = [k.ar(f"CtA{i}", [128, TW], F32) for i in range(6)]
        tAi = [0]
        tB = [k.ar(f"CtB{i}", [128, TW], BF16) for i in range(2)]
        tBi = [0]
        rs_t = k.ar("Crstd", [128, TW], F32)
        otile = [k.ar(f"Cot{i}", [128, 1024], F32) for i in range(2)]
        oti = [0]
        print("arena phase C bytes", k.arena_off)
        odn, sq2, mg, h2 = b_of, b_ob, b_zs, b_of
        fm = lambda arr: arr.rearrange("(c p) n -> p c n", p=128)

        def wload(src2, rows0, cols0, ncols, nk=8, dst=None, dcol=0):
            wt = dst if dst is not None else nxt(wb, wi)
            k.dma("sp", wt[:, 0:nk, dcol:dcol + ncols],
                  src2[rows0:rows0 + nk * 128, cols0:cols0 + ncols].rearrange("(c p) n -> p c n", p=128), w=(wt,))
            return wt

        for (t0, W) in cfg.tiles:
            pcol = t0 + PADF
            k.dma("sp", x[:, :, 0:W], xTv[:, :, 2 + t0:2 + t0 + W], w=(x,))
            for buf, arr in ((b_of, oT[0]), (b_ob, oT[1]), (b_zs, zsT), (b_ysc, yscT), (b_ga, gaT), (b_gb, gbgT)):
                k.dma("sp", buf[:, :, 0:W], fm(arr)[:, :, pcol:pcol + W], w=(buf,))
            k.tt("pool", osum[:, :, 0:W], b_of[:, :, 0:W], b_ob[:, :, 0:W], ALU.add, r=(b_of, b_ob), w=(osum,))
            for c in range(8):
                s2 = nxt(tB, tBi)
                k.act(s2[:, 0:W], osum[:, c, 0:W], AF.Square, r=(osum,), w=(s2,))
                p2 = nextps()
                k.mm(p2, p2[:, 0:W], [(ones_b[:], s2[:, 0:W])], r=(s2, ones_b))
                rs = nxt(tA, tAi)
                k.act(rs[:, 0:W], p2[:, 0:W], AF.Sqrt, r=(p2, eps_t), w=(rs,), bias=eps_t[:], scale=1.0 / 128)
                k.recip(rs[:, 0:W], rs[:, 0:W], r=(rs,), w=(rs,))
                on = nxt(tA, tAi)
                k.stt("dve", on[:, 0:W], osum[:, c, 0:W], dnw[l][:, 0:1], rs[:, 0:W], ALU.mult, ALU.mult,
                      r=(osum, dnw[l], rs), w=(on,))
                k.tt("pool", odn[:, c, 0:W], on[:, 0:W], b_zs[:, c, 0:W], ALU.mult, r=(on, b_zs), w=(odn,))
            wdn = wload(wb_bdn[l], 0, 0, 1024)
            wsc = wload(wb_bsc[l], 0, 0, 1024)
            for m in range(8):
                pa = nextps()
                k.mm(pa, pa[:, 0:W], [(wdn[:, c, m * 128:(m + 1) * 128], odn[:, c, 0:W]) for c in range(8)],
                     r=(wdn, odn))
                pb = nextps()
                k.mm(pb, pb[:, 0:W], [(wsc[:, c, m * 128:(m + 1) * 128], b_ysc[:, c, 0:W]) for c in range(8)],
                     r=(wsc, b_ysc))
                t1 = nxt(tA, tAi)
                k.tt("dve", t1[:, 0:W], pa[:, 0:W], b_ga[:, m, 0:W], ALU.mult, r=(pa, b_ga), w=(t1,))
                t2 = nxt(tA, tAi)
                k.tt("dve", t2[:, 0:W], pb[:, 0:W], b_gb[:, m, 0:W], ALU.mult, r=(pb, b_gb), w=(t2,))
                k.tt("pool", mg[:, m, 0:W], t1[:, 0:W], t2[:, 0:W], ALU.add, r=(t1, t2), w=(mg,))
            wo = wload(wb_out[l], 0, 0, 1024)
            for m in range(8):
                pm = nextps()
                k.mm(pm, pm[:, 0:W], [(wo[:, c, m * 128:(m + 1) * 128], mg[:, c, 0:W]) for c in range(8)],
                     r=(wo, mg))
                k.tt("dve", x[:, m, 0:W], x[:, m, 0:W], pm[:, 0:W], ALU.add, r=(x, pm), w=(x,))
            k.act(sq2[:, :, 0:W], x[:, :, 0:W], AF.Square, r=(x,), w=(sq2,))
            pt = nextps()
            k.mm(pt, pt[:, 0:W], [(ones_b[:], sq2[:, c, 0:W]) for c in range(8)], r=(sq2, ones_b))
            k.act(rs_t[:, 0:W], pt[:, 0:W], AF.Sqrt, r=(pt, eps_t), w=(rs_t,), bias=eps_t[:], scale=1.0 / D)
            k.recip(rs_t[:, 0:W], rs_t[:, 0:W], r=(rs_t,), w=(rs_t,))
            for c in range(8):
                k.stt("dve", h2[:, c, 0:W], x[:, c, 0:W], n2w[l][:, c:c + 1], rs_t[:, 0:W], ALU.mult, ALU.mult,
                      r=(x, rs_t, n2w[l]), w=(h2,))
            for j0 in range(0, NFF, 4):
                nj = min(4, NFF - j0)
                wt = nxt(wb, wi)
                wload(wb_gu[l], 0, j0 * 128, nj * 128, dst=wt, dcol=0)
                wload(wb_gu[l], 0, DFF + j0 * 128, nj * 128, dst=wt, dcol=512)
                for jj in range(nj):
                    j = j0 + jj
                    pg = nextps()
                    k.mm(pg, pg[:, 0:W], [(wt[:, c, jj * 128:(jj + 1) * 128], h2[:, c, 0:W]) for c in range(8)],
                         r=(wt, h2))
                    pu = nextps()
                    k.mm(pu, pu[:, 0:W], [(wt[:, c, 512 + jj * 128:512 + (jj + 1) * 128], h2[:, c, 0:W])
                                          for c in range(8)], r=(wt, h2))
                    sg = nxt(tA, tAi)
                    k.act(sg[:, 0:W], pg[:, 0:W], AF.Silu, r=(pg,), w=(sg,))
                    k.tt("dve", a_t[:, j, 0:W], sg[:, 0:W], pu[:, 0:W], ALU.mult, r=(sg, pu), w=(a_t,))
            wd = [wload(wb_down[l], kb * 1024, 0, 1024, nk=min(8, NFF - kb * 8)) for kb in range(3)]
            for m in range(8):
                pd = nextps()
                k.mm(pd, pd[:, 0:W], [(wd[j // 8][:, j % 8, m * 128:(m + 1) * 128], a_t[:, j, 0:W])
                                      for j in range(NFF)], r=(wd[0], wd[1], wd[2], a_t))
                k.tt("dve", x[:, m, 0:W], x[:, m, 0:W], pd[:, 0:W], ALU.add, r=(x, pd), w=(x,))
            if not last:
                k.dma(STQ, xTv[:, :, 2 + t0:2 + t0 + W], x[:, :, 0:W], r=(x,))
            else:
                k.act(sq2[:, :, 0:W], x[:, :, 0:W], AF.Square, r=(x,), w=(sq2,))
                pt = nextps()
                k.mm(pt, pt[:, 0:W], [(ones_b[:], sq2[:, c, 0:W]) for c in range(8)], r=(sq2, ones_b))
                k.act(rs_t[:, 0:W], pt[:, 0:W], AF.Sqrt, r=(pt, eps_t), w=(rs_t,), bias=eps_t[:], scale=1.0 / D)
                k.recip(rs_t[:, 0:W], rs_t[:, 0:W], r=(rs_t,), w=(rs_t,))
                xn = osum
                for c in range(8):
                    k.stt("dve", xn[:, c, 0:W], x[:, c, 0:W], fw[:, c:c + 1], rs_t[:, 0:W], ALU.mult, ALU.mult,
                          r=(x, rs_t, fw), w=(xn,))
                lo = max(t0, NMETA)
                while lo < t0 + W:
                    nn = min(128, t0 + W - lo)
                    ot = nxt(otile, oti)
                    for half in range(2):
                        pt = nextps()
                        k.mm_multi(pt, [(pt[0:nn, cc * 128:(cc + 1) * 128],
                                         xn[:, half * 4 + cc, lo - t0:lo - t0 + nn], ident_f[:], True)
                                        for cc in range(4)], r=(xn, ident_f))
                        k.copy("act" if half else "dve", ot[0:nn, half * 512:(half + 1) * 512], pt[0:nn, :],
                               r=(pt,), w=(ot,))
                    k.dma(STQ, out[lo - NMETA:lo - NMETA + nn, :], ot[0:nn, :], r=(ot,))
                    lo += nn
        k.barrier()

    phases = cfg.__dict__.get("phases", "ABC")
    nl = cfg.__dict__.get("nlayers", DEPTH)
    for l in range(nl):
        phaseA(l)
        if "B" in phases:
            phaseB(l)
        if "C" in phases:
            phaseC(l, l == nl - 1)

    k.barrier()
    with nc.Block() as block:
        @block.tensor
        def _(g):
            k.replay("pe", g)

        @block.scalar
        def _(g):
            k.replay("act", g)

        @block.vector
        def _(g):
            k.replay("dve", g)

        @block.gpsimd
        def _(g):
            k.replay("pool", g)

        @block.sync
        def _(g):
            k.replay("sp", g)
    es.close()
    return nc


def make_masks():
    m = np.zeros((8, 128, 128), np.float32)
    t = np.arange(128)[:, None]
    i = np.arange(128)[None, :]
    m[0] = (t <= i)
    m[1] = (t > i)
    m[2] = (t >= i)
    m[3] = (t < i)
    return m


def core_inputs(cfg, inputs, b):
    f = np.float32
    xin = np.concatenate([inputs["meta_tokens"].astype(f), inputs["x"][b].astype(f)], axis=0)
    d = dict(
        xin=np.ascontiguousarray(xin),
        norm1_w=inputs["norm1_w"], w_in=inputs["w_in"], dn_conv_w=inputs["dn_conv_w"],
        A_log=inputs["A_log"].reshape(cfg.depth, 16), dt_bias=inputs["dt_bias"].reshape(cfg.depth, 16),
        dn_norm_w=inputs["dn_norm_w"], sc_conv_w=inputs["sc_conv_w"],
        w_branch_dn=inputs["w_branch_dn"], w_branch_sc=inputs["w_branch_sc"], w_out=inputs["w_out"],
        norm2_w=inputs["norm2_w"], w_gate_up=inputs["w_gate_up"], w_down=inputs["w_down"],
        final_norm_w=inputs["final_norm_w"],
        c_ident_f=np.eye(128, dtype=f), c_masks=make_masks())
    return {k_: np.ascontiguousarray(np.asarray(v, dtype=f)) for k_, v in d.items()}


_NC_CACHE = {}


def kernel(**inputs):
    x = inputs["x"]
    bsz, seq, _ = x.shape
    cfg = Cfg(seq, 2)
    key = (seq,)
    if key not in _NC_CACHE:
        _NC_CACHE[key] = build(cfg)
    nc = _NC_CACHE[key]
    in_maps = [core_inputs(cfg, inputs, b) for b in range(bsz)]
    res = run_bass_kernel_spmd(nc, in_maps, core_ids=list(range(bsz)))
    return np.stack([np.asarray(r["out"], dtype=np.float32) for r in res.results], axis=0)
```

```python
import numpy as np
import ml_dtypes
from contextlib import ExitStack
import concourse.bass as bass
import concourse.mybir as mybir
from concourse.bass_utils import run_bass_kernel_spmd

F32 = mybir.dt.float32
BF16 = mybir.dt.bfloat16
AF = mybir.ActivationFunctionType
ALU = mybir.AluOpType

D = 1024
NCH = 8
H = 8
NMETA = 16
DFF = 2816
NFF = 22
WIN = 9248
RMS_EPS = 1e-6
L2_EPS = 1e-6
TW = 508
SEM_ROLL = 30000
CHAIN_FP32 = True
CHAIN_R = False
STQ = "sp"

C_Q, C_K, C_V, C_Z, C_B, C_A, C_SB, C_SC, C_SX, C_GA, C_GB = (
    0, 1024, 2048, 3072, 4096, 4112, 4128, 5152, 6176, 7200, 8224)


class Tl:
    def __init__(self, name, ap):
        self.name = name
        self.ap = ap
        self.w = None
        self.r = {}

    def __getitem__(self, idx):
        return self.ap[idx]


class Eng:
    def __init__(self, name, sems):
        self.name = name
        self.sems = sems
        self.si = 0
        self.cnt = 0
        self.ops = []
        self.waited = {}


class KB:
    def __init__(self, nc, es):
        self.nc = nc
        self.es = es
        self.semh = []
        self.eng = {}
        for n in ("pe", "act", "dve", "pool", "sp"):
            ids = [self._newsem(f"s_{n}{i}") for i in range(3)]
            self.eng[n] = Eng(n, ids)
        self.dq = {}
        for q, cnt in (("sp", 40), ("pool", 4), ("act", 36)):
            self.dq[q] = dict(ids=[self._newsem(f"d_{q}{i}") for i in range(cnt)],
                              cnt=[0] * cnt, nxt=0)
        self.all_tokens = {}
        self.ntile = 0

    def _newsem(self, name):
        h = self.es.enter_context(self.nc.semaphore(name))
        self.semh.append(h)
        return len(self.semh) - 1

    def sb(self, name, shape, dt):
        t = self.es.enter_context(self.nc.sbuf_tensor(name, list(shape), dt))
        return Tl(name, t)

    def arena_init(self, nbytes):
        self.arena = self.es.enter_context(self.nc.sbuf_tensor("arena", [128, nbytes // 2], BF16))
        self.arena_size = nbytes
        self.arena_off = 0

    def arena_reset(self):
        self.arena_off = 0

    def ar(self, name, shape, dt):
        esz = 4 if dt == F32 else 2
        n = 1
        for d_ in shape[1:]:
            n *= d_
        nb = (n * esz + 31) // 32 * 32
        off = self.arena_off
        assert off + nb <= self.arena_size, f"arena overflow at {name}: {off + nb}"
        self.arena_off += nb
        ap = self.arena[0:shape[0], off // 2:(off + n * esz) // 2]
        if dt == F32:
            ap = ap.bitcast(F32)
        if len(shape) == 3:
            ap = ap.rearrange("p (a b) -> p a b", a=shape[1])
        elif len(shape) == 4:
            ap = ap.rearrange("p (a b c) -> p a b c", a=shape[1], b=shape[2])
        return Tl(name, ap)

    def ps(self, name, shape, dt=F32):
        t = self.es.enter_context(self.nc.psum_tensor(name, list(shape), dt))
        return Tl(name, t)

    def _deps(self, e, r, w, extra=()):
        waits = {}

        def add(tok):
            if tok is None:
                return
            s, v = tok
            if waits.get(s, 0) < v:
                waits[s] = v
        for t in r:
            add(t.w)
        for t in w:
            add(t.w)
            for s, v in t.r.items():
                add((s, v))
        for tok in extra:
            add(tok)
        need = []
        for s, v in waits.items():
            if e.waited.get(s, 0) < v:
                e.waited[s] = v
                need.append((s, v))
        return need

    def _mark(self, tok, r, w):
        s, v = tok
        for t in r:
            if t.r.get(s, 0) < v:
                t.r[s] = v
        for t in w:
            t.w = tok
            t.r = {}
        if self.all_tokens.get(s, 0) < v:
            self.all_tokens[s] = v

    def op(self, engine, fn, r=(), w=(), extra=()):
        e = self.eng[engine]
        need = self._deps(e, r, w, extra)
        if e.cnt >= SEM_ROLL:
            e.si += 1
            e.cnt = 0
        e.cnt += 1
        sid = e.sems[e.si]
        tok = (sid, e.cnt)
        e.ops.append((need, fn, sid, 1))
        self._mark(tok, r, w)
        return tok

    def dma(self, queue, out, in_, r=(), w=(), extra=()):
        e = self.eng[queue]
        dq = self.dq[queue]
        i = dq["nxt"]
        dq["nxt"] = (i + 1) % len(dq["ids"])
        sid = dq["ids"][i]
        prev = (sid, dq["cnt"][i]) if dq["cnt"][i] else None
        need = self._deps(e, r, w, tuple(extra) + ((prev,) if prev else ()))
        dq["cnt"][i] += 16
        tok = (sid, dq["cnt"][i])
        e.ops.append((need, lambda g, o=out, s=in_: g.dma_start(out=o, in_=s), sid, 16))
        self._mark(tok, r, w)
        return tok

    def barrier(self):
        toks = list(self.all_tokens.items())
        for e in self.eng.values():
            need = []
            for s, v in toks:
                if e.waited.get(s, 0) < v:
                    e.waited[s] = v
                    need.append((s, v))
            if need:
                e.ops.append((need, None, None, 0))

    def replay(self, engine, g):
        for need, fn, sid, inc in self.eng[engine].ops:
            for s, v in need:
                g.wait_ge(self.semh[s], v)
            if fn is None:
                continue
            ins = fn(g)
            if inc:
                ins.then_inc(self.semh[sid], inc)

    def mm(self, out_t, out_ap, pairs, r=(), transpose=False):
        n = len(pairs)

        def fn(g, out_ap=out_ap, pairs=pairs, n=n):
            ins = None
            for i, (a, b) in enumerate(pairs):
                ins = g.matmul(out_ap, lhsT=a, rhs=b, start=(i == 0), stop=(i == n - 1))
            return ins
        return self.op("pe", fn, r=r, w=(out_t,))

    def mm_multi(self, out_t, groups, r=()):
        def fn(g, groups=groups):
            ins = None
            for (o, a, b, tr) in groups:
                if tr:
                    ins = g.transpose(o, a, b)
                else:
                    ins = g.matmul(o, lhsT=a, rhs=b, start=True, stop=True)
            return ins
        return self.op("pe", fn, r=r, w=(out_t,))

    def act(self, out, in_, func, r=(), w=(), bias=None, scale=None, eng="act"):
        kw = {}
        if bias is not None:
            kw["bias"] = bias
        if scale is not None:
            kw["scale"] = scale
        return self.op(eng, lambda g, o=out, i=in_, f=func, kw=kw: g.activation(out=o, in_=i, func=f, **kw),
                       r=r, w=w)

    def tt(self, eng, out, in0, in1, op, r=(), w=()):
        return self.op(eng, lambda g, o=out, a=in0, b=in1, p=op: g.tensor_tensor(out=o, in0=a, in1=b, op=p),
                       r=r, w=w)

    def stt(self, eng, out, in0, scalar, in1, op0, op1, r=(), w=()):
        return self.op(eng, lambda g, o=out, a=in0, s=scalar, b=in1, p0=op0, p1=op1:
                       g.scalar_tensor_tensor(out=o, in0=a, scalar=s, in1=b, op0=p0, op1=p1), r=r, w=w)

    def ts(self, eng, out, in0, s1, s2, op0, op1=None, r=(), w=()):
        if op1 is None:
            return self.op(eng, lambda g, o=out, a=in0, s=s1, p0=op0:
                           g.tensor_scalar(out=o, in0=a, scalar1=s, scalar2=None, op0=p0), r=r, w=w)
        return self.op(eng, lambda g, o=out, a=in0, x=s1, y=s2, p0=op0, p1=op1:
                       g.tensor_scalar(out=o, in0=a, scalar1=x, scalar2=y, op0=p0, op1=p1), r=r, w=w)

    def copy(self, eng, out, in_, r=(), w=()):
        if eng == "act":
            return self.op(eng, lambda g, o=out, i=in_: g.copy(out=o, in_=i), r=r, w=w)
        return self.op(eng, lambda g, o=out, i=in_: g.tensor_copy(out=o, in_=i), r=r, w=w)

    def recip(self, out, in_, r=(), w=()):
        return self.op("dve", lambda g, o=out, i=in_: g.reciprocal(out=o, in_=i), r=r, w=w)

    def memset(self, eng, ap, val, w=()):
        return self.op(eng, lambda g, a=ap, v=val: g.memset(a, v), w=w)


def bc(ap, shape):
    return ap.to_broadcast(list(shape))


class Cfg:
    def __init__(self, seq, depth):
        self.seq = seq
        self.depth = depth
        self.L = seq + NMETA
        self.PADF = (-self.L) % 128
        self.T = self.L + self.PADF
        self.NCK = self.T // 128
        self.XOFF = 2
        self.XW = self.L + 4
        self.tiles = []
        t0 = 0
        while t0 < self.L:
            w = min(TW, self.L - t0)
            self.tiles.append((t0, w))
            t0 += w


def build(cfg, debug=False):
    nc = bass.Bass("TRN2", target_bir_lowering=False)
    L, T, DEPTH = cfg.L, cfg.T, cfg.depth
    es = ExitStack()
    k = KB(nc, es)

    def din(name, shape, dt=F32):
        return nc.dram_tensor(name, list(shape), dt, kind="ExternalInput").ap()

    def dscr(name, shape, dt):
        kind = "ExternalOutput" if debug else "Internal"
        return nc.dram_tensor(name, list(shape), dt, kind=kind).ap()

    xin = din("xin", [L, D])
    norm1_w = din("norm1_w", [DEPTH, D])
    w_in = din("w_in", [DEPTH, D, WIN])
    dn_conv_w = din("dn_conv_w", [DEPTH, 5, 3072])
    A_log = din("A_log", [DEPTH, 16])
    dt_bias = din("dt_bias", [DEPTH, 16])
    dn_norm_w = din("dn_norm_w", [DEPTH, 128])
    sc_conv_w = din("sc_conv_w", [DEPTH, 3, 1024])
    w_bdn = din("w_branch_dn", [DEPTH, D, D])
    w_bsc = din("w_branch_sc", [DEPTH, D, D])
    w_out = din("w_out", [DEPTH, D, D])
    norm2_w = din("norm2_w", [DEPTH, D])
    w_gu = din("w_gate_up", [DEPTH, D, 2 * DFF])
    w_down = din("w_down", [DEPTH, DFF, D])
    final_w = din("final_norm_w", [D])
    c_ident_f = din("c_ident_f", [128, 128])
    c_masks = din("c_masks", [8, 128, 128])
    out = nc.dram_tensor("out", [cfg.seq, D], F32, kind="ExternalOutput").ap()

    wb_in = dscr("wb_in", [DEPTH, D, WIN], BF16)
    wb_bdn = dscr("wb_bdn", [DEPTH, D, D], BF16)
    wb_bsc = dscr("wb_bsc", [DEPTH, D, D], BF16)
    wb_out = dscr("wb_out", [DEPTH, D, D], BF16)
    wb_gu = dscr("wb_gu", [DEPTH, D, 2 * DFF], BF16)
    wb_down = dscr("wb_down", [DEPTH, DFF, D], BF16)
    xT = dscr("xT", [D, cfg.XW], F32)
    qT = dscr("qT", [D, T], BF16)
    kT = dscr("kT", [D, T], BF16)
    vT = dscr("vT", [D, T], BF16)
    gbT = dscr("gbT", [32, T], F32)
    zsT = dscr("zsT", [D, T], BF16)
    yscT = dscr("yscT", [D, T], BF16)
    gaT = dscr("gaT", [D, T], BF16)
    gbgT = dscr("gbgT", [D, T], BF16)
    oT = [dscr(f"oT{d}", [D, T], BF16) for d in range(2)]

    ident_f = k.sb("ident_f", [128, 128], F32)
    ident_b = k.sb("ident_b", [128, 128], BF16)
    ones_b = k.sb("ones_b", [128, 128], BF16)
    ones_f = k.sb("ones_f", [128, 128], F32)
    masks = k.sb("masks", [128, 8, 128], F32)
    zero_b = k.sb("zero_b", [128, 1024], BF16)
    zero_f = k.sb("zero_f", [128, 512], F32)
    k.dma("sp", ident_f[:], c_ident_f[:, :], w=(ident_f,))
    k.dma("sp", masks[:], c_masks.rearrange("m p n -> p m n"), w=(masks,))
    k.copy("dve", ident_b[:], ident_f[:], r=(ident_f,), w=(ident_b,))
    k.memset("dve", ones_b[:], 1.0, w=(ones_b,))
    k.memset("dve", ones_f[:], 1.0, w=(ones_f,))
    k.memset("pool", zero_b[:], 0.0, w=(zero_b,))
    k.memset("pool", zero_f[:], 0.0, w=(zero_f,))

    def load_vec(name, src_ap, nchunk):
        t = k.sb(name, [128, nchunk], F32)
        k.dma("sp", t[:], src_ap.rearrange("(c p) -> p c", p=128), w=(t,))
        return t

    nc_allow = nc.allow_non_contiguous_dma(reason="tiny parameter vectors")
    es.enter_context(nc_allow)

    n1w = [load_vec(f"n1w{l}", norm1_w[l], 8) for l in range(DEPTH)]
    n2w = [load_vec(f"n2w{l}", norm2_w[l], 8) for l in range(DEPTH)]
    fw = load_vec("fw", final_w, 8)
    dcw = []
    scw = []
    dnw = []
    nAexp = []
    dtb = []
    for l in range(DEPTH):
        t = k.sb(f"dcw{l}", [128, 5, 24], F32)
        k.dma("sp", t[:], dn_conv_w[l].rearrange("d (c p) -> p d c", p=128), w=(t,))
        dcw.append(t)
        t = k.sb(f"scw{l}", [128, 3, 8], F32)
        k.dma("sp", t[:], sc_conv_w[l].rearrange("d (c p) -> p d c", p=128), w=(t,))
        scw.append(t)
        t = k.sb(f"dnw{l}", [128, 1], F32)
        k.dma("sp", t[:], dn_norm_w[l].rearrange("(p o) -> p o", o=1), w=(t,))
        dnw.append(t)
        ta = k.sb(f"alog{l}", [16, 1], F32)
        k.dma("sp", ta[:], A_log[l].rearrange("(p o) -> p o", o=1), w=(ta,))
        tb = k.sb(f"dtb{l}", [16, 1], F32)
        k.dma("sp", tb[:], dt_bias[l].rearrange("(p o) -> p o", o=1), w=(tb,))
        dtb.append(tb)
        te = k.sb(f"nAexp{l}", [16, 1], F32)
        k.act(te[:], ta[:], AF.Exp, r=(ta,), w=(te,))
        k.ts("dve", te[:], te[:], -1.0, None, ALU.mult, r=(te,), w=(te,))
        nAexp.append(te)
    eps_t = k.sb("eps_t", [128, 1], F32)
    k.memset("dve", eps_t[:], RMS_EPS, w=(eps_t,))
    eps128_t = k.sb("eps128_t", [128, 1], F32)
    k.memset("dve", eps128_t[:], 128.0 * L2_EPS, w=(eps128_t,))
    one_t = k.sb("one_t", [128, 1], F32)
    k.memset("dve", one_t[:], 1.0, w=(one_t,))

    k.arena_init(196 * 1024)
    CW = 4096
    cf = [k.ar(f"castf{i}", [128, CW], F32) for i in range(3)]
    cb = [k.ar(f"castb{i}", [128, CW], BF16) for i in range(3)]
    cidx = [0]

    def cast_w(dst, src, rows, cols):
        for l in range(DEPTH):
            for r0 in range(0, rows, 128):
                for c0 in range(0, cols, CW):
                    cw = min(CW, cols - c0)
                    i = cidx[0] % 3
                    cidx[0] += 1
                    k.dma("sp", cf[i][:, 0:cw], src[l, r0:r0 + 128, c0:c0 + cw], w=(cf[i],))
                    eng = ("act", "dve", "pool")[i]
                    k.copy(eng, cb[i][:, 0:cw], cf[i][:, 0:cw], r=(cf[i],), w=(cb[i],))
                    k.dma(STQ, dst[l, r0:r0 + 128, c0:c0 + cw], cb[i][:, 0:cw], r=(cb[i],))
    cast_w(wb_in, w_in, D, WIN)
    cast_w(wb_bdn, w_bdn, D, D)
    cast_w(wb_bsc, w_bsc, D, D)
    cast_w(wb_out, w_out, D, D)
    cast_w(wb_gu, w_gu, D, 2 * DFF)
    cast_w(wb_down, w_down, DFF, D)
    k.barrier()

    PADF = cfg.PADF
    if PADF:
        for arr in (qT, kT, vT):
            k.dma("sp", arr.rearrange("(c p) n -> p c n", p=128)[:, :, 0:PADF],
                  zero_b[:, 0:8 * PADF].rearrange("p (c n) -> p c n", c=8), r=(zero_b,))
        k.dma("sp", gbT[:, 0:PADF], zero_f[0:32, 0:PADF], r=(zero_f,))
    xTv = xT.rearrange("(c p) n -> p c n", p=128)
    k.dma("sp", xTv[:, :, 0:2], zero_f[:, 0:16].rearrange("p (c n) -> p c n", c=8), r=(zero_f,))
    k.dma("sp", xTv[:, :, L + 2:L + 4], zero_f[:, 0:16].rearrange("p (c n) -> p c n", c=8), r=(zero_f,))

    PS = [k.ps(f"ps{i}", [128, 512], F32) for i in range(8)]
    psi = [0]

    def nextps():
        t = PS[psi[0] % 8]
        psi[0] += 1
        return t

    k.arena_reset()
    p0_in = [k.ar(f"p0in{i}", [128, 4, D], F32) for i in range(2)]
    p0_out = [k.ar(f"p0out{i}", [128, 8, 512], F32) for i in range(2)]
    it = 0
    for t0 in range(0, L, 512):
        n = min(512, L - t0)
        tin = p0_in[it % 2]
        tout = p0_out[it % 2]
        nb = (n + 127) // 128
        for b in range(nb):
            nn = min(128, n - b * 128)
            k.dma("sp", tin[0:nn, b, :], xin[t0 + b * 128:t0 + b * 128 + nn, :], w=(tin,))
        for c in range(8):
            pt = nextps()
            groups = []
            for b in range(nb):
                nn = min(128, n - b * 128)
                groups.append((pt[:, b * 128:b * 128 + nn], tin[0:nn, b, c * 128:(c + 1) * 128],
                               ident_f[0:nn, 0:nn], True))
            k.mm_multi(pt, groups, r=(tin, ident_f))
            k.copy("act" if c % 2 else "dve", tout[:, c, 0:n], pt[:, 0:n], r=(pt,), w=(tout,))
        k.dma("sp", xTv[:, :, 2 + t0:2 + t0 + n], tout[:, :, 0:n], r=(tout,))
        it += 1
    k.barrier()

    NCMAX = TW + 4

    def nxt(lst, ctr):
        t = lst[ctr[0] % len(lst)]
        ctr[0] += 1
        return t

    psfree = []

    def ps_alloc():
        return psfree.pop(0)

    def ps_free(t):
        psfree.append(t)

    def run_pipeline(tasks, depth, budget=7):
        psfree[:] = list(PS)
        live = []
        pending = list(tasks)
        pi = 0
        while True:
            while pi < len(pending) and len(live) < depth:
                tk = pending[pi]
                nb = getattr(tk, "nb", 0)
                if sum(n for _, n in live) + nb > budget:
                    break
                pi += 1
                g_ = tk()
                if g_ is not None:
                    try:
                        next(g_)
                        live.append((g_, nb))
                    except StopIteration:
                        pass
            if not live:
                if pi >= len(pending):
                    break
                continue
            for ent in list(live):
                try:
                    next(ent[0])
                except StopIteration:
                    live.remove(ent)

    def phaseA(l):
        k.arena_reset()
        xts = [k.ar(f"xt{i}", [128, 8, NCMAX], F32) for i in range(1)]
        hTs = [k.ar(f"hT{i}", [128, 8, NCMAX], BF16) for i in range(2)]
        sq = k.ar("sq", [128, 8, NCMAX], BF16)
        rstd = k.ar("rstd", [128, NCMAX], F32)
        wbuf = [k.ar(f"wbuf{i}", [128, 8, 1024], BF16) for i in range(4)]
        wsm = k.ar("wsm", [128, 8, 32], BF16)
        stg = [k.ar(f"stg{i}", [128, 8, TW], BF16) for i in range(3)]
        tmpA = [k.ar(f"tmpA{i}", [128, NCMAX], F32) for i in range(8)]
        ssq8 = [k.ar(f"ssq8{i}", [128, 8, TW], F32) for i in range(2)]
        tmpB = [k.ar(f"tmpB{i}", [128, NCMAX], BF16) for i in range(6)]

        def ta():
            return tmpA.pop(0)

        def tb():
            return tmpB.pop(0)
        gbs = k.ar("gbs", [16, 2, NCMAX], F32)
        print("arena phase A bytes", k.arena_off)
        wl = wb_in[l]
        fm = lambda arr: arr.rearrange("(c p) n -> p c n", p=128)
        BLK = [C_Q, C_K, C_V, C_Z, C_SC, C_SX, C_SB, C_GA, C_GB]
        nblk = len(BLK)
        wslot = {}
        gblk = [0]

        def t_wload(gi):
            def f():
                wt = wbuf[gi % 4]
                k.dma("sp", wt[:, :, :], wl[:, BLK[gi % nblk]:BLK[gi % nblk] + 1024].rearrange("(c p) n -> p c n", p=128),
                      w=(wt,))
                wslot[gi] = wt
            return f

        def t_xload(ti):
            def f():
                t0, W = cfg.tiles[ti]
                NC = W + 4
                k.dma("sp", xts[0][:, :, 0:NC], xTv[:, :, t0:t0 + NC], w=(xts[0],))
            return f

        def t_pro1(ti):
            def f():
                t0, W = cfg.tiles[ti]
                NC = W + 4
                xt = xts[0]
                k.act(sq[:, :, 0:NC], xt[:, :, 0:NC], AF.Square, r=(xt,), w=(sq,))
            return f

        def t_pro2(ti):
            def f():
                t0, W = cfg.tiles[ti]
                NC = W + 4
                xt, hT = xts[0], hTs[ti % 2]
                pt = ps_alloc()
                k.mm(pt, pt[:, 0:NC], [(ones_b[:], sq[:, c, 0:NC]) for c in range(8)], r=(sq, ones_b))
                k.act(rstd[:, 0:NC], pt[:, 0:NC], AF.Sqrt, r=(pt, eps_t), w=(rstd,), bias=eps_t[:], scale=1.0 / D)
                ps_free(pt)
                k.recip(rstd[:, 0:NC], rstd[:, 0:NC], r=(rstd,), w=(rstd,))
                for c in range(8):
                    k.stt("dve", hT[:, c, 0:NC], xt[:, c, 0:NC], n1w[l][:, c:c + 1],
                          rstd[:, 0:NC], ALU.mult, ALU.mult, r=(xt, rstd, n1w[l]), w=(hT,))
            return f

        def proj(hT, NC, wt, j, M=128):
            pt = ps_alloc()
            k.mm(pt, pt[0:M, 0:NC], [(wt[:, c, j:j + M], hT[:, c, 0:NC]) for c in range(8)], r=(wt, hT))
            return pt

        class Grp:
            def __init__(self, st, dst, pcol, W, n=8, norm=None, ssq=None):
                self.st, self.dst, self.pcol, self.W, self.left = st, dst, pcol, W, n
                self.norm, self.ssq = norm, ssq

            def done(self):
                self.left -= 1
                if self.left == 0:
                    W = self.W
                    if self.norm is not None:
                        sq_ = self.ssq
                        if self.norm == 0:
                            k.act(sq_[:, :, 0:W], sq_[:, :, 0:W], AF.Sqrt, r=(sq_, eps128_t), w=(sq_,),
                                  bias=eps128_t[:], scale=128.0)
                        else:
                            k.act(sq_[:, :, 0:W], sq_[:, :, 0:W], AF.Sqrt, r=(sq_, eps_t), w=(sq_,),
                                  bias=eps_t[:], scale=1.0)
                        k.recip(sq_[:, :, 0:W], sq_[:, :, 0:W], r=(sq_,), w=(sq_,))
                        k.tt("dve", self.st[:, :, 0:W], self.st[:, :, 0:W], sq_[:, :, 0:W], ALU.mult,
                             r=(self.st, sq_), w=(self.st,))
                    k.dma(STQ, fm(self.dst)[:, :, self.pcol:self.pcol + W], self.st[:, :, 0:W],
                          r=(self.st,))

        def t_qkv(ti, gi, grp, c, G):
            def gen():
                t0, W = cfg.tiles[ti]
                NC = W + 4
                hT = hTs[ti % 2]
                wt = wslot[gi]
                st = G.st
                pt = proj(hT, NC, wt, c * 128)
                yield
                cc = grp * 8 + c
                acc = ta()
                k.ts("dve", acc[:, 0:W], pt[:, 0:W], dcw[l][:, 0, cc:cc + 1], None, ALU.mult,
                     r=(pt, dcw[l]), w=(acc,))
                for d in range(1, 5):
                    k.stt("dve", acc[:, 0:W], pt[:, d:d + W], dcw[l][:, d, cc:cc + 1], acc[:, 0:W],
                          ALU.mult, ALU.add, r=(pt, dcw[l], acc), w=(acc,))
                ps_free(pt)
                yield
                if grp == 2:
                    k.act(st[:, c, 0:W], acc[:, 0:W], AF.Silu, r=(acc,), w=(st,))
                    tmpA.append(acc)
                    G.done()
                    return
                k.act(st[:, c, 0:W], acc[:, 0:W], AF.Silu, r=(acc,), w=(st,))
                tmpA.append(acc)
                s2 = tb()
                k.act(s2[:, 0:W], st[:, c, 0:W], AF.Square, r=(st,), w=(s2,))
                yield
                p2 = ps_alloc()
                k.mm(p2, p2[:, 0:W], [(ones_b[:], s2[:, 0:W])], r=(s2, ones_b))
                tmpB.append(s2)
                yield
                k.copy("act", G.ssq[:, c, 0:W], p2[:, 0:W], r=(p2,), w=(G.ssq,))
                ps_free(p2)
                G.done()
            gen.nb = 1
            return gen

        def t_simple(ti, gi, c, G, func):
            def gen():
                t0, W = cfg.tiles[ti]
                NC = W + 4
                pt = proj(hTs[ti % 2], NC, wslot[gi], c * 128)
                yield
                k.act(G.st[:, c, 0:W], pt[:, 2:2 + W], func, r=(pt,), w=(G.st,))
                ps_free(pt)
                G.done()
            gen.nb = 1
            return gen

        def t_bg(ti):
            def gen():
                t0, W = cfg.tiles[ti]
                NC = W + 4
                pcol = t0 + PADF
                hT = hTs[ti % 2]
                k.dma("sp", wsm[:, :, :], wl[:, C_B:C_B + 32].rearrange("(c p) n -> p c n", p=128), w=(wsm,))
                pts = []
                for which in range(2):
                    pt = ps_alloc()
                    k.mm(pt, pt[0:16, 0:NC], [(wsm[:, c, which * 16:which * 16 + 16], hT[:, c, 0:NC])
                                              for c in range(8)], r=(wsm, hT))
                    pts.append(pt)
                yield
                k.act(gbs[:, 0, 0:W], pts[0][0:16, 2:2 + W], AF.Sigmoid, r=(pts[0],), w=(gbs,))
                k.act(gbs[:, 1, 0:W], pts[1][0:16, 2:2 + W], AF.Exp, r=(pts[1], dtb[l]), w=(gbs,), bias=dtb[l][:])
                ps_free(pts[0])
                ps_free(pts[1])
                k.act(gbs[:, 1, 0:W], gbs[:, 1, 0:W], AF.Ln, r=(gbs, one_t), w=(gbs,), bias=one_t[0:16, :])
                yield
                k.ts("dve", gbs[:, 1, 0:W], gbs[:, 1, 0:W], nAexp[l][:, 0:1], None, ALU.mult,
                     r=(gbs, nAexp[l]), w=(gbs,))
                k.dma(STQ, gbT.rearrange("(a p) n -> p a n", p=16)[:, :, pcol:pcol + W], gbs[:, :, 0:W], r=(gbs,))
            gen.nb = 2
            return gen

        def t_sc(ti, gi_c, gi_x, gi_b, c, G):
            def gen():
                t0, W = cfg.tiles[ti]
                NC = W + 4
                hT = hTs[ti % 2]
                pc = proj(hT, NC, wslot[gi_c], c * 128)
                px = proj(hT, NC, wslot[gi_x], c * 128)
                pb = proj(hT, NC, wslot[gi_b], c * 128)
                yield
                cx = ta()
                k.copy("act", cx[:, 0:NC], pc[:, 0:NC], r=(pc,), w=(cx,))
                ps_free(pc)
                yield
                pr = ta()
                k.tt("dve", pr[:, 0:NC], px[:, 0:NC], cx[:, 0:NC], ALU.mult, r=(px, cx), w=(pr,))
                ps_free(px)
                tmpA.append(cx)
                yield
                acc = ta()
                k.ts("dve", acc[:, 0:W], pr[:, 1:1 + W], scw[l][:, 0, c:c + 1], None, ALU.mult,
                     r=(pr, scw[l]), w=(acc,))
                for d in range(1, 3):
                    k.stt("dve", acc[:, 0:W], pr[:, 1 + d:1 + d + W], scw[l][:, d, c:c + 1], acc[:, 0:W],
                          ALU.mult, ALU.add, r=(pr, scw[l], acc), w=(acc,))
                tmpA.append(pr)
                yield
                k.tt("dve", G.st[:, c, 0:W], pb[:, 2:2 + W], acc[:, 0:W], ALU.mult, r=(pb, acc), w=(G.st,))
                ps_free(pb)
                tmpA.append(acc)
                G.done()
            gen.nb = 3
            return gen

        tasks = []
        ntile = len(cfg.tiles)
        stgc = [0]
        total_blocks = ntile * nblk
        tasks.append(t_xload(0))
        for gi in range(4):
            tasks.append(t_wload(gi))
        tasks.append(t_pro1(0))
        tasks.append(t_pro2(0))
        for ti, (t0, W) in enumerate(cfg.tiles):
            pcol = t0 + PADF
            base = ti * nblk

            def post(gi):
                if gi + 4 < total_blocks:
                    tasks.append(t_wload(gi + 4))

            def newG(dst, n=8, norm=None, ssq=None):
                st = stg[stgc[0] % 3]
                stgc[0] += 1
                return Grp(st, dst, pcol, W, n, norm, ssq)
            for grp, dst in enumerate((qT, kT, vT)):
                G = newG(dst, norm=(grp if grp < 2 else None), ssq=(ssq8[grp] if grp < 2 else None))
                for c in range(8):
                    tasks.append(t_qkv(ti, base + grp, grp, c, G))
                post(base + grp)
                if grp == 1 and ti + 1 < ntile:
                    tasks.append(t_xload(ti + 1))
            G = newG(zsT)
            for c in range(8):
                tasks.append(t_simple(ti, base + 3, c, G, AF.Silu))
            post(base + 3)
            tasks.append(t_bg(ti))
            G = newG(yscT)
            for c in range(8):
                tasks.append(t_sc(ti, base + 4, base + 5, base + 6, c, G))
            post(base + 4)
            post(base + 5)
            post(base + 6)
            if ti + 1 < ntile:
                tasks.append(t_pro1(ti + 1))
            for bi, dst in ((7, gaT), (8, gbgT)):
                G = newG(dst)
                for c in range(8):
                    tasks.append(t_simple(ti, base + bi, c, G, AF.Sigmoid))
                post(base + bi)
                if bi == 7 and ti + 1 < ntile:
                    tasks.append(t_pro2(ti + 1))
        run_pipeline(tasks, 5)
        k.barrier()

    def run_threads(gens):
        live = list(gens)
        while live:
            for g_ in list(live):
                try:
                    next(g_)
                except StopIteration:
                    live.remove(g_)

    def v3(ap, a):
        return ap.rearrange("p (a b) -> p a b", a=a)

    def phaseB(l):
        k.arena_reset()
        NCK = cfg.NCK
        qTv = qT.rearrange("(c p) n -> p c n", p=128)
        kTv = kT.rearrange("(c p) n -> p c n", p=128)
        vTv = vT.rearrange("(c p) n -> p c n", p=128)
        B = {}
        CDT = F32 if CHAIN_FP32 else BF16
        identc = ident_f if CHAIN_FP32 else ident_b
        for d in range(2):
            rGa_ = k.ar(f"rGa{d}", [128, 8, 128], F32)
            rGi_ = k.ar(f"rGi{d}", [128, 8, 128], F32)
            for sl in range(2):
                B[d, sl] = dict(
                    kq=k.ar(f"kq{d}{sl}", [128, 8, 2, 128], BF16),
                    vt=k.ar(f"vt{d}{sl}", [128, 8, 128], BF16),
                    gbt=k.ar(f"gbt{d}{sl}", [32, 128], F32),
                    gb=k.ar(f"gb{d}{sl}", [128, 32], F32),
                    E=k.ar(f"E{d}{sl}", [128, 24], F32),
                    bege=k.ar(f"bege{d}{sl}", [128, 8], F32),
                    nbeta=k.ar(f"nbeta{d}{sl}", [128, 8], F32),
                    ost=k.ar(f"ost{d}{sl}", [128, 8, 128], BF16),
                    rGa=rGa_, rGi=rGi_,
                )
                for hh in range(2):
                    B[d, sl, hh] = dict(
                        qkm=k.ar(f"qkm{d}{sl}{hh}", [128, 4, 128], BF16),
                        kdec=k.ar(f"kdec{d}{sl}{hh}", [128, 4, 128], BF16),
                        u=k.ar(f"u{d}{sl}{hh}", [128, 4, 128], F32),
                        wT=k.ar(f"wT{d}{sl}{hh}", [128, 4, 128], BF16),
                        qdT=k.ar(f"qdT{d}{sl}{hh}", [128, 4, 128], BF16),
                    )
            for hh in range(2):
                B["t", d, hh] = dict(
                    Dx=k.ar(f"Dx{d}{hh}", [128, 4, 128], F32),
                    DTx=k.ar(f"DTx{d}{hh}", [128, 4, 128], F32),
                    egcb=k.ar(f"egcb{d}{hh}", [128, 4, 128], F32),
                    U=[k.ar(f"U{d}{hh}{i}", [128, 4, 128], CDT) for i in range(1 if CHAIN_FP32 else 2)],
                    W=[k.ar(f"W{d}{hh}{i}", [128, 4, 128], CDT) for i in range(1 if CHAIN_FP32 else 2)],
                    P=[k.ar(f"P{d}{hh}{i}", [128, 4, 128], CDT) for i in range(1 if CHAIN_FP32 else 2)],
                    Pb=k.ar(f"Pb{d}{hh}", [128, 4, 128], BF16),
                    ktok=k.ar(f"ktok{d}{hh}", [128, 4, 128], BF16),
                    bkg=k.ar(f"bkg{d}{hh}", [128, 4, 128], BF16),
                    bv=k.ar(f"bv{d}{hh}", [128, 4, 128], BF16),
                    vnew=k.ar(f"vnew{d}{hh}", [128, 4, 128], BF16),
                    Ssc=k.ar(f"Ssc{d}{hh}", [128, 4, 128], F32),
                    S=k.ar(f"S{d}{hh}", [128, 4, 128], F32),
                    Sb=k.ar(f"Sb{d}{hh}", [128, 4, 128], BF16),
                )
                if CHAIN_FP32:
                    tt_ = B["t", d, hh]
                    tt_["U"].append(tt_["Dx"])
                    tt_["W"].append(tt_["DTx"])
                    tt_["P"].append(tt_["egcb"])
                k.memset("pool", B["t", d, hh]["S"][:], 0.0, w=(B["t", d, hh]["S"],))
                k.memset("pool", B["t", d, hh]["Sb"][:], 0.0, w=(B["t", d, hh]["Sb"],))
        print("arena phase B bytes", k.arena_off)

        def setup(d, c, sl):
            b = B[d, sl]
            Mincl, Maft = masks[:, 2 * d, :], masks[:, 2 * d + 1, :]
            cs = slice(c * 128, (c + 1) * 128)
            k.dma("sp", b["kq"][:, :, 0, :], kTv[:, :, cs], w=(b["kq"],))
            k.dma("sp", b["kq"][:, :, 1, :], qTv[:, :, cs], w=(b["kq"],))
            k.dma("sp", b["vt"][:], vTv[:, :, cs], w=(b["vt"],))
            k.dma("sp", b["gbt"][:], gbT[:, cs], w=(b["gbt"],))
            yield
            pt = nextps()
            k.mm_multi(pt, [(pt[:, 0:32], b["gbt"][:], ident_f[0:32, 0:32], True)], r=(b["gbt"], ident_f))
            k.copy("dve", b["gb"][:], pt[:, 0:32], r=(pt,), w=(b["gb"],))
            yield
            beta = b["gb"][:, d * 8:d * 8 + 8]
            g = b["gb"][:, 16 + d * 8:16 + d * 8 + 8]
            pt = nextps()
            k.mm_multi(pt, [(pt[:, 0:8], Mincl, g, False), (pt[:, 8:16], Maft, g, False),
                            (pt[:, 16:24], ones_f[:], g, False)], r=(masks, ones_f, b["gb"]))
            k.act(b["E"][:], pt[:, 0:24], AF.Exp, r=(pt,), w=(b["E"],))
            k.tt("pool", b["rGa"][:], bc(masks[:, 2 * d + 1:2 * d + 2, :], [128, 8, 128]),
                 bc(g.unsqueeze(2), [128, 8, 128]), ALU.mult, r=(masks, b["gb"]), w=(b["rGa"],))
            k.tt("pool", b["rGi"][:], bc(masks[:, 2 * d:2 * d + 1, :], [128, 8, 128]),
                 bc(g.unsqueeze(2), [128, 8, 128]), ALU.mult, r=(masks, b["gb"]), w=(b["rGi"],))
            yield
            k.tt("dve", b["bege"][:], beta, b["E"][:, 0:8], ALU.mult, r=(b["gb"], b["E"]), w=(b["bege"],))
            k.ts("dve", b["nbeta"][:], beta, -1.0, None, ALU.mult, r=(b["gb"],), w=(b["nbeta"],))
            yield

        def prep(d, c, sl, hh):
            b = B[d, sl]
            bh = B[d, sl, hh]
            t = B["t", d, hh]
            hs = slice(4 * hh, 4 * hh + 4)
            Mincl, Maft = masks[:, 2 * d, :], masks[:, 2 * d + 1, :]
            Mincl_bc = bc(masks[:, 2 * d:2 * d + 1, :], [128, 4, 128])
            Maft_bc = bc(masks[:, 2 * d + 1:2 * d + 2, :], [128, 4, 128])
            beta = b["gb"][:, d * 8 + 4 * hh:d * 8 + 4 * hh + 4]
            cr = (lambda a: a.bitcast(mybir.dt.float32r)) if (CHAIN_FP32 and CHAIN_R) else (lambda a: a)
            pD = nextps()
            k.mm(pD, pD[:], [(Mincl, b["rGa"][:, hs, :])], r=(masks, b["rGa"]))
            k.act(v3(t["Dx"][:].rearrange("p a b -> p (a b)"), 4), v3(pD[:], 4), AF.Exp, r=(pD,), w=(t["Dx"],))
            pDT = nextps()
            k.mm(pDT, pDT[:], [(Maft, b["rGi"][:, hs, :])], r=(masks, b["rGi"]))
            k.act(t["DTx"][:], v3(pDT[:], 4), AF.Exp, r=(pDT,), w=(t["DTx"],))
            yield
            pG = nextps()
            k.mm(pG, pG[:], [(ones_f[:], b["rGi"][:, hs, :])], r=(ones_f, b["rGi"]))
            k.act(t["egcb"][:], v3(pG[:], 4), AF.Exp, r=(pG,), w=(t["egcb"],))
            k.tt("pool", t["Dx"][:], t["Dx"][:], Maft_bc, ALU.mult, r=(t["Dx"], masks), w=(t["Dx"],))
            k.tt("pool", t["Dx"][:], t["Dx"][:], bc(b["nbeta"][:, hs].unsqueeze(2), [128, 4, 128]), ALU.mult,
                 r=(t["Dx"], b["nbeta"]), w=(t["Dx"],))
            k.tt("pool", t["DTx"][:], t["DTx"][:], Mincl_bc, ALU.mult, r=(t["DTx"], masks), w=(t["DTx"],))
            yield
            W0 = t["W"][0]
            for pair in range(2):
                pk = nextps()
                groups = []
                for hl in range(2):
                    h = 4 * hh + 2 * pair + hl
                    groups.append((pk[:, hl * 256:(hl + 1) * 256], b["kq"][:, h, 0, :],
                                   b["kq"][:, h, :, :].rearrange("p a b -> p (a b)"), False))
                k.mm_multi(pk, groups, r=(b["kq"],))
                pkv = v3(pk[:], 2)
                k.tt("dve", cr(W0[:, 2 * pair:2 * pair + 2, :]), pkv[:, :, 0:128], t["Dx"][:, 2 * pair:2 * pair + 2, :],
                     ALU.mult, r=(pk, t["Dx"]), w=(W0,))
                k.tt("dve", bh["qkm"][:, 2 * pair:2 * pair + 2, :], pkv[:, :, 128:256],
                     t["DTx"][:, 2 * pair:2 * pair + 2, :], ALU.mult, r=(pk, t["DTx"]), w=(bh["qkm"],))
            yield
            U0 = t["U"][0]
            pt = nextps()
            ptb = v3(pt[:], 4) if CHAIN_FP32 else v3(pt[:].bitcast(BF16)[:, 0:512], 4)
            k.mm_multi(pt, [(ptb[:, hl, :], W0[:, hl, :], identc[:], True) for hl in range(4)], r=(W0, identc))
            k.copy("dve" if CHAIN_R else "act", cr(U0[:]), ptb, r=(pt,), w=(U0,))
            pt = nextps()
            ptb = v3(pt[:].bitcast(BF16)[:, 0:512], 4)
            k.mm_multi(pt, [(ptb[:, hl, :], b["kq"][:, 4 * hh + hl, 0, :], ident_b[:], True) for hl in range(4)],
                       r=(b["kq"], ident_b))
            k.copy("act", t["ktok"][:], ptb, r=(pt,), w=(t["ktok"],))
            pt = nextps()
            ptb = v3(pt[:].bitcast(BF16)[:, 0:512], 4)
            k.mm_multi(pt, [(ptb[:, hl, :], b["vt"][:, 4 * hh + hl, :], ident_b[:], True) for hl in range(4)],
                       r=(b["vt"], ident_b))
            k.tt("dve", t["bv"][:], ptb, bc(beta.unsqueeze(2), [128, 4, 128]), ALU.mult, r=(pt, b["gb"]),
                 w=(t["bv"],))
            yield
            k.tt("pool", t["bkg"][:], t["ktok"][:], bc(b["bege"][:, hs].unsqueeze(2), [128, 4, 128]), ALU.mult,
                 r=(t["ktok"], b["bege"]), w=(t["bkg"],))
            k.tt("pool", bh["kdec"][:], t["ktok"][:], bc(b["E"][:, 8 + 4 * hh:12 + 4 * hh].unsqueeze(2), [128, 4, 128]),
                 ALU.mult, r=(t["ktok"], b["E"]), w=(bh["kdec"],))
            k.tt("pool", bh["qdT"][:], b["kq"][:, hs, 1, :], t["egcb"][:], ALU.mult, r=(b["kq"], t["egcb"]),
                 w=(bh["qdT"],))
            k.tt("dve", cr(t["P"][0][:]), U0[:], bc(identc[:].unsqueeze(1), [128, 4, 128]), ALU.add,
                 r=(U0, identc), w=(t["P"][0],))
            yield
            for lev in range(6):
                Uc, Wc = t["U"][lev % 2], t["W"][lev % 2]
                Un, Wn = t["U"][(lev + 1) % 2], t["W"][(lev + 1) % 2]
                Pc, Pn = t["P"][lev % 2], t["P"][(lev + 1) % 2]
                pA = nextps()
                k.mm_multi(pA, [(pA[:, hl * 128:(hl + 1) * 128], cr(Wc[:, hl, :]), cr(Uc[:, hl, :]), False) for hl in range(4)],
                           r=(Wc, Uc))
                pB = nextps()
                k.mm_multi(pB, [(pB[:, hl * 128:(hl + 1) * 128], cr(Uc[:, hl, :]), cr(Wc[:, hl, :]), False) for hl in range(4)],
                           r=(Wc, Uc))
                k.copy("dve" if CHAIN_R else "act", cr(Un[:]), v3(pA[:], 4), r=(pA,), w=(Un,))
                k.copy("dve", cr(Wn[:]), v3(pB[:], 4), r=(pB,), w=(Wn,))
                yield
                pC = nextps()
                k.mm_multi(pC, [(pC[:, hl * 128:(hl + 1) * 128], cr(Wn[:, hl, :]), cr(Pc[:, hl, :]), False) for hl in range(4)],
                           r=(Wn, Pc))
                k.tt("dve", cr(Pn[:]), v3(pC[:], 4), Pc[:], ALU.add, r=(pC, Pc), w=(Pn,))
                yield
            Pf = t["P"][0]
            if CHAIN_FP32:
                k.copy("pool", t["Pb"][:], Pf[:], r=(Pf,), w=(t["Pb"],))
                Pf = t["Pb"]
            pu = nextps()
            k.mm_multi(pu, [(pu[:, hl * 128:(hl + 1) * 128], Pf[:, hl, :], t["bv"][:, hl, :], False) for hl in range(4)],
                       r=(Pf, t["bv"]))
            k.copy("act", bh["u"][:], v3(pu[:], 4), r=(pu,), w=(bh["u"],))
            pw = nextps()
            k.mm_multi(pw, [(pw[:, hl * 128:(hl + 1) * 128], t["bkg"][:, hl, :], Pf[:, hl, :], False) for hl in range(4)],
                       r=(Pf, t["bkg"]))
            k.copy("dve", bh["wT"][:], v3(pw[:], 4), r=(pw,), w=(bh["wT"],))
            yield

        def scan(d, c, sl, hh):
            b = B[d, sl]
            bh = B[d, sl, hh]
            t = B["t", d, hh]
            S, Sb = t["S"], t["Sb"]
            pws = nextps()
            k.mm_multi(pws, [(pws[:, hl * 128:(hl + 1) * 128], bh["wT"][:, hl, :], Sb[:, hl, :], False)
                             for hl in range(4)], r=(bh["wT"], Sb))
            k.tt("dve", t["vnew"][:], bh["u"][:], v3(pws[:], 4), ALU.subtract, r=(bh["u"], pws), w=(t["vnew"],))
            k.tt("pool", t["Ssc"][:], S[:], bc(b["E"][:, 16 + 4 * hh:20 + 4 * hh].unsqueeze(2), [128, 4, 128]),
                 ALU.mult, r=(S, b["E"]), w=(t["Ssc"],))
            yield
            po = nextps()

            def fn(g, po=po, Sb=Sb, bh=bh, t=t):
                ins = None
                for hl in range(4):
                    o_ = po[:, hl * 128:(hl + 1) * 128]
                    g.matmul(o_, lhsT=Sb[:, hl, :], rhs=bh["qdT"][:, hl, :], start=True, stop=False)
                    ins = g.matmul(o_, lhsT=t["vnew"][:, hl, :], rhs=bh["qkm"][:, hl, :], start=False, stop=True)
                return ins
            k.op("pe", fn, r=(Sb, bh["qdT"], t["vnew"], bh["qkm"]), w=(po,))
            k.copy("act", b["ost"][:, 4 * hh:4 * hh + 4, :], v3(po[:], 4), r=(po,), w=(b["ost"],))
            pds = nextps()
            k.mm_multi(pds, [(pds[:, hl * 128:(hl + 1) * 128], bh["kdec"][:, hl, :], t["vnew"][:, hl, :], False)
                             for hl in range(4)], r=(bh["kdec"], t["vnew"]))
            k.tt("dve", S[:], t["Ssc"][:], v3(pds[:], 4), ALU.add, r=(t["Ssc"], pds), w=(S,))
            k.copy("act", Sb[:], S[:], r=(S,), w=(Sb,))
            yield

        def store(d, c, sl):
            b = B[d, sl]
            k.dma(STQ, oT[d].rearrange("(h p) n -> p h n", p=128)[:, :, c * 128:(c + 1) * 128], b["ost"][:],
                  r=(b["ost"],))

        def chunk_of(d, s):
            return s if d == 0 else NCK - 1 - s

        run_threads([setup(d, chunk_of(d, 0), 0) for d in range(2)])
        run_threads([prep(d, chunk_of(d, 0), 0, hh) for d in range(2) for hh in range(2)])
        for s in range(NCK):
            sl = s % 2
            th = []
            if s + 1 < NCK:
                run_threads([setup(d, chunk_of(d, s + 1), 1 - sl) for d in range(2)])
                th += [prep(d, chunk_of(d, s + 1), 1 - sl, hh) for d in range(2) for hh in range(2)]
            th += [scan(d, chunk_of(d, s), sl, hh) for d in range(2) for hh in range(2)]
            run_threads(th)
            for d in range(2):
                store(d, chunk_of(d, s), sl)
        k.barrier()


    def phaseC(l, last):
        k.arena_reset()
        x = k.ar("Cx", [128, 8, TW], F32)
        osum = k.ar("Cosum", [128, 8, TW], F32)
        b_of = k.ar("Cof", [128, 8, TW], BF16)
        b_ob = k.ar("Cob", [128, 8, TW], BF16)
        b_zs = k.ar("Czs", [128, 8, TW], BF16)
        b_ysc = k.ar("Cysc", [128, 8, TW], BF16)
        b_ga = k.ar("Cga", [128, 8, TW], BF16)
        b_gb = k.ar("Cgb", [128, 8, TW], BF16)
        a_t = k.ar("Ca", [128, 22, TW], BF16)
        wb = [k.ar(f"Cw{i}", [128, 8, 1024], BF16) for i in range(4)]
        wi = [0]
        tA = [k.ar(f"CtA{i}", [128, TW], F32) for i in range(6)]
        tAi = [0]
        tB = [k.ar(f"CtB{i}", [128, TW], BF16) for i in range(2)]
        tBi = [0]
        rs_t = k.ar("Crstd", [128, TW], F32)
        otile = [k.ar(f"Cot{i}", [128, 1024], F32) for i in range(2)]
        oti = [0]
        print("arena phase C bytes", k.arena_off)
        odn, sq2, mg, h2 = b_of, b_ob, b_zs, b_of
        fm = lambda arr: arr.rearrange("(c p) n -> p c n", p=128)

        def wload(src2, rows0, cols0, ncols, nk=8, dst=None, dcol=0):
            wt = dst if dst is not None else nxt(wb, wi)
            k.dma("sp", wt[:, 0:nk, dcol:dcol + ncols],
                  src2[rows0:rows0 + nk * 128, cols0:cols0 + ncols].rearrange("(c p) n -> p c n", p=128), w=(wt,))
            return wt

        for (t0, W) in cfg.tiles:
            pcol = t0 + PADF
            k.dma("sp", x[:, :, 0:W], xTv[:, :, 2 + t0:2 + t0 + W], w=(x,))
            for buf, arr in ((b_of, oT[0]), (b_ob, oT[1]), (b_zs, zsT), (b_ysc, yscT), (b_ga, gaT), (b_gb, gbgT)):
                k.dma("sp", buf[:, :, 0:W], fm(arr)[:, :, pcol:pcol + W], w=(buf,))
            k.tt("pool", osum[:, :, 0:W], b_of[:, :, 0:W], b_ob[:, :, 0:W], ALU.add, r=(b_of, b_ob), w=(osum,))
            for c in range(8):
                s2 = nxt(tB, tBi)
                k.act(s2[:, 0:W], osum[:, c, 0:W], AF.Square, r=(osum,), w=(s2,))
                p2 = nextps()
                k.mm(p2, p2[:, 0:W], [(ones_b[:], s2[:, 0:W])], r=(s2, ones_b))
                rs = nxt(tA, tAi)
                k.act(rs[:, 0:W], p2[:, 0:W], AF.Sqrt, r=(p2, eps_t), w=(rs,), bias=eps_t[:], scale=1.0 / 128)
                k.recip(rs[:, 0:W], rs[:, 0:W], r=(rs,), w=(rs,))
                on = nxt(tA, tAi)
                k.stt("dve", on[:, 0:W], osum[:, c, 0:W], dnw[l][:, 0:1], rs[:, 0:W], ALU.mult, ALU.mult,
                      r=(osum, dnw[l], rs), w=(on,))
                k.tt("pool", odn[:, c, 0:W], on[:, 0:W], b_zs[:, c, 0:W], ALU.mult, r=(on, b_zs), w=(odn,))
            wdn = wload(wb_bdn[l], 0, 0, 1024)
            wsc = wload(wb_bsc[l], 0, 0, 1024)
            for m in range(8):
                pa = nextps()
                k.mm(pa, pa[:, 0:W], [(wdn[:, c, m * 128:(m + 1) * 128], odn[:, c, 0:W]) for c in range(8)],
                     r=(wdn, odn))
                pb = nextps()
                k.mm(pb, pb[:, 0:W], [(wsc[:, c, m * 128:(m + 1) * 128], b_ysc[:, c, 0:W]) for c in range(8)],
                     r=(wsc, b_ysc))
                t1 = nxt(tA, tAi)
                k.tt("dve", t1[:, 0:W], pa[:, 0:W], b_ga[:, m, 0:W], ALU.mult, r=(pa, b_ga), w=(t1,))
                t2 = nxt(tA, tAi)
                k.tt("dve", t2[:, 0:W], pb[:, 0:W], b_gb[:, m, 0:W], ALU.mult, r=(pb, b_gb), w=(t2,))
                k.tt("pool", mg[:, m, 0:W], t1[:, 0:W], t2[:, 0:W], ALU.add, r=(t1, t2), w=(mg,))
            wo = wload(wb_out[l], 0, 0, 1024)
            for m in range(8):
                pm = nextps()
                k.mm(pm, pm[:, 0:W], [(wo[:, c, m * 128:(m + 1) * 128], mg[:, c, 0:W]) for c in range(8)],
                     r=(wo, mg))
                k.tt("dve", x[:, m, 0:W], x[:, m, 0:W], pm[:, 0:W], ALU.add, r=(x, pm), w=(x,))
            k.act(sq2[:, :, 0:W], x[:, :, 0:W], AF.Square, r=(x,), w=(sq2,))
            pt = nextps()
            k.mm(pt, pt[:, 0:W], [(ones_b[:], sq2[:, c, 0:W]) for c in range(8)], r=(sq2, ones_b))
            k.act(rs_t[:, 0:W], pt[:, 0:W], AF.Sqrt, r=(pt, eps_t), w=(rs_t,), bias=eps_t[:], scale=1.0 / D)
            k.recip(rs_t[:, 0:W], rs_t[:, 0:W], r=(rs_t,), w=(rs_t,))
            for c in range(8):
                k.stt("dve", h2[:, c, 0:W], x[:, c, 0:W], n2w[l][:, c:c + 1], rs_t[:, 0:W], ALU.mult, ALU.mult,
                      r=(x, rs_t, n2w[l]), w=(h2,))
            for j0 in range(0, NFF, 4):
                nj = min(4, NFF - j0)
                wt = nxt(wb, wi)
                wload(wb_gu[l], 0, j0 * 128, nj * 128, dst=wt, dcol=0)
                wload(wb_gu[l], 0, DFF + j0 * 128, nj * 128, dst=wt, dcol=512)
                for jj in range(nj):
                    j = j0 + jj
                    pg = nextps()
                    k.mm(pg, pg[:, 0:W], [(wt[:, c, jj * 128:(jj + 1) * 128], h2[:, c, 0:W]) for c in range(8)],
                         r=(wt, h2))
                    pu = nextps()
                    k.mm(pu, pu[:, 0:W], [(wt[:, c, 512 + jj * 128:512 + (jj + 1) * 128], h2[:, c, 0:W])
                                          for c in range(8)], r=(wt, h2))
                    sg = nxt(tA, tAi)
                    k.act(sg[:, 0:W], pg[:, 0:W], AF.Silu, r=(pg,), w=(sg,))
                    k.tt("dve", a_t[:, j, 0:W], sg[:, 0:W], pu[:, 0:W], ALU.mult, r=(sg, pu), w=(a_t,))
            wd = [wload(wb_down[l], kb * 1024, 0, 1024, nk=min(8, NFF - kb * 8)) for kb in range(3)]
            for m in range(8):
                pd = nextps()
                k.mm(pd, pd[:, 0:W], [(wd[j // 8][:, j % 8, m * 128:(m + 1) * 128], a_t[:, j, 0:W])
                                      for j in range(NFF)], r=(wd[0], wd[1], wd[2], a_t))
                k.tt("dve", x[:, m, 0:W], x[:, m, 0:W], pd[:, 0:W], ALU.add, r=(x, pd), w=(x,))
            if not last:
                k.dma(STQ, xTv[:, :, 2 + t0:2 + t0 + W], x[:, :, 0:W], r=(x,))
            else:
                k.act(sq2[:, :, 0:W], x[:, :, 0:W], AF.Square, r=(x,), w=(sq2,))
                pt = nextps()
                k.mm(pt, pt[:, 0:W], [(ones_b[:], sq2[:, c, 0:W]) for c in range(8)], r=(sq2, ones_b))
                k.act(rs_t[:, 0:W], pt[:, 0:W], AF.Sqrt, r=(pt, eps_t), w=(rs_t,), bias=eps_t[:], scale=1.0 / D)
                k.recip(rs_t[:, 0:W], rs_t[:, 0:W], r=(rs_t,), w=(rs_t,))
                xn = osum
                for c in range(8):
                    k.stt("dve", xn[:, c, 0:W], x[:, c, 0:W], fw[:, c:c + 1], rs_t[:, 0:W], ALU.mult, ALU.mult,
                          r=(x, rs_t, fw), w=(xn,))
                lo = max(t0, NMETA)
                while lo < t0 + W:
                    nn = min(128, t0 + W - lo)
                    ot = nxt(otile, oti)
                    for half in range(2):
                        pt = nextps()
                        k.mm_multi(pt, [(pt[0:nn, cc * 128:(cc + 1) * 128],
                                         xn[:, half * 4 + cc, lo - t0:lo - t0 + nn], ident_f[:], True)
                                        for cc in range(4)], r=(xn, ident_f))
                        k.copy("act" if half else "dve", ot[0:nn, half * 512:(half + 1) * 512], pt[0:nn, :],
                               r=(pt,), w=(ot,))
                    k.dma(STQ, out[lo - NMETA:lo - NMETA + nn, :], ot[0:nn, :], r=(ot,))
                    lo += nn
        k.barrier()

    phases = cfg.__dict__.get("phases", "ABC")
    nl = cfg.__dict__.get("nlayers", DEPTH)
    for l in range(nl):
        phaseA(l)
        if "B" in phases:
            phaseB(l)
        if "C" in phases:
            phaseC(l, l == nl - 1)

    k.barrier()
    with nc.Block() as block:
        @block.tensor
        def _(g):
            k.replay("pe", g)

        @block.scalar
        def _(g):
            k.replay("act", g)

        @block.vector
        def _(g):
            k.replay("dve", g)

        @block.gpsimd
        def _(g):
            k.replay("pool", g)

        @block.sync
        def _(g):
            k.replay("sp", g)
    es.close()
    return nc


def make_masks():
    m = np.zeros((8, 128, 128), np.float32)
    t = np.arange(128)[:, None]
    i = np.arange(128)[None, :]
    m[0] = (t <= i)
    m[1] = (t > i)
    m[2] = (t >= i)
    m[3] = (t < i)
    return m


def core_inputs(cfg, inputs, b):
    f = np.float32
    xin = np.concatenate([inputs["meta_tokens"].astype(f), inputs["x"][b].astype(f)], axis=0)
    d = dict(
        xin=np.ascontiguousarray(xin),
        norm1_w=inputs["norm1_w"], w_in=inputs["w_in"], dn_conv_w=inputs["dn_conv_w"],
        A_log=inputs["A_log"].reshape(cfg.depth, 16), dt_bias=inputs["dt_bias"].reshape(cfg.depth, 16),
        dn_norm_w=inputs["dn_norm_w"], sc_conv_w=inputs["sc_conv_w"],
        w_branch_dn=inputs["w_branch_dn"], w_branch_sc=inputs["w_branch_sc"], w_out=inputs["w_out"],
        norm2_w=inputs["norm2_w"], w_gate_up=inputs["w_gate_up"], w_down=inputs["w_down"],
        final_norm_w=inputs["final_norm_w"],
        c_ident_f=np.eye(128, dtype=f), c_masks=make_masks())
    return {k_: np.ascontiguousarray(np.asarray(v, dtype=f)) for k_, v in d.items()}


_NC_CACHE = {}


def kernel(**inputs):
    x = inputs["x"]
    bsz, seq, _ = x.shape
    cfg = Cfg(seq, 2)
    key = (seq,)
    if key not in _NC_CACHE:
        _NC_CACHE[key] = build(cfg)
    nc = _NC_CACHE[key]
    in_maps = [core_inputs(cfg, inputs, b) for b in range(bsz)]
    res = run_bass_kernel_spmd(nc, in_maps, core_ids=list(range(bsz)))
    return np.stack([np.asarray(r["out"], dtype=np.float32) for r in res.results], axis=0)
```

```python
import numpy as np
import ml_dtypes
from contextlib import ExitStack
import concourse.bass as bass
import concourse.mybir as mybir
from concourse.bass_utils import run_bass_kernel_spmd

F32 = mybir.dt.float32
BF16 = mybir.dt.bfloat16
AF = mybir.ActivationFunctionType
ALU = mybir.AluOpType

D = 1024
NCH = 8
H = 8
NMETA = 16
DFF = 2816
NFF = 22
WIN = 9248
RMS_EPS = 1e-6
L2_EPS = 1e-6
TW = 508
SEM_ROLL = 30000
CHAIN_FP32 = True
CHAIN_R = False
STQ = "sp"

C_Q, C_K, C_V, C_Z, C_B, C_A, C_SB, C_SC, C_SX, C_GA, C_GB = (
    0, 1024, 2048, 3072, 4096, 4112, 4128, 5152, 6176, 7200, 8224)


class Tl:
    def __init__(self, name, ap):
        self.name = name
        self.ap = ap
        self.w = None
        self.r = {}

    def __getitem__(self, idx):
        return self.ap[idx]


class Eng:
    def __init__(self, name, sems):
        self.name = name
        self.sems = sems
        self.si = 0
        self.cnt = 0
        self.ops = []
        self.waited = {}


class KB:
    def __init__(self, nc, es):
        self.nc = nc
        self.es = es
        self.semh = []
        self.eng = {}
        for n in ("pe", "act", "dve", "pool", "sp"):
            ids = [self._newsem(f"s_{n}{i}") for i in range(3)]
            self.eng[n] = Eng(n, ids)
        self.dq = {}
        for q, cnt in (("sp", 40), ("pool", 4), ("act", 36)):
            self.dq[q] = dict(ids=[self._newsem(f"d_{q}{i}") for i in range(cnt)],
                              cnt=[0] * cnt, nxt=0)
        self.all_tokens = {}
        self.ntile = 0

    def _newsem(self, name):
        h = self.es.enter_context(self.nc.semaphore(name))
        self.semh.append(h)
        return len(self.semh) - 1

    def sb(self, name, shape, dt):
        t = self.es.enter_context(self.nc.sbuf_tensor(name, list(shape), dt))
        return Tl(name, t)

    def arena_init(self, nbytes):
        self.arena = self.es.enter_context(self.nc.sbuf_tensor("arena", [128, nbytes // 2], BF16))
        self.arena_size = nbytes
        self.arena_off = 0

    def arena_reset(self):
        self.arena_off = 0

    def ar(self, name, shape, dt):
        esz = 4 if dt == F32 else 2
        n = 1
        for d_ in shape[1:]:
            n *= d_
        nb = (n * esz + 31) // 32 * 32
        off = self.arena_off
        assert off + nb <= self.arena_size, f"arena overflow at {name}: {off + nb}"
        self.arena_off += nb
        ap = self.arena[0:shape[0], off // 2:(off + n * esz) // 2]
        if dt == F32:
            ap = ap.bitcast(F32)
        if len(shape) == 3:
            ap = ap.rearrange("p (a b) -> p a b", a=shape[1])
        elif len(shape) == 4:
            ap = ap.rearrange("p (a b c) -> p a b c", a=shape[1], b=shape[2])
        return Tl(name, ap)

    def ps(self, name, shape, dt=F32):
        t = self.es.enter_context(self.nc.psum_tensor(name, list(shape), dt))
        return Tl(name, t)

    def _deps(self, e, r, w, extra=()):
        waits = {}

        def add(tok):
            if tok is None:
                return
            s, v = tok
            if waits.get(s, 0) < v:
                waits[s] = v
        for t in r:
            add(t.w)
        for t in w:
            add(t.w)
            for s, v in t.r.items():
                add((s, v))
        for tok in extra:
            add(tok)
        need = []
        for s, v in waits.items():
            if e.waited.get(s, 0) < v:
                e.waited[s] = v
                need.append((s, v))
        return need

    def _mark(self, tok, r, w):
        s, v = tok
        for t in r:
            if t.r.get(s, 0) < v:
                t.r[s] = v
        for t in w:
            t.w = tok
            t.r = {}
        if self.all_tokens.get(s, 0) < v:
            self.all_tokens[s] = v

    def op(self, engine, fn, r=(), w=(), extra=()):
        e = self.eng[engine]
        need = self._deps(e, r, w, extra)
        if e.cnt >= SEM_ROLL:
            e.si += 1
            e.cnt = 0
        e.cnt += 1
        sid = e.sems[e.si]
        tok = (sid, e.cnt)
        e.ops.append((need, fn, sid, 1))
        self._mark(tok, r, w)
        return tok

    def dma(self, queue, out, in_, r=(), w=(), extra=()):
        e = self.eng[queue]
        dq = self.dq[queue]
        i = dq["nxt"]
        dq["nxt"] = (i + 1) % len(dq["ids"])
        sid = dq["ids"][i]
        prev = (sid, dq["cnt"][i]) if dq["cnt"][i] else None
        need = self._deps(e, r, w, tuple(extra) + ((prev,) if prev else ()))
        dq["cnt"][i] += 16
        tok = (sid, dq["cnt"][i])
        e.ops.append((need, lambda g, o=out, s=in_: g.dma_start(out=o, in_=s), sid, 16))
        self._mark(tok, r, w)
        return tok

    def barrier(self):
        toks = list(self.all_tokens.items())
        for e in self.eng.values():
            need = []
            for s, v in toks:
                if e.waited.get(s, 0) < v:
                    e.waited[s] = v
                    need.append((s, v))
            if need:
                e.ops.append((need, None, None, 0))

    def replay(self, engine, g):
        for need, fn, sid, inc in self.eng[engine].ops:
            for s, v in need:
                g.wait_ge(self.semh[s], v)
            if fn is None:
                continue
            ins = fn(g)
            if inc:
                ins.then_inc(self.semh[sid], inc)

    def mm(self, out_t, out_ap, pairs, r=(), transpose=False):
        n = len(pairs)

        def fn(g, out_ap=out_ap, pairs=pairs, n=n):
            ins = None
            for i, (a, b) in enumerate(pairs):
                ins = g.matmul(out_ap, lhsT=a, rhs=b, start=(i == 0), stop=(i == n - 1))
            return ins
        return self.op("pe", fn, r=r, w=(out_t,))

    def mm_multi(self, out_t, groups, r=()):
        def fn(g, groups=groups):
            ins = None
            for (o, a, b, tr) in groups:
                if tr:
                    ins = g.transpose(o, a, b)
                else:
                    ins = g.matmul(o, lhsT=a, rhs=b, start=True, stop=True)
            return ins
        return self.op("pe", fn, r=r, w=(out_t,))

    def act(self, out, in_, func, r=(), w=(), bias=None, scale=None, eng="act"):
        kw = {}
        if bias is not None:
            kw["bias"] = bias
        if scale is not None:
            kw["scale"] = scale
        return self.op(eng, lambda g, o=out, i=in_, f=func, kw=kw: g.activation(out=o, in_=i, func=f, **kw),
                       r=r, w=w)

    def tt(self, eng, out, in0, in1, op, r=(), w=()):
        return self.op(eng, lambda g, o=out, a=in0, b=in1, p=op: g.tensor_tensor(out=o, in0=a, in1=b, op=p),
                       r=r, w=w)

    def stt(self, eng, out, in0, scalar, in1, op0, op1, r=(), w=()):
        return self.op(eng, lambda g, o=out, a=in0, s=scalar, b=in1, p0=op0, p1=op1:
                       g.scalar_tensor_tensor(out=o, in0=a, scalar=s, in1=b, op0=p0, op1=p1), r=r, w=w)

    def ts(self, eng, out, in0, s1, s2, op0, op1=None, r=(), w=()):
        if op1 is None:
            return self.op(eng, lambda g, o=out, a=in0, s=s1, p0=op0:
                           g.tensor_scalar(out=o, in0=a, scalar1=s, scalar2=None, op0=p0), r=r, w=w)
        return self.op(eng, lambda g, o=out, a=in0, x=s1, y=s2, p0=op0, p1=op1:
                       g.tensor_scalar(out=o, in0=a, scalar1=x, scalar2=y, op0=p0, op1=p1), r=r, w=w)

    def copy(self, eng, out, in_, r=(), w=()):
        if eng == "act":
            return self.op(eng, lambda g, o=out, i=in_: g.copy(out=o, in_=i), r=r, w=w)
        return self.op(eng, lambda g, o=out, i=in_: g.tensor_copy(out=o, in_=i), r=r, w=w)

    def recip(self, out, in_, r=(), w=()):
        return self.op("dve", lambda g, o=out, i=in_: g.reciprocal(out=o, in_=i), r=r, w=w)

    def memset(self, eng, ap, val, w=()):
        return self.op(eng, lambda g, a=ap, v=val: g.memset(a, v), w=w)


def bc(ap, shape):
    return ap.to_broadcast(list(shape))


class Cfg:
    def __init__(self, seq, depth):
        self.seq = seq
        self.depth = depth
        self.L = seq + NMETA
        self.PADF = (-self.L) % 128
        self.T = self.L + self.PADF
        self.NCK = self.T // 128
        self.XOFF = 2
        self.XW = self.L + 4
        self.tiles = []
        t0 = 0
        while t0 < self.L:
            w = min(TW, self.L - t0)
            self.tiles.append((t0, w))
            t0 += w


def build(cfg, debug=False):
    nc = bass.Bass("TRN2", target_bir_lowering=False)
    L, T, DEPTH = cfg.L, cfg.T, cfg.depth
    es = ExitStack()
    k = KB(nc, es)

    def din(name, shape, dt=F32):
        return nc.dram_tensor(name, list(shape), dt, kind="ExternalInput").ap()

    def dscr(name, shape, dt):
        kind = "ExternalOutput" if debug else "Internal"
        return nc.dram_tensor(name, list(shape), dt, kind=kind).ap()

    xin = din("xin", [L, D])
    norm1_w = din("norm1_w", [DEPTH, D])
    w_in = din("w_in", [DEPTH, D, WIN])
    dn_conv_w = din("dn_conv_w", [DEPTH, 5, 3072])
    A_log = din("A_log", [DEPTH, 16])
    dt_bias = din("dt_bias", [DEPTH, 16])
    dn_norm_w = din("dn_norm_w", [DEPTH, 128])
    sc_conv_w = din("sc_conv_w", [DEPTH, 3, 1024])
    w_bdn = din("w_branch_dn", [DEPTH, D, D])
    w_bsc = din("w_branch_sc", [DEPTH, D, D])
    w_out = din("w_out", [DEPTH, D, D])
    norm2_w = din("norm2_w", [DEPTH, D])
    w_gu = din("w_gate_up", [DEPTH, D, 2 * DFF])
    w_down = din("w_down", [DEPTH, DFF, D])
    final_w = din("final_norm_w", [D])
    c_ident_f = din("c_ident_f", [128, 128])
    c_masks = din("c_masks", [8, 128, 128])
    out = nc.dram_tensor("out", [cfg.seq, D], F32, kind="ExternalOutput").ap()

    wb_in = dscr("wb_in", [DEPTH, D, WIN], BF16)
    wb_bdn = dscr("wb_bdn", [DEPTH, D, D], BF16)
    wb_bsc = dscr("wb_bsc", [DEPTH, D, D], BF16)
    wb_out = dscr("wb_out", [DEPTH, D, D], BF16)
    wb_gu = dscr("wb_gu", [DEPTH, D, 2 * DFF], BF16)
    wb_down = dscr("wb_down", [DEPTH, DFF, D], BF16)
    xT = dscr("xT", [D, cfg.XW], F32)
    qT = dscr("qT", [D, T], BF16)
    kT = dscr("kT", [D, T], BF16)
    vT = dscr("vT", [D, T], BF16)
    gbT = dscr("gbT", [32, T], F32)
    zsT = dscr("zsT", [D, T], BF16)
    yscT = dscr("yscT", [D, T], BF16)
    gaT = dscr("gaT", [D, T], BF16)
    gbgT = dscr("gbgT", [D, T], BF16)
    oT = [dscr(f"oT{d}", [D, T], BF16) for d in range(2)]

    ident_f = k.sb("ident_f", [128, 128], F32)
    ident_b = k.sb("ident_b", [128, 128], BF16)
    ones_b = k.sb("ones_b", [128, 128], BF16)
    ones_f = k.sb("ones_f", [128, 128], F32)
    masks = k.sb("masks", [128, 8, 128], F32)
    zero_b = k.sb("zero_b", [128, 1024], BF16)
    zero_f = k.sb("zero_f", [128, 512], F32)
    k.dma("sp", ident_f[:], c_ident_f[:, :], w=(ident_f,))
    k.dma("sp", masks[:], c_masks.rearrange("m p n -> p m n"), w=(masks,))
    k.copy("dve", ident_b[:], ident_f[:], r=(ident_f,), w=(ident_b,))
    k.memset("dve", ones_b[:], 1.0, w=(ones_b,))
    k.memset("dve", ones_f[:], 1.0, w=(ones_f,))
    k.memset("pool", zero_b[:], 0.0, w=(zero_b,))
    k.memset("pool", zero_f[:], 0.0, w=(zero_f,))

    def load_vec(name, src_ap, nchunk):
        t = k.sb(name, [128, nchunk], F32)
        k.dma("sp", t[:], src_ap.rearrange("(c p) -> p c", p=128), w=(t,))
        return t

    nc_allow = nc.allow_non_contiguous_dma(reason="tiny parameter vectors")
    es.enter_context(nc_allow)

    n1w = [load_vec(f"n1w{l}", norm1_w[l], 8) for l in range(DEPTH)]
    n2w = [load_vec(f"n2w{l}", norm2_w[l], 8) for l in range(DEPTH)]
    fw = load_vec("fw", final_w, 8)
    dcw = []
    scw = []
    dnw = []
    nAexp = []
    dtb = []
    for l in range(DEPTH):
        t = k.sb(f"dcw{l}", [128, 5, 24], F32)
        k.dma("sp", t[:], dn_conv_w[l].rearrange("d (c p) -> p d c", p=128), w=(t,))
        dcw.append(t)
        t = k.sb(f"scw{l}", [128, 3, 8], F32)
        k.dma("sp", t[:], sc_conv_w[l].rearrange("d (c p) -> p d c", p=128), w=(t,))
        scw.append(t)
        t = k.sb(f"dnw{l}", [128, 1], F32)
        k.dma("sp", t[:], dn_norm_w[l].rearrange("(p o) -> p o", o=1), w=(t,))
        dnw.append(t)
        ta = k.sb(f"alog{l}", [16, 1], F32)
        k.dma("sp", ta[:], A_log[l].rearrange("(p o) -> p o", o=1), w=(ta,))
        tb = k.sb(f"dtb{l}", [16, 1], F32)
        k.dma("sp", tb[:], dt_bias[l].rearrange("(p o) -> p o", o=1), w=(tb,))
        dtb.append(tb)
        te = k.sb(f"nAexp{l}", [16, 1], F32)
        k.act(te[:], ta[:], AF.Exp, r=(ta,), w=(te,))
        k.ts("dve", te[:], te[:], -1.0, None, ALU.mult, r=(te,), w=(te,))
        nAexp.append(te)
    eps_t = k.sb("eps_t", [128, 1], F32)
    k.memset("dve", eps_t[:], RMS_EPS, w=(eps_t,))
    eps128_t = k.sb("eps128_t", [128, 1], F32)
    k.memset("dve", eps128_t[:], 128.0 * L2_EPS, w=(eps128_t,))
    one_t = k.sb("one_t", [128, 1], F32)
    k.memset("dve", one_t[:], 1.0, w=(one_t,))

    k.arena_init(196 * 1024)
    CW = 4096
    cf = [k.ar(f"castf{i}", [128, CW], F32) for i in range(3)]
    cb = [k.ar(f"castb{i}", [128, CW], BF16) for i in range(3)]
    cidx = [0]

    def cast_w(dst, src, rows, cols):
        for l in range(DEPTH):
            for r0 in range(0, rows, 128):
                for c0 in range(0, cols, CW):
                    cw = min(CW, cols - c0)
                    i = cidx[0] % 3
                    cidx[0] += 1
                    k.dma("sp", cf[i][:, 0:cw], src[l, r0:r0 + 128, c0:c0 + cw], w=(cf[i],))
                    eng = ("act", "dve", "pool")[i]
                    k.copy(eng, cb[i][:, 0:cw], cf[i][:, 0:cw], r=(cf[i],), w=(cb[i],))
                    k.dma(STQ, dst[l, r0:r0 + 128, c0:c0 + cw], cb[i][:, 0:cw], r=(cb[i],))
    cast_w(wb_in, w_in, D, WIN)
    cast_w(wb_bdn, w_bdn, D, D)
    cast_w(wb_bsc, w_bsc, D, D)
    cast_w(wb_out, w_out, D, D)
    cast_w(wb_gu, w_gu, D, 2 * DFF)
    cast_w(wb_down, w_down, DFF, D)
    k.barrier()

    PADF = cfg.PADF
    if PADF:
        for arr in (qT, kT, vT):
            k.dma("sp", arr.rearrange("(c p) n -> p c n", p=128)[:, :, 0:PADF],
                  zero_b[:, 0:8 * PADF].rearrange("p (c n) -> p c n", c=8), r=(zero_b,))
        k.dma("sp", gbT[:, 0:PADF], zero_f[0:32, 0:PADF], r=(zero_f,))
    xTv = xT.rearrange("(c p) n -> p c n", p=128)
    k.dma("sp", xTv[:, :, 0:2], zero_f[:, 0:16].rearrange("p (c n) -> p c n", c=8), r=(zero_f,))
    k.dma("sp", xTv[:, :, L + 2:L + 4], zero_f[:, 0:16].rearrange("p (c n) -> p c n", c=8), r=(zero_f,))

    PS = [k.ps(f"ps{i}", [128, 512], F32) for i in range(8)]
    psi = [0]

    def nextps():
        t = PS[psi[0] % 8]
        psi[0] += 1
        return t

    k.arena_reset()
    p0_in = [k.ar(f"p0in{i}", [128, 4, D], F32) for i in range(2)]
    p0_out = [k.ar(f"p0out{i}", [128, 8, 512], F32) for i in range(2)]
    it = 0
    for t0 in range(0, L, 512):
        n = min(512, L - t0)
        tin = p0_in[it % 2]
        tout = p0_out[it % 2]
        nb = (n + 127) // 128
        for b in range(nb):
            nn = min(128, n - b * 128)
            k.dma("sp", tin[0:nn, b, :], xin[t0 + b * 128:t0 + b * 128 + nn, :], w=(tin,))
        for c in range(8):
            pt = nextps()
            groups = []
            for b in range(nb):
                nn = min(128, n - b * 128)
                groups.append((pt[:, b * 128:b * 128 + nn], tin[0:nn, b, c * 128:(c + 1) * 128],
                               ident_f[0:nn, 0:nn], True))
            k.mm_multi(pt, groups, r=(tin, ident_f))
            k.copy("act" if c % 2 else "dve", tout[:, c, 0:n], pt[:, 0:n], r=(pt,), w=(tout,))
        k.dma("sp", xTv[:, :, 2 + t0:2 + t0 + n], tout[:, :, 0:n], r=(tout,))
        it += 1
    k.barrier()

    NCMAX = TW + 4

    def nxt(lst, ctr):
        t = lst[ctr[0] % len(lst)]
        ctr[0] += 1
        return t

    psfree = []

    def ps_alloc():
        return psfree.pop(0)

    def ps_free(t):
        psfree.append(t)

    def run_pipeline(tasks, depth, budget=7):
        psfree[:] = list(PS)
        live = []
        pending = list(tasks)
        pi = 0
        while True:
            while pi < len(pending) and len(live) < depth:
                tk = pending[pi]
                nb = getattr(tk, "nb", 0)
                if sum(n for _, n in live) + nb > budget:
                    break
                pi += 1
                g_ = tk()
                if g_ is not None:
                    try:
                        next(g_)
                        live.append((g_, nb))
                    except StopIteration:
                        pass
            if not live:
                if pi >= len(pending):
                    break
                continue
            for ent in list(live):
                try:
                    next(ent[0])
                except StopIteration:
                    live.remove(ent)

    def phaseA(l):
        k.arena_reset()
        xts = [k.ar(f"xt{i}", [128, 8, NCMAX], F32) for i in range(1)]
        hTs = [k.ar(f"hT{i}", [128, 8, NCMAX], BF16) for i in range(2)]
        sq = k.ar("sq", [128, 8, NCMAX], BF16)
        rstd = k.ar("rstd", [128, NCMAX], F32)
        wbuf = [k.ar(f"wbuf{i}", [128, 8, 1024], BF16) for i in range(4)]
        wsm = k.ar("wsm", [128, 8, 32], BF16)
        stg = [k.ar(f"stg{i}", [128, 8, TW], BF16) for i in range(3)]
        tmpA = [k.ar(f"tmpA{i}", [128, NCMAX], F32) for i in range(8)]
        ssq8 = [k.ar(f"ssq8{i}", [128, 8, TW], F32) for i in range(2)]
        tmpB = [k.ar(f"tmpB{i}", [128, NCMAX], BF16) for i in range(6)]

        def ta():
            return tmpA.pop(0)

        def tb():
            return tmpB.pop(0)
        gbs = k.ar("gbs", [16, 2, NCMAX], F32)
        print("arena phase A bytes", k.arena_off)
        wl = wb_in[l]
        fm = lambda arr: arr.rearrange("(c p) n -> p c n", p=128)
        BLK = [C_Q, C_K, C_V, C_Z, C_SC, C_SX, C_SB, C_GA, C_GB]
        nblk = len(BLK)
        wslot = {}
        gblk = [0]

        def t_wload(gi):
            def f():
                wt = wbuf[gi % 4]
                k.dma("sp", wt[:, :, :], wl[:, BLK[gi % nblk]:BLK[gi % nblk] + 1024].rearrange("(c p) n -> p c n", p=128),
                      w=(wt,))
                wslot[gi] = wt
            return f

        def t_xload(ti):
            def f():
                t0, W = cfg.tiles[ti]
                NC = W + 4
                k.dma("sp", xts[0][:, :, 0:NC], xTv[:, :, t0:t0 + NC], w=(xts[0],))
            return f

        def t_pro1(ti):
            def f():
                t0, W = cfg.tiles[ti]
                NC = W + 4
                xt = xts[0]
                k.act(sq[:, :, 0:NC], xt[:, :, 0:NC], AF.Square, r=(xt,), w=(sq,))
            return f

        def t_pro2(ti):
            def f():
                t0, W = cfg.tiles[ti]
                NC = W + 4
                xt, hT = xts[0], hTs[ti % 2]
                pt = ps_alloc()
                k.mm(pt, pt[:, 0:NC], [(ones_b[:], sq[:, c, 0:NC]) for c in range(8)], r=(sq, ones_b))
                k.act(rstd[:, 0:NC], pt[:, 0:NC], AF.Sqrt, r=(pt, eps_t), w=(rstd,), bias=eps_t[:], scale=1.0 / D)
                ps_free(pt)
                k.recip(rstd[:, 0:NC], rstd[:, 0:NC], r=(rstd,), w=(rstd,))
                for c in range(8):
                    k.stt("dve", hT[:, c, 0:NC], xt[:, c, 0:NC], n1w[l][:, c:c + 1],
                          rstd[:, 0:NC], ALU.mult, ALU.mult, r=(xt, rstd, n1w[l]), w=(hT,))
            return f

        def proj(hT, NC, wt, j, M=128):
            pt = ps_alloc()
            k.mm(pt, pt[0:M, 0:NC], [(wt[:, c, j:j + M], hT[:, c, 0:NC]) for c in range(8)], r=(wt, hT))
            return pt

        class Grp:
            def __init__(self, st, dst, pcol, W, n=8, norm=None, ssq=None):
                self.st, self.dst, self.pcol, self.W, self.left = st, dst, pcol, W, n
                self.norm, self.ssq = norm, ssq

            def done(self):
                self.left -= 1
                if self.left == 0:
                    W = self.W
                    if self.norm is not None:
                        sq_ = self.ssq
                        if self.norm == 0:
                            k.act(sq_[:, :, 0:W], sq_[:, :, 0:W], AF.Sqrt, r=(sq_, eps128_t), w=(sq_,),
                                  bias=eps128_t[:], scale=128.0)
                        else:
                            k.act(sq_[:, :, 0:W], sq_[:, :, 0:W], AF.Sqrt, r=(sq_, eps_t), w=(sq_,),
                                  bias=eps_t[:], scale=1.0)
                        k.recip(sq_[:, :, 0:W], sq_[:, :, 0:W], r=(sq_,), w=(sq_,))
                        k.tt("dve", self.st[:, :, 0:W], self.st[:, :, 0:W], sq_[:, :, 0:W], ALU.mult,
                             r=(self.st, sq_), w=(self.st,))
                    k.dma(STQ, fm(self.dst)[:, :, self.pcol:self.pcol + W], self.st[:, :, 0:W],
                          r=(self.st,))

        def t_qkv(ti, gi, grp, c, G):
            def gen():
                t0, W = cfg.tiles[ti]
                NC = W + 4
                hT = hTs[ti % 2]
                wt = wslot[gi]
                st = G.st
                pt = proj(hT, NC, wt, c * 128)
                yield
                cc = grp * 8 + c
                acc = ta()
                k.ts("dve", acc[:, 0:W], pt[:, 0:W], dcw[l][:, 0, cc:cc + 1], None, ALU.mult,
                     r=(pt, dcw[l]), w=(acc,))
                for d in range(1, 5):
                    k.stt("dve", acc[:, 0:W], pt[:, d:d + W], dcw[l][:, d, cc:cc + 1], acc[:, 0:W],
                          ALU.mult, ALU.add, r=(pt, dcw[l], acc), w=(acc,))
                ps_free(pt)
                yield
                if grp == 2:
                    k.act(st[:, c, 0:W], acc[:, 0:W], AF.Silu, r=(acc,), w=(st,))
                    tmpA.append(acc)
                    G.done()
                    return
                k.act(st[:, c, 0:W], acc[:, 0:W], AF.Silu, r=(acc,), w=(st,))
                tmpA.append(acc)
                s2 = tb()
                k.act(s2[:, 0:W], st[:, c, 0:W], AF.Square, r=(st,), w=(s2,))
                yield
                p2 = ps_alloc()
                k.mm(p2, p2[:, 0:W], [(ones_b[:], s2[:, 0:W])], r=(s2, ones_b))
                tmpB.append(s2)
                yield
                k.copy("act", G.ssq[:, c, 0:W], p2[:, 0:W], r=(p2,), w=(G.ssq,))
                ps_free(p2)
                G.done()
            gen.nb = 1
            return gen

        def t_simple(ti, gi, c, G, func):
            def gen():
                t0, W = cfg.tiles[ti]
                NC = W + 4
                pt = proj(hTs[ti % 2], NC, wslot[gi], c * 128)
                yield
                k.act(G.st[:, c, 0:W], pt[:, 2:2 + W], func, r=(pt,), w=(G.st,))
                ps_free(pt)
                G.done()
            gen.nb = 1
            return gen

        def t_bg(ti):
            def gen():
                t0, W = cfg.tiles[ti]
                NC = W + 4
                pcol = t0 + PADF
                hT = hTs[ti % 2]
                k.dma("sp", wsm[:, :, :], wl[:, C_B:C_B + 32].rearrange("(c p) n -> p c n", p=128), w=(wsm,))
                pts = []
                for which in range(2):
                    pt = ps_alloc()
                    k.mm(pt, pt[0:16, 0:NC], [(wsm[:, c, which * 16:which * 16 + 16], hT[:, c, 0:NC])
                                              for c in range(8)], r=(wsm, hT))
                    pts.append(pt)
                yield
                k.act(gbs[:, 0, 0:W], pts[0][0:16, 2:2 + W], AF.Sigmoid, r=(pts[0],), w=(gbs,))
                k.act(gbs[:, 1, 0:W], pts[1][0:16, 2:2 + W], AF.Exp, r=(pts[1], dtb[l]), w=(gbs,), bias=dtb[l][:])
                ps_free(pts[0])
                ps_free(pts[1])
                k.act(gbs[:, 1, 0:W], gbs[:, 1, 0:W], AF.Ln, r=(gbs, one_t), w=(gbs,), bias=one_t[0:16, :])
                yield
                k.ts("dve", gbs[:, 1, 0:W], gbs[:, 1, 0:W], nAexp[l][:, 0:1], None, ALU.mult,
                     r=(gbs, nAexp[l]), w=(gbs,))
                k.dma(STQ, gbT.rearrange("(a p) n -> p a n", p=16)[:, :, pcol:pcol + W], gbs[:, :, 0:W], r=(gbs,))
            gen.nb = 2
            return gen

        def t_sc(ti, gi_c, gi_x, gi_b, c, G):
            def gen():
                t0, W = cfg.tiles[ti]
                NC = W + 4
                hT = hTs[ti % 2]
                pc = proj(hT, NC, wslot[gi_c], c * 128)
                px = proj(hT, NC, wslot[gi_x], c * 128)
                pb = proj(hT, NC, wslot[gi_b], c * 128)
                yield
                cx = ta()
                k.copy("act", cx[:, 0:NC], pc[:, 0:NC], r=(pc,), w=(cx,))
                ps_free(pc)
                yield
                pr = ta()
                k.tt("dve", pr[:, 0:NC], px[:, 0:NC], cx[:, 0:NC], ALU.mult, r=(px, cx), w=(pr,))
                ps_free(px)
                tmpA.append(cx)
                yield
                acc = ta()
                k.ts("dve", acc[:, 0:W], pr[:, 1:1 + W], scw[l][:, 0, c:c + 1], None, ALU.mult,
                     r=(pr, scw[l]), w=(acc,))
                for d in range(1, 3):
                    k.stt("dve", acc[:, 0:W], pr[:, 1 + d:1 + d + W], scw[l][:, d, c:c + 1], acc[:, 0:W],
                          ALU.mult, ALU.add, r=(pr, scw[l], acc), w=(acc,))
                tmpA.append(pr)
                yield
                k.tt("dve", G.st[:, c, 0:W], pb[:, 2:2 + W], acc[:, 0:W], ALU.mult, r=(pb, acc), w=(G.st,))
                ps_free(pb)
                tmpA.append(acc)
                G.done()
            gen.nb = 3
            return gen

        tasks = []
        ntile = len(cfg.tiles)
        stgc = [0]
        total_blocks = ntile * nblk
        tasks.append(t_xload(0))
        for gi in range(4):
            tasks.append(t_wload(gi))
        tasks.append(t_pro1(0))
        tasks.append(t_pro2(0))
        for ti, (t0, W) in enumerate(cfg.tiles):
            pcol = t0 + PADF
            base = ti * nblk

            def post(gi):
                if gi + 4 < total_blocks:
                    tasks.append(t_wload(gi + 4))

            def newG(dst, n=8, norm=None, ssq=None):
                st = stg[stgc[0] % 3]
                stgc[0] += 1
                return Grp(st, dst, pcol, W, n, norm, ssq)
            for grp, dst in enumerate((qT, kT, vT)):
                G = newG(dst, norm=(grp if grp < 2 else None), ssq=(ssq8[grp] if grp < 2 else None))
                for c in range(8):
                    tasks.append(t_qkv(ti, base + grp, grp, c, G))
                post(base + grp)
                if grp == 1 and ti + 1 < ntile:
                    tasks.append(t_xload(ti + 1))
            G = newG(zsT)
            for c in range(8):
                tasks.append(t_simple(ti, base + 3, c, G, AF.Silu))
            post(base + 3)
            tasks.append(t_bg(ti))
            G = newG(yscT)
            for c in range(8):
                tasks.append(t_sc(ti, base + 4, base + 5, base + 6, c, G))
            post(base + 4)
            post(base + 5)
            post(base + 6)
            if ti + 1 < ntile:
                tasks.append(t_pro1(ti + 1))
            for bi, dst in ((7, gaT), (8, gbgT)):
                G = newG(dst)
                for c in range(8):
                    tasks.append(t_simple(ti, base + bi, c, G, AF.Sigmoid))
                post(base + bi)
                if bi == 7 and ti + 1 < ntile:
                    tasks.append(t_pro2(ti + 1))
        run_pipeline(tasks, 5)
        k.barrier()

    def run_threads(gens):
        live = list(gens)
        while live:
            for g_ in list(live):
                try:
                    next(g_)
                except StopIteration:
                    live.remove(g_)

    def v3(ap, a):
        return ap.rearrange("p (a b) -> p a b", a=a)

    def phaseB(l):
        k.arena_reset()
        NCK = cfg.NCK
        qTv = qT.rearrange("(c p) n -> p c n", p=128)
        kTv = kT.rearrange("(c p) n -> p c n", p=128)
        vTv = vT.rearrange("(c p) n -> p c n", p=128)
        B = {}
        CDT = F32 if CHAIN_FP32 else BF16
        identc = ident_f if CHAIN_FP32 else ident_b
        for d in range(2):
            rGa_ = k.ar(f"rGa{d}", [128, 8, 128], F32)
            rGi_ = k.ar(f"rGi{d}", [128, 8, 128], F32)
            for sl in range(2):
                B[d, sl] = dict(
                    kq=k.ar(f"kq{d}{sl}", [128, 8, 2, 128], BF16),
                    vt=k.ar(f"vt{d}{sl}", [128, 8, 128], BF16),
                    gbt=k.ar(f"gbt{d}{sl}", [32, 128], F32),
                    gb=k.ar(f"gb{d}{sl}", [128, 32], F32),
                    E=k.ar(f"E{d}{sl}", [128, 24], F32),
                    bege=k.ar(f"bege{d}{sl}", [128, 8], F32),
                    nbeta=k.ar(f"nbeta{d}{sl}", [128, 8], F32),
                    ost=k.ar(f"ost{d}{sl}", [128, 8, 128], BF16),
                    rGa=rGa_, rGi=rGi_,
                )
                for hh in range(2):
                    B[d, sl, hh] = dict(
                        qkm=k.ar(f"qkm{d}{sl}{hh}", [128, 4, 128], BF16),
                        kdec=k.ar(f"kdec{d}{sl}{hh}", [128, 4, 128], BF16),
                        u=k.ar(f"u{d}{sl}{hh}", [128, 4, 128], F32),
                        wT=k.ar(f"wT{d}{sl}{hh}", [128, 4, 128], BF16),
                        qdT=k.ar(f"qdT{d}{sl}{hh}", [128, 4, 128], BF16),
                    )
            for hh in range(2):
                B["t", d, hh] = dict(
                    Dx=k.ar(f"Dx{d}{hh}", [128, 4, 128], F32),
                    DTx=k.ar(f"DTx{d}{hh}", [128, 4, 128], F32),
                    egcb=k.ar(f"egcb{d}{hh}", [128, 4, 128], F32),
                    U=[k.ar(f"U{d}{hh}{i}", [128, 4, 128], CDT) for i in range(1 if CHAIN_FP32 else 2)],
                    W=[k.ar(f"W{d}{hh}{i}", [128, 4, 128], CDT) for i in range(1 if CHAIN_FP32 else 2)],
                    P=[k.ar(f"P{d}{hh}{i}", [128, 4, 128], CDT) for i in range(1 if CHAIN_FP32 else 2)],
                    Pb=k.ar(f"Pb{d}{hh}", [128, 4, 128], BF16),
                    ktok=k.ar(f"ktok{d}{hh}", [128, 4, 128], BF16),
                    bkg=k.ar(f"bkg{d}{hh}", [128, 4, 128], BF16),
                    bv=k.ar(f"bv{d}{hh}", [128, 4, 128], BF16),
                    vnew=k.ar(f"vnew{d}{hh}", [128, 4, 128], BF16),
                    Ssc=k.ar(f"Ssc{d}{hh}", [128, 4, 128], F32),
                    S=k.ar(f"S{d}{hh}", [128, 4, 128], F32),
                    Sb=k.ar(f"Sb{d}{hh}", [128, 4, 128], BF16),
                )
                if CHAIN_FP32:
                    tt_ = B["t", d, hh]
                    tt_["U"].append(tt_["Dx"])
                    tt_["W"].append(tt_["DTx"])
                    tt_["P"].append(tt_["egcb"])
                k.memset("pool", B["t", d, hh]["S"][:], 0.0, w=(B["t", d, hh]["S"],))
                k.memset("pool", B["t", d, hh]["Sb"][:], 0.0, w=(B["t", d, hh]["Sb"],))
        print("arena phase B bytes", k.arena_off)

        def setup(d, c, sl):
            b = B[d, sl]
            Mincl, Maft = masks[:, 2 * d, :], masks[:, 2 * d + 1, :]
            cs = slice(c * 128, (c + 1) * 128)
            k.dma("sp", b["kq"][:, :, 0, :], kTv[:, :, cs], w=(b["kq"],))
            k.dma("sp", b["kq"][:, :, 1, :], qTv[:, :, cs], w=(b["kq"],))
            k.dma("sp", b["vt"][:], vTv[:, :, cs], w=(b["vt"],))
            k.dma("sp", b["gbt"][:], gbT[:, cs], w=(b["gbt"],))
            yield
            pt = nextps()
            k.mm_multi(pt, [(pt[:, 0:32], b["gbt"][:], ident_f[0:32, 0:32], True)], r=(b["gbt"], ident_f))
            k.copy("dve", b["gb"][:], pt[:, 0:32], r=(pt,), w=(b["gb"],))
            yield
            beta = b["gb"][:, d * 8:d * 8 + 8]
            g = b["gb"][:, 16 + d * 8:16 + d * 8 + 8]
            pt = nextps()
            k.mm_multi(pt, [(pt[:, 0:8], Mincl, g, False), (pt[:, 8:16], Maft, g, False),
                            (pt[:, 16:24], ones_f[:], g, False)], r=(masks, ones_f, b["gb"]))
            k.act(b["E"][:], pt[:, 0:24], AF.Exp, r=(pt,), w=(b["E"],))
            k.tt("pool", b["rGa"][:], bc(masks[:, 2 * d + 1:2 * d + 2, :], [128, 8, 128]),
                 bc(g.unsqueeze(2), [128, 8, 128]), ALU.mult, r=(masks, b["gb"]), w=(b["rGa"],))
            k.tt("pool", b["rGi"][:], bc(masks[:, 2 * d:2 * d + 1, :], [128, 8, 128]),
                 bc(g.unsqueeze(2), [128, 8, 128]), ALU.mult, r=(masks, b["gb"]), w=(b["rGi"],))
            yield
            k.tt("dve", b["bege"][:], beta, b["E"][:, 0:8], ALU.mult, r=(b["gb"], b["E"]), w=(b["bege"],))
            k.ts("dve", b["nbeta"][:], beta, -1.0, None, ALU.mult, r=(b["gb"],), w=(b["nbeta"],))
            yield

        def prep(d, c, sl, hh):
            b = B[d, sl]
            bh = B[d, sl, hh]
            t = B["t", d, hh]
            hs = slice(4 * hh, 4 * hh + 4)
            Mincl, Maft = masks[:, 2 * d, :], masks[:, 2 * d + 1, :]
            Mincl_bc = bc(masks[:, 2 * d:2 * d + 1, :], [128, 4, 128])
            Maft_bc = bc(masks[:, 2 * d + 1:2 * d + 2, :], [128, 4, 128])
            beta = b["gb"][:, d * 8 + 4 * hh:d * 8 + 4 * hh + 4]
            cr = (lambda a: a.bitcast(mybir.dt.float32r)) if (CHAIN_FP32 and CHAIN_R) else (lambda a: a)
            pD = nextps()
            k.mm(pD, pD[:], [(Mincl, b["rGa"][:, hs, :])], r=(masks, b["rGa"]))
            k.act(v3(t["Dx"][:].rearrange("p a b -> p (a b)"), 4), v3(pD[:], 4), AF.Exp, r=(pD,), w=(t["Dx"],))
            pDT = nextps()
            k.mm(pDT, pDT[:], [(Maft, b["rGi"][:, hs, :])], r=(masks, b["rGi"]))
            k.act(t["DTx"][:], v3(pDT[:], 4), AF.Exp, r=(pDT,), w=(t["DTx"],))
            yield
            pG = nextps()
            k.mm(pG, pG[:], [(ones_f[:], b["rGi"][:, hs, :])], r=(ones_f, b["rGi"]))
            k.act(t["egcb"][:], v3(pG[:], 4), AF.Exp, r=(pG,), w=(t["egcb"],))
            k.tt("pool", t["Dx"][:], t["Dx"][:], Maft_bc, ALU.mult, r=(t["Dx"], masks), w=(t["Dx"],))
            k.tt("pool", t["Dx"][:], t["Dx"][:], bc(b["nbeta"][:, hs].unsqueeze(2), [128, 4, 128]), ALU.mult,
                 r=(t["Dx"], b["nbeta"]), w=(t["Dx"],))
            k.tt("pool", t["DTx"][:], t["DTx"][:], Mincl_bc, ALU.mult, r=(t["DTx"], masks), w=(t["DTx"],))
            yield
            W0 = t["W"][0]
            for pair in range(2):
                pk = nextps()
                groups = []
                for hl in range(2):
                    h = 4 * hh + 2 * pair + hl
                    groups.append((pk[:, hl * 256:(hl + 1) * 256], b["kq"][:, h, 0, :],
                                   b["kq"][:, h, :, :].rearrange("p a b -> p (a b)"), False))
                k.mm_multi(pk, groups, r=(b["kq"],))
                pkv = v3(pk[:], 2)
                k.tt("dve", cr(W0[:, 2 * pair:2 * pair + 2, :]), pkv[:, :, 0:128], t["Dx"][:, 2 * pair:2 * pair + 2, :],
                     ALU.mult, r=(pk, t["Dx"]), w=(W0,))
                k.tt("dve", bh["qkm"][:, 2 * pair:2 * pair + 2, :], pkv[:, :, 128:256],
                     t["DTx"][:, 2 * pair:2 * pair + 2, :], ALU.mult, r=(pk, t["DTx"]), w=(bh["qkm"],))
            yield
            U0 = t["U"][0]
            pt = nextps()
            ptb = v3(pt[:], 4) if CHAIN_FP32 else v3(pt[:].bitcast(BF16)[:, 0:512], 4)
            k.mm_multi(pt, [(ptb[:, hl, :], W0[:, hl, :], identc[:], True) for hl in range(4)], r=(W0, identc))
            k.copy("dve" if CHAIN_R else "act", cr(U0[:]), ptb, r=(pt,), w=(U0,))
            pt = nextps()
            ptb = v3(pt[:].bitcast(BF16)[:, 0:512], 4)
            k.mm_multi(pt, [(ptb[:, hl, :], b["kq"][:, 4 * hh + hl, 0, :], ident_b[:], True) for hl in range(4)],
                       r=(b["kq"], ident_b))
            k.copy("act", t["ktok"][:], ptb, r=(pt,), w=(t["ktok"],))
            pt = nextps()
            ptb = v3(pt[:].bitcast(BF16)[:, 0:512], 4)
            k.mm_multi(pt, [(ptb[:, hl, :], b["vt"][:, 4 * hh + hl, :], ident_b[:], True) for hl in range(4)],
                       r=(b["vt"], ident_b))
            k.tt("dve", t["bv"][:], ptb, bc(beta.unsqueeze(2), [128, 4, 128]), ALU.mult, r=(pt, b["gb"]),
                 w=(t["bv"],))
            yield
            k.tt("pool", t["bkg"][:], t["ktok"][:], bc(b["bege"][:, hs].unsqueeze(2), [128, 4, 128]), ALU.mult,
                 r=(t["ktok"], b["bege"]), w=(t["bkg"],))
            k.tt("pool", bh["kdec"][:], t["ktok"][:], bc(b["E"][:, 8 + 4 * hh:12 + 4 * hh].unsqueeze(2), [128, 4, 128]),
                 ALU.mult, r=(t["ktok"], b["E"]), w=(bh["kdec"],))
            k.tt("pool", bh["qdT"][:], b["kq"][:, hs, 1, :], t["egcb"][:], ALU.mult, r=(b["kq"], t["egcb"]),
                 w=(bh["qdT"],))
            k.tt("dve", cr(t["P"][0][:]), U0[:], bc(identc[:].unsqueeze(1), [128, 4, 128]), ALU.add,
                 r=(U0, identc), w=(t["P"][0],))
            yield
            for lev in range(6):
                Uc, Wc = t["U"][lev % 2], t["W"][lev % 2]
                Un, Wn = t["U"][(lev + 1) % 2], t["W"][(lev + 1) % 2]
                Pc, Pn = t["P"][lev % 2], t["P"][(lev + 1) % 2]
                pB = nextps()
                k.mm_multi(pB, [(pB[:, hl * 128:(hl + 1) * 128], cr(Uc[:, hl, :]), cr(Wc[:, hl, :]), False) for hl in range(4)],
                           r=(Wc, Uc))
                k.copy("dve", cr(Wn[:]), v3(pB[:], 4), r=(pB,), w=(Wn,))
                yield
                if lev < 5:
                    pA = nextps()
                    if CHAIN_FP32:
                        pav = v3(pA[:], 4)
                    else:
                        pav = v3(pA[:].bitcast(BF16)[:, 0:512], 4)
                    k.mm_multi(pA, [(pav[:, hl, :], Wn[:, hl, :], identc[:], True) for hl in range(4)],
                               r=(Wn, identc))
                    k.copy("act", cr(Un[:]), pav, r=(pA,), w=(Un,))
                pC = nextps()
                k.mm_multi(pC, [(pC[:, hl * 128:(hl + 1) * 128], cr(Wn[:, hl, :]), cr(Pc[:, hl, :]), False) for hl in range(4)],
                           r=(Wn, Pc))
                k.tt("dve", cr(Pn[:]), v3(pC[:], 4), Pc[:], ALU.add, r=(pC, Pc), w=(Pn,))
                yield
            Pf = t["P"][0]
            if CHAIN_FP32:
                k.copy("pool", t["Pb"][:], Pf[:], r=(Pf,), w=(t["Pb"],))
                Pf = t["Pb"]
            pu = nextps()
            k.mm_multi(pu, [(pu[:, hl * 128:(hl + 1) * 128], Pf[:, hl, :], t["bv"][:, hl, :], False) for hl in range(4)],
                       r=(Pf, t["bv"]))
            k.copy("act", bh["u"][:], v3(pu[:], 4), r=(pu,), w=(bh["u"],))
            pw = nextps()
            k.mm_multi(pw, [(pw[:, hl * 128:(hl + 1) * 128], t["bkg"][:, hl, :], Pf[:, hl, :], False) for hl in range(4)],
                       r=(Pf, t["bkg"]))
            k.copy("dve", bh["wT"][:], v3(pw[:], 4), r=(pw,), w=(bh["wT"],))
            yield

        def scan(d, c, sl, hh):
            b = B[d, sl]
            bh = B[d, sl, hh]
            t = B["t", d, hh]
            S, Sb = t["S"], t["Sb"]
            pws = nextps()
            k.mm_multi(pws, [(pws[:, hl * 128:(hl + 1) * 128], bh["wT"][:, hl, :], Sb[:, hl, :], False)
                             for hl in range(4)], r=(bh["wT"], Sb))
            k.tt("dve", t["vnew"][:], bh["u"][:], v3(pws[:], 4), ALU.subtract, r=(bh["u"], pws), w=(t["vnew"],))
            k.tt("pool", t["Ssc"][:], S[:], bc(b["E"][:, 16 + 4 * hh:20 + 4 * hh].unsqueeze(2), [128, 4, 128]),
                 ALU.mult, r=(S, b["E"]), w=(t["Ssc"],))
            yield
            po = nextps()

            def fn(g, po=po, Sb=Sb, bh=bh, t=t):
                ins = None
                for hl in range(4):
                    o_ = po[:, hl * 128:(hl + 1) * 128]
                    g.matmul(o_, lhsT=Sb[:, hl, :], rhs=bh["qdT"][:, hl, :], start=True, stop=False)
                    ins = g.matmul(o_, lhsT=t["vnew"][:, hl, :], rhs=bh["qkm"][:, hl, :], start=False, stop=True)
                return ins
            k.op("pe", fn, r=(Sb, bh["qdT"], t["vnew"], bh["qkm"]), w=(po,))
            k.copy("act", b["ost"][:, 4 * hh:4 * hh + 4, :], v3(po[:], 4), r=(po,), w=(b["ost"],))
            pds = nextps()
            k.mm_multi(pds, [(pds[:, hl * 128:(hl + 1) * 128], bh["kdec"][:, hl, :], t["vnew"][:, hl, :], False)
                             for hl in range(4)], r=(bh["kdec"], t["vnew"]))
            k.tt("dve", S[:], t["Ssc"][:], v3(pds[:], 4), ALU.add, r=(t["Ssc"], pds), w=(S,))
            k.copy("act", Sb[:], S[:], r=(S,), w=(Sb,))
            yield

        def store(d, c, sl):
            b = B[d, sl]
            k.dma(STQ, oT[d].rearrange("(h p) n -> p h n", p=128)[:, :, c * 128:(c + 1) * 128], b["ost"][:],
                  r=(b["ost"],))

        def chunk_of(d, s):
            return s if d == 0 else NCK - 1 - s

        run_threads([setup(d, chunk_of(d, 0), 0) for d in range(2)])
        run_threads([prep(d, chunk_of(d, 0), 0, hh) for d in range(2) for hh in range(2)])
        for s in range(NCK):
            sl = s % 2
            th = []
            if s + 1 < NCK:
                run_threads([setup(d, chunk_of(d, s + 1), 1 - sl) for d in range(2)])
                th += [prep(d, chunk_of(d, s + 1), 1 - sl, hh) for d in range(2) for hh in range(2)]
            th += [scan(d, chunk_of(d, s), sl, hh) for d in range(2) for hh in range(2)]
            run_threads(th)
            for d in range(2):
                store(d, chunk_of(d, s), sl)
        k.barrier()


    def phaseC(l, last):
        k.arena_reset()
        x = k.ar("Cx", [128, 8, TW], F32)
        osum = k.ar("Cosum", [128, 8, TW], F32)
        b_of = k.ar("Cof", [128, 8, TW], BF16)
        b_ob = k.ar("Cob", [128, 8, TW], BF16)
        b_zs = k.ar("Czs", [128, 8, TW], BF16)
        b_ysc = k.ar("Cysc", [128, 8, TW], BF16)
        b_ga = k.ar("Cga", [128, 8, TW], BF16)
        b_gb = k.ar("Cgb", [128, 8, TW], BF16)
        a_t = k.ar("Ca", [128, 22, TW], BF16)
        ssq8 = a_t.ap.rearrange("p a b -> p (a b)")[:, 0:16 * TW].bitcast(F32).rearrange("p (a b) -> p a b", a=8)
        wb = [k.ar(f"Cw{i}", [128, 8, 1024], BF16) for i in range(4)]
        wi = [0]
        tA = [k.ar(f"CtA{i}", [128, TW], F32) for i in range(6)]
        tAi = [0]
        tB = [k.ar(f"CtB{i}", [128, TW], BF16) for i in range(2)]
        tBi = [0]
        rs_t = k.ar("Crstd", [128, TW], F32)
        otile = [k.ar(f"Cot{i}", [128, 1024], F32) for i in range(2)]
        oti = [0]
        print("arena phase C bytes", k.arena_off)
        odn, sq2, mg, h2 = b_of, b_ob, b_zs, b_of
        fm = lambda arr: arr.rearrange("(c p) n -> p c n", p=128)

        def wload(src2, rows0, cols0, ncols, nk=8, dst=None, dcol=0):
            wt = dst if dst is not None else nxt(wb, wi)
            k.dma("sp", wt[:, 0:nk, dcol:dcol + ncols],
                  src2[rows0:rows0 + nk * 128, cols0:cols0 + ncols].rearrange("(c p) n -> p c n", p=128), w=(wt,))
            return wt

        for (t0, W) in cfg.tiles:
            pcol = t0 + PADF
            k.dma("sp", x[:, :, 0:W], xTv[:, :, 2 + t0:2 + t0 + W], w=(x,))
            for buf, arr in ((b_of, oT[0]), (b_ob, oT[1]), (b_zs, zsT), (b_ysc, yscT), (b_ga, gaT), (b_gb, gbgT)):
                k.dma("sp", buf[:, :, 0:W], fm(arr)[:, :, pcol:pcol + W], w=(buf,))
            k.tt("dve", osum[:, :, 0:W], b_of[:, :, 0:W], b_ob[:, :, 0:W], ALU.add, r=(b_of, b_ob), w=(osum,))
            k.act(sq2[:, :, 0:W], osum[:, :, 0:W], AF.Square, r=(osum,), w=(sq2,))
            for c in range(8):
                p2 = nextps()
                k.mm(p2, p2[:, 0:W], [(ones_b[:], sq2[:, c, 0:W])], r=(sq2, ones_b))
                k.copy("act", ssq8[:, c, 0:W], p2[:, 0:W], r=(p2,), w=(a_t,))
            k.act(ssq8[:, :, 0:W], ssq8[:, :, 0:W], AF.Sqrt, r=(a_t, eps_t), w=(a_t,), bias=eps_t[:], scale=1.0 / 128)
            k.recip(ssq8[:, :, 0:W], ssq8[:, :, 0:W], r=(a_t,), w=(a_t,))
            k.stt("dve", osum[:, :, 0:W], osum[:, :, 0:W], dnw[l][:, 0:1], ssq8[:, :, 0:W], ALU.mult, ALU.mult,
                  r=(osum, dnw[l], a_t), w=(osum,))
            k.tt("dve", odn[:, :, 0:W], osum[:, :, 0:W], b_zs[:, :, 0:W], ALU.mult, r=(osum, b_zs), w=(odn,))
            wdn = wload(wb_bdn[l], 0, 0, 1024)
            wsc = wload(wb_bsc[l], 0, 0, 1024)
            for m in range(8):
                pa = nextps()
                k.mm(pa, pa[:, 0:W], [(wdn[:, c, m * 128:(m + 1) * 128], odn[:, c, 0:W]) for c in range(8)],
                     r=(wdn, odn))
                pb = nextps()
                k.mm(pb, pb[:, 0:W], [(wsc[:, c, m * 128:(m + 1) * 128], b_ysc[:, c, 0:W]) for c in range(8)],
                     r=(wsc, b_ysc))
                t1 = nxt(tA, tAi)
                k.tt("dve", t1[:, 0:W], pa[:, 0:W], b_ga[:, m, 0:W], ALU.mult, r=(pa, b_ga), w=(t1,))
                t2 = nxt(tA, tAi)
                k.tt("dve", t2[:, 0:W], pb[:, 0:W], b_gb[:, m, 0:W], ALU.mult, r=(pb, b_gb), w=(t2,))
                k.tt("dve", mg[:, m, 0:W], t1[:, 0:W], t2[:, 0:W], ALU.add, r=(t1, t2), w=(mg,))
            wo = wload(wb_out[l], 0, 0, 1024)
            for m in range(8):
                pm = nextps()
                k.mm(pm, pm[:, 0:W], [(wo[:, c, m * 128:(m + 1) * 128], mg[:, c, 0:W]) for c in range(8)],
                     r=(wo, mg))
                k.tt("dve", x[:, m, 0:W], x[:, m, 0:W], pm[:, 0:W], ALU.add, r=(x, pm), w=(x,))
            k.act(sq2[:, :, 0:W], x[:, :, 0:W], AF.Square, r=(x,), w=(sq2,))
            pt = nextps()
            k.mm(pt, pt[:, 0:W], [(ones_b[:], sq2[:, c, 0:W]) for c in range(8)], r=(sq2, ones_b))
            k.act(rs_t[:, 0:W], pt[:, 0:W], AF.Sqrt, r=(pt, eps_t), w=(rs_t,), bias=eps_t[:], scale=1.0 / D)
            k.recip(rs_t[:, 0:W], rs_t[:, 0:W], r=(rs_t,), w=(rs_t,))
            for c in range(8):
                k.stt("dve", h2[:, c, 0:W], x[:, c, 0:W], n2w[l][:, c:c + 1], rs_t[:, 0:W], ALU.mult, ALU.mult,
                      r=(x, rs_t, n2w[l]), w=(h2,))
            for j0 in range(0, NFF, 4):
                nj = min(4, NFF - j0)
                wt = nxt(wb, wi)
                wload(wb_gu[l], 0, j0 * 128, nj * 128, dst=wt, dcol=0)
                wload(wb_gu[l], 0, DFF + j0 * 128, nj * 128, dst=wt, dcol=512)
                for jj in range(nj):
                    j = j0 + jj
                    pg = nextps()
                    k.mm(pg, pg[:, 0:W], [(wt[:, c, jj * 128:(jj + 1) * 128], h2[:, c, 0:W]) for c in range(8)],
                         r=(wt, h2))
                    pu = nextps()
                    k.mm(pu, pu[:, 0:W], [(wt[:, c, 512 + jj * 128:512 + (jj + 1) * 128], h2[:, c, 0:W])
                                          for c in range(8)], r=(wt, h2))
                    sg = nxt(tA, tAi)
                    k.act(sg[:, 0:W], pg[:, 0:W], AF.Silu, r=(pg,), w=(sg,))
                    k.tt("dve", a_t[:, j, 0:W], sg[:, 0:W], pu[:, 0:W], ALU.mult, r=(sg, pu), w=(a_t,))
            wd = [wload(wb_down[l], kb * 1024, 0, 1024, nk=min(8, NFF - kb * 8)) for kb in range(3)]
            for m in range(8):
                pd = nextps()
                k.mm(pd, pd[:, 0:W], [(wd[j // 8][:, j % 8, m * 128:(m + 1) * 128], a_t[:, j, 0:W])
                                      for j in range(NFF)], r=(wd[0], wd[1], wd[2], a_t))
                k.tt("dve", x[:, m, 0:W], x[:, m, 0:W], pd[:, 0:W], ALU.add, r=(x, pd), w=(x,))
            if not last:
                k.dma(STQ, xTv[:, :, 2 + t0:2 + t0 + W], x[:, :, 0:W], r=(x,))
            else:
                k.act(sq2[:, :, 0:W], x[:, :, 0:W], AF.Square, r=(x,), w=(sq2,))
                pt = nextps()
                k.mm(pt, pt[:, 0:W], [(ones_b[:], sq2[:, c, 0:W]) for c in range(8)], r=(sq2, ones_b))
                k.act(rs_t[:, 0:W], pt[:, 0:W], AF.Sqrt, r=(pt, eps_t), w=(rs_t,), bias=eps_t[:], scale=1.0 / D)
                k.recip(rs_t[:, 0:W], rs_t[:, 0:W], r=(rs_t,), w=(rs_t,))
                xn = osum
                for c in range(8):
                    k.stt("dve", xn[:, c, 0:W], x[:, c, 0:W], fw[:, c:c + 1], rs_t[:, 0:W], ALU.mult, ALU.mult,
                          r=(x, rs_t, fw), w=(xn,))
                lo = max(t0, NMETA)
                while lo < t0 + W:
                    nn = min(128, t0 + W - lo)
                    ot = nxt(otile, oti)
                    for half in range(2):
                        pt = nextps()
                        k.mm_multi(pt, [(pt[0:nn, cc * 128:(cc + 1) * 128],
                                         xn[:, half * 4 + cc, lo - t0:lo - t0 + nn], ident_f[:], True)
                                        for cc in range(4)], r=(xn, ident_f))
                        k.copy("act" if half else "dve", ot[0:nn, half * 512:(half + 1) * 512], pt[0:nn, :],
                               r=(pt,), w=(ot,))
                    k.dma(STQ, out[lo - NMETA:lo - NMETA + nn, :], ot[0:nn, :], r=(ot,))
                    lo += nn
        k.barrier()

    phases = cfg.__dict__.get("phases", "ABC")
    nl = cfg.__dict__.get("nlayers", DEPTH)
    for l in range(nl):
        phaseA(l)
        if "B" in phases:
            phaseB(l)
        if "C" in phases:
            phaseC(l, l == nl - 1)

    k.barrier()
    with nc.Block() as block:
        @block.tensor
        def _(g):
            k.replay("pe", g)

        @block.scalar
        def _(g):
            k.replay("act", g)

        @block.vector
        def _(g):
            k.replay("dve", g)

        @block.gpsimd
        def _(g):
            k.replay("pool", g)

        @block.sync
        def _(g):
            k.replay("sp", g)
    es.close()
    return nc


def make_masks():
    m = np.zeros((8, 128, 128), np.float32)
    t = np.arange(128)[:, None]
    i = np.arange(128)[None, :]
    m[0] = (t <= i)
    m[1] = (t > i)
    m[2] = (t >= i)
    m[3] = (t < i)
    return m


def core_inputs(cfg, inputs, b):
    f = np.float32
    xin = np.concatenate([inputs["meta_tokens"].astype(f), inputs["x"][b].astype(f)], axis=0)
    d = dict(
        xin=np.ascontiguousarray(xin),
        norm1_w=inputs["norm1_w"], w_in=inputs["w_in"], dn_conv_w=inputs["dn_conv_w"],
        A_log=inputs["A_log"].reshape(cfg.depth, 16), dt_bias=inputs["dt_bias"].reshape(cfg.depth, 16),
        dn_norm_w=inputs["dn_norm_w"], sc_conv_w=inputs["sc_conv_w"],
        w_branch_dn=inputs["w_branch_dn"], w_branch_sc=inputs["w_branch_sc"], w_out=inputs["w_out"],
        norm2_w=inputs["norm2_w"], w_gate_up=inputs["w_gate_up"], w_down=inputs["w_down"],
        final_norm_w=inputs["final_norm_w"],
        c_ident_f=np.eye(128, dtype=f), c_masks=make_masks())
    return {k_: np.ascontiguousarray(np.asarray(v, dtype=f)) for k_, v in d.items()}


_NC_CACHE = {}


def kernel(**inputs):
    x = inputs["x"]
    bsz, seq, _ = x.shape
    cfg = Cfg(seq, 2)
    key = (seq,)
    if key not in _NC_CACHE:
        _NC_CACHE[key] = build(cfg)
    nc = _NC_CACHE[key]
    in_maps = [core_inputs(cfg, inputs, b) for b in range(bsz)]
    res = run_bass_kernel_spmd(nc, in_maps, core_ids=list(range(bsz)))
    return np.stack([np.asarray(r["out"], dtype=np.float32) for r in res.results], axis=0)
```

```python
import numpy as np
import ml_dtypes
from contextlib import ExitStack
import concourse.bass as bass
import concourse.mybir as mybir
from concourse.bass_utils import run_bass_kernel_spmd

F32 = mybir.dt.float32
BF16 = mybir.dt.bfloat16
AF = mybir.ActivationFunctionType
ALU = mybir.AluOpType

D = 1024
NCH = 8
H = 8
NMETA = 16
DFF = 2816
NFF = 22
WIN = 9248
RMS_EPS = 1e-6
L2_EPS = 1e-6
TW = 508
SEM_ROLL = 30000
CHAIN_FP32 = True
CHAIN_R = False
STQ = "sp"

C_Q, C_K, C_V, C_Z, C_B, C_A, C_SB, C_SC, C_SX, C_GA, C_GB = (
    0, 1024, 2048, 3072, 4096, 4112, 4128, 5152, 6176, 7200, 8224)


class Tl:
    def __init__(self, name, ap):
        self.name = name
        self.ap = ap
        self.w = None
        self.r = {}

    def __getitem__(self, idx):
        return self.ap[idx]


class Eng:
    def __init__(self, name, sems):
        self.name = name
        self.sems = sems
        self.si = 0
        self.cnt = 0
        self.ops = []
        self.waited = {}


class KB:
    def __init__(self, nc, es):
        self.nc = nc
        self.es = es
        self.semh = []
        self.eng = {}
        for n in ("pe", "act", "dve", "pool", "sp"):
            ids = [self._newsem(f"s_{n}{i}") for i in range(3)]
            self.eng[n] = Eng(n, ids)
        self.dq = {}
        for q, cnt in (("sp", 40), ("pool", 4), ("act", 36)):
            self.dq[q] = dict(ids=[self._newsem(f"d_{q}{i}") for i in range(cnt)],
                              cnt=[0] * cnt, nxt=0)
        self.all_tokens = {}
        self.ntile = 0

    def _newsem(self, name):
        h = self.es.enter_context(self.nc.semaphore(name))
        self.semh.append(h)
        return len(self.semh) - 1

    def sb(self, name, shape, dt):
        t = self.es.enter_context(self.nc.sbuf_tensor(name, list(shape), dt))
        return Tl(name, t)

    def arena_init(self, nbytes):
        self.arena = self.es.enter_context(self.nc.sbuf_tensor("arena", [128, nbytes // 2], BF16))
        self.arena_size = nbytes
        self.arena_off = 0

    def arena_reset(self):
        self.arena_off = 0

    def ar(self, name, shape, dt):
        esz = 4 if dt == F32 else 2
        n = 1
        for d_ in shape[1:]:
            n *= d_
        nb = (n * esz + 31) // 32 * 32
        off = self.arena_off
        assert off + nb <= self.arena_size, f"arena overflow at {name}: {off + nb}"
        self.arena_off += nb
        ap = self.arena[0:shape[0], off // 2:(off + n * esz) // 2]
        if dt == F32:
            ap = ap.bitcast(F32)
        if len(shape) == 3:
            ap = ap.rearrange("p (a b) -> p a b", a=shape[1])
        elif len(shape) == 4:
            ap = ap.rearrange("p (a b c) -> p a b c", a=shape[1], b=shape[2])
        return Tl(name, ap)

    def ps(self, name, shape, dt=F32):
        t = self.es.enter_context(self.nc.psum_tensor(name, list(shape), dt))
        return Tl(name, t)

    def _deps(self, e, r, w, extra=()):
        waits = {}

        def add(tok):
            if tok is None:
                return
            s, v = tok
            if waits.get(s, 0) < v:
                waits[s] = v
        for t in r:
            add(t.w)
        for t in w:
            add(t.w)
            for s, v in t.r.items():
                add((s, v))
        for tok in extra:
            add(tok)
        need = []
        for s, v in waits.items():
            if e.waited.get(s, 0) < v:
                e.waited[s] = v
                need.append((s, v))
        return need

    def _mark(self, tok, r, w):
        s, v = tok
        for t in r:
            if t.r.get(s, 0) < v:
                t.r[s] = v
        for t in w:
            t.w = tok
            t.r = {}
        if self.all_tokens.get(s, 0) < v:
            self.all_tokens[s] = v

    def op(self, engine, fn, r=(), w=(), extra=()):
        e = self.eng[engine]
        need = self._deps(e, r, w, extra)
        if e.cnt >= SEM_ROLL:
            e.si += 1
            e.cnt = 0
        e.cnt += 1
        sid = e.sems[e.si]
        tok = (sid, e.cnt)
        e.ops.append((need, fn, sid, 1))
        self._mark(tok, r, w)
        return tok

    def dma(self, queue, out, in_, r=(), w=(), extra=()):
        e = self.eng[queue]
        dq = self.dq[queue]
        i = dq["nxt"]
        dq["nxt"] = (i + 1) % len(dq["ids"])
        sid = dq["ids"][i]
        prev = (sid, dq["cnt"][i]) if dq["cnt"][i] else None
        need = self._deps(e, r, w, tuple(extra) + ((prev,) if prev else ()))
        dq["cnt"][i] += 16
        tok = (sid, dq["cnt"][i])
        e.ops.append((need, lambda g, o=out, s=in_: g.dma_start(out=o, in_=s), sid, 16))
        self._mark(tok, r, w)
        return tok

    def barrier(self):
        toks = list(self.all_tokens.items())
        for e in self.eng.values():
            need = []
            for s, v in toks:
                if e.waited.get(s, 0) < v:
                    e.waited[s] = v
                    need.append((s, v))
            if need:
                e.ops.append((need, None, None, 0))

    def replay(self, engine, g):
        for need, fn, sid, inc in self.eng[engine].ops:
            for s, v in need:
                g.wait_ge(self.semh[s], v)
            if fn is None:
                continue
            ins = fn(g)
            if inc:
                ins.then_inc(self.semh[sid], inc)

    def mm(self, out_t, out_ap, pairs, r=(), transpose=False):
        n = len(pairs)

        def fn(g, out_ap=out_ap, pairs=pairs, n=n):
            ins = None
            for i, (a, b) in enumerate(pairs):
                ins = g.matmul(out_ap, lhsT=a, rhs=b, start=(i == 0), stop=(i == n - 1))
            return ins
        return self.op("pe", fn, r=r, w=(out_t,))

    def mm_multi(self, out_t, groups, r=()):
        def fn(g, groups=groups):
            ins = None
            for (o, a, b, tr) in groups:
                if tr:
                    ins = g.transpose(o, a, b)
                else:
                    ins = g.matmul(o, lhsT=a, rhs=b, start=True, stop=True)
            return ins
        return self.op("pe", fn, r=r, w=(out_t,))

    def act(self, out, in_, func, r=(), w=(), bias=None, scale=None, eng="act"):
        kw = {}
        if bias is not None:
            kw["bias"] = bias
        if scale is not None:
            kw["scale"] = scale
        return self.op(eng, lambda g, o=out, i=in_, f=func, kw=kw: g.activation(out=o, in_=i, func=f, **kw),
                       r=r, w=w)

    def tt(self, eng, out, in0, in1, op, r=(), w=()):
        return self.op(eng, lambda g, o=out, a=in0, b=in1, p=op: g.tensor_tensor(out=o, in0=a, in1=b, op=p),
                       r=r, w=w)

    def stt(self, eng, out, in0, scalar, in1, op0, op1, r=(), w=()):
        return self.op(eng, lambda g, o=out, a=in0, s=scalar, b=in1, p0=op0, p1=op1:
                       g.scalar_tensor_tensor(out=o, in0=a, scalar=s, in1=b, op0=p0, op1=p1), r=r, w=w)

    def ts(self, eng, out, in0, s1, s2, op0, op1=None, r=(), w=()):
        if op1 is None:
            return self.op(eng, lambda g, o=out, a=in0, s=s1, p0=op0:
                           g.tensor_scalar(out=o, in0=a, scalar1=s, scalar2=None, op0=p0), r=r, w=w)
        return self.op(eng, lambda g, o=out, a=in0, x=s1, y=s2, p0=op0, p1=op1:
                       g.tensor_scalar(out=o, in0=a, scalar1=x, scalar2=y, op0=p0, op1=p1), r=r, w=w)

    def copy(self, eng, out, in_, r=(), w=()):
        if eng == "act":
            return self.op(eng, lambda g, o=out, i=in_: g.copy(out=o, in_=i), r=r, w=w)
        return self.op(eng, lambda g, o=out, i=in_: g.tensor_copy(out=o, in_=i), r=r, w=w)

    def rsqrt(self, out, in_, scale, bias_ap, r=(), w=()):
        self.act(out, in_, AF.Ln, r=r, w=w, bias=bias_ap, scale=scale)
        return self.act(out, out, AF.Exp, r=w, w=w, scale=-0.5)

    def recip(self, out, in_, r=(), w=()):
        return self.op("dve", lambda g, o=out, i=in_: g.reciprocal(out=o, in_=i), r=r, w=w)

    def memset(self, eng, ap, val, w=()):
        return self.op(eng, lambda g, a=ap, v=val: g.memset(a, v), w=w)


def bc(ap, shape):
    return ap.to_broadcast(list(shape))


class Cfg:
    def __init__(self, seq, depth):
        self.seq = seq
        self.depth = depth
        self.L = seq + NMETA
        self.PADF = (-self.L) % 128
        self.T = self.L + self.PADF
        self.NCK = self.T // 128
        self.XOFF = 2
        self.XW = self.L + 4
        self.tiles = []
        t0 = 0
        while t0 < self.L:
            w = min(TW, self.L - t0)
            self.tiles.append((t0, w))
            t0 += w


def build(cfg, debug=False):
    nc = bass.Bass("TRN2", target_bir_lowering=False)
    L, T, DEPTH = cfg.L, cfg.T, cfg.depth
    es = ExitStack()
    k = KB(nc, es)

    def din(name, shape, dt=F32):
        return nc.dram_tensor(name, list(shape), dt, kind="ExternalInput").ap()

    def dscr(name, shape, dt):
        kind = "ExternalOutput" if debug else "Internal"
        return nc.dram_tensor(name, list(shape), dt, kind=kind).ap()

    xin = din("xin", [L, D])
    norm1_w = din("norm1_w", [DEPTH, D])
    w_in = din("w_in", [DEPTH, D, WIN])
    dn_conv_w = din("dn_conv_w", [DEPTH, 5, 3072])
    A_log = din("A_log", [DEPTH, 16])
    dt_bias = din("dt_bias", [DEPTH, 16])
    dn_norm_w = din("dn_norm_w", [DEPTH, 128])
    sc_conv_w = din("sc_conv_w", [DEPTH, 3, 1024])
    w_bdn = din("w_branch_dn", [DEPTH, D, D])
    w_bsc = din("w_branch_sc", [DEPTH, D, D])
    w_out = din("w_out", [DEPTH, D, D])
    norm2_w = din("norm2_w", [DEPTH, D])
    w_gu = din("w_gate_up", [DEPTH, D, 2 * DFF])
    w_down = din("w_down", [DEPTH, DFF, D])
    final_w = din("final_norm_w", [D])
    c_ident_f = din("c_ident_f", [128, 128])
    c_masks = din("c_masks", [8, 128, 128])
    out = nc.dram_tensor("out", [cfg.seq, D], F32, kind="ExternalOutput").ap()

    wb_in = dscr("wb_in", [DEPTH, D, WIN], BF16)
    wb_bdn = dscr("wb_bdn", [DEPTH, D, D], BF16)
    wb_bsc = dscr("wb_bsc", [DEPTH, D, D], BF16)
    wb_out = dscr("wb_out", [DEPTH, D, D], BF16)
    wb_gu = dscr("wb_gu", [DEPTH, D, 2 * DFF], BF16)
    wb_down = dscr("wb_down", [DEPTH, DFF, D], BF16)
    xT = dscr("xT", [D, cfg.XW], F32)
    qT = dscr("qT", [D, T], BF16)
    kT = dscr("kT", [D, T], BF16)
    vT = dscr("vT", [D, T], BF16)
    gbT = dscr("gbT", [32, T], F32)
    zsT = dscr("zsT", [D, T], BF16)
    yscT = dscr("yscT", [D, T], BF16)
    gaT = dscr("gaT", [D, T], BF16)
    gbgT = dscr("gbgT", [D, T], BF16)
    oT = [dscr(f"oT{d}", [D, T], BF16) for d in range(2)]

    ident_f = k.sb("ident_f", [128, 128], F32)
    ident_b = k.sb("ident_b", [128, 128], BF16)
    ones_b = k.sb("ones_b", [128, 128], BF16)
    ones_f = k.sb("ones_f", [128, 128], F32)
    masks = k.sb("masks", [128, 8, 128], F32)
    zero_b = k.sb("zero_b", [128, 1024], BF16)
    zero_f = k.sb("zero_f", [128, 512], F32)
    k.dma("sp", ident_f[:], c_ident_f[:, :], w=(ident_f,))
    k.dma("sp", masks[:], c_masks.rearrange("m p n -> p m n"), w=(masks,))
    k.copy("dve", ident_b[:], ident_f[:], r=(ident_f,), w=(ident_b,))
    k.memset("dve", ones_b[:], 1.0, w=(ones_b,))
    k.memset("dve", ones_f[:], 1.0, w=(ones_f,))
    k.memset("pool", zero_b[:], 0.0, w=(zero_b,))
    k.memset("pool", zero_f[:], 0.0, w=(zero_f,))

    def load_vec(name, src_ap, nchunk):
        t = k.sb(name, [128, nchunk], F32)
        k.dma("sp", t[:], src_ap.rearrange("(c p) -> p c", p=128), w=(t,))
        return t

    nc_allow = nc.allow_non_contiguous_dma(reason="tiny parameter vectors")
    es.enter_context(nc_allow)

    n1w = [load_vec(f"n1w{l}", norm1_w[l], 8) for l in range(DEPTH)]
    n2w = [load_vec(f"n2w{l}", norm2_w[l], 8) for l in range(DEPTH)]
    fw = load_vec("fw", final_w, 8)
    dcw = []
    scw = []
    dnw = []
    nAexp = []
    dtb = []
    for l in range(DEPTH):
        t = k.sb(f"dcw{l}", [128, 5, 24], F32)
        k.dma("sp", t[:], dn_conv_w[l].rearrange("d (c p) -> p d c", p=128), w=(t,))
        dcw.append(t)
        t = k.sb(f"scw{l}", [128, 3, 8], F32)
        k.dma("sp", t[:], sc_conv_w[l].rearrange("d (c p) -> p d c", p=128), w=(t,))
        scw.append(t)
        t = k.sb(f"dnw{l}", [128, 1], F32)
        k.dma("sp", t[:], dn_norm_w[l].rearrange("(p o) -> p o", o=1), w=(t,))
        dnw.append(t)
        ta = k.sb(f"alog{l}", [16, 1], F32)
        k.dma("sp", ta[:], A_log[l].rearrange("(p o) -> p o", o=1), w=(ta,))
        tb = k.sb(f"dtb{l}", [16, 1], F32)
        k.dma("sp", tb[:], dt_bias[l].rearrange("(p o) -> p o", o=1), w=(tb,))
        dtb.append(tb)
        te = k.sb(f"nAexp{l}", [16, 1], F32)
        k.act(te[:], ta[:], AF.Exp, r=(ta,), w=(te,))
        k.ts("dve", te[:], te[:], -1.0, None, ALU.mult, r=(te,), w=(te,))
        nAexp.append(te)
    eps_t = k.sb("eps_t", [128, 1], F32)
    k.memset("dve", eps_t[:], RMS_EPS, w=(eps_t,))
    eps128_t = k.sb("eps128_t", [128, 1], F32)
    k.memset("dve", eps128_t[:], 128.0 * L2_EPS, w=(eps128_t,))
    one_t = k.sb("one_t", [128, 1], F32)
    k.memset("dve", one_t[:], 1.0, w=(one_t,))

    k.arena_init(196 * 1024)
    CW = 4096
    cf = [k.ar(f"castf{i}", [128, CW], F32) for i in range(3)]
    cb = [k.ar(f"castb{i}", [128, CW], BF16) for i in range(3)]
    cidx = [0]

    def cast_w(dst, src, rows, cols):
        for l in range(DEPTH):
            for r0 in range(0, rows, 128):
                for c0 in range(0, cols, CW):
                    cw = min(CW, cols - c0)
                    i = cidx[0] % 3
                    cidx[0] += 1
                    k.dma("sp", cf[i][:, 0:cw], src[l, r0:r0 + 128, c0:c0 + cw], w=(cf[i],))
                    eng = ("act", "dve", "pool")[i]
                    k.copy(eng, cb[i][:, 0:cw], cf[i][:, 0:cw], r=(cf[i],), w=(cb[i],))
                    k.dma(STQ, dst[l, r0:r0 + 128, c0:c0 + cw], cb[i][:, 0:cw], r=(cb[i],))
    cast_w(wb_in, w_in, D, WIN)
    cast_w(wb_bdn, w_bdn, D, D)
    cast_w(wb_bsc, w_bsc, D, D)
    cast_w(wb_out, w_out, D, D)
    cast_w(wb_gu, w_gu, D, 2 * DFF)
    cast_w(wb_down, w_down, DFF, D)
    k.barrier()

    PADF = cfg.PADF
    if PADF:
        for arr in (qT, kT, vT):
            k.dma("sp", arr.rearrange("(c p) n -> p c n", p=128)[:, :, 0:PADF],
                  zero_b[:, 0:8 * PADF].rearrange("p (c n) -> p c n", c=8), r=(zero_b,))
        k.dma("sp", gbT[:, 0:PADF], zero_f[0:32, 0:PADF], r=(zero_f,))
    xTv = xT.rearrange("(c p) n -> p c n", p=128)
    k.dma("sp", xTv[:, :, 0:2], zero_f[:, 0:16].rearrange("p (c n) -> p c n", c=8), r=(zero_f,))
    k.dma("sp", xTv[:, :, L + 2:L + 4], zero_f[:, 0:16].rearrange("p (c n) -> p c n", c=8), r=(zero_f,))

    PS = [k.ps(f"ps{i}", [128, 512], F32) for i in range(8)]
    psi = [0]

    def nextps():
        t = PS[psi[0] % 8]
        psi[0] += 1
        return t

    k.arena_reset()
    p0_in = [k.ar(f"p0in{i}", [128, 4, D], F32) for i in range(2)]
    p0_out = [k.ar(f"p0out{i}", [128, 8, 512], F32) for i in range(2)]
    it = 0
    for t0 in range(0, L, 512):
        n = min(512, L - t0)
        tin = p0_in[it % 2]
        tout = p0_out[it % 2]
        nb = (n + 127) // 128
        for b in range(nb):
            nn = min(128, n - b * 128)
            k.dma("sp", tin[0:nn, b, :], xin[t0 + b * 128:t0 + b * 128 + nn, :], w=(tin,))
        for c in range(8):
            pt = nextps()
            groups = []
            for b in range(nb):
                nn = min(128, n - b * 128)
                groups.append((pt[:, b * 128:b * 128 + nn], tin[0:nn, b, c * 128:(c + 1) * 128],
                               ident_f[0:nn, 0:nn], True))
            k.mm_multi(pt, groups, r=(tin, ident_f))
            k.copy("act" if c % 2 else "dve", tout[:, c, 0:n], pt[:, 0:n], r=(pt,), w=(tout,))
        k.dma("sp", xTv[:, :, 2 + t0:2 + t0 + n], tout[:, :, 0:n], r=(tout,))
        it += 1
    k.barrier()

    NCMAX = TW + 4

    def nxt(lst, ctr):
        t = lst[ctr[0] % len(lst)]
        ctr[0] += 1
        return t

    psfree = []

    def ps_alloc():
        return psfree.pop(0)

    def ps_free(t):
        psfree.append(t)

    def run_pipeline(tasks, depth, budget=7):
        psfree[:] = list(PS)
        live = []
        pending = list(tasks)
        pi = 0
        while True:
            while pi < len(pending) and len(live) < depth:
                tk = pending[pi]
                nb = getattr(tk, "nb", 0)
                if sum(n for _, n in live) + nb > budget:
                    break
                pi += 1
                g_ = tk()
                if g_ is not None:
                    try:
                        next(g_)
                        live.append((g_, nb))
                    except StopIteration:
                        pass
            if not live:
                if pi >= len(pending):
                    break
                continue
            for ent in list(live):
                try:
                    next(ent[0])
                except StopIteration:
                    live.remove(ent)

    def phaseA(l):
        k.arena_reset()
        xts = [k.ar(f"xt{i}", [128, 8, NCMAX], F32) for i in range(1)]
        hTs = [k.ar(f"hT{i}", [128, 8, NCMAX], BF16) for i in range(2)]
        sq = k.ar("sq", [128, 8, NCMAX], BF16)
        rstd = k.ar("rstd", [128, NCMAX], F32)
        wbuf = [k.ar(f"wbuf{i}", [128, 8, 1024], BF16) for i in range(4)]
        wsm = k.ar("wsm", [128, 8, 32], BF16)
        stg = [k.ar(f"stg{i}", [128, 8, TW], BF16) for i in range(3)]
        tmpA = [k.ar(f"tmpA{i}", [128, NCMAX], F32) for i in range(8)]
        ssq8 = [k.ar(f"ssq8{i}", [128, 8, TW], F32) for i in range(2)]
        tmpB = [k.ar(f"tmpB{i}", [128, NCMAX], BF16) for i in range(6)]

        def ta():
            return tmpA.pop(0)

        def tb():
            return tmpB.pop(0)
        gbs = k.ar("gbs", [16, 2, NCMAX], F32)
        print("arena phase A bytes", k.arena_off)
        wl = wb_in[l]
        fm = lambda arr: arr.rearrange("(c p) n -> p c n", p=128)
        BLK = [C_Q, C_K, C_V, C_Z, C_SC, C_SX, C_SB, C_GA, C_GB]
        nblk = len(BLK)
        wslot = {}
        gblk = [0]

        def t_wload(gi):
            def f():
                wt = wbuf[gi % 4]
                k.dma("sp", wt[:, :, :], wl[:, BLK[gi % nblk]:BLK[gi % nblk] + 1024].rearrange("(c p) n -> p c n", p=128),
                      w=(wt,))
                wslot[gi] = wt
            return f

        def t_xload(ti):
            def f():
                t0, W = cfg.tiles[ti]
                NC = W + 4
                k.dma("sp", xts[0][:, :, 0:NC], xTv[:, :, t0:t0 + NC], w=(xts[0],))
            return f

        def t_pro1(ti):
            def f():
                t0, W = cfg.tiles[ti]
                NC = W + 4
                xt = xts[0]
                k.act(sq[:, :, 0:NC], xt[:, :, 0:NC], AF.Square, r=(xt,), w=(sq,))
            return f

        def t_pro2(ti):
            def f():
                t0, W = cfg.tiles[ti]
                NC = W + 4
                xt, hT = xts[0], hTs[ti % 2]
                pt = ps_alloc()
                k.mm(pt, pt[:, 0:NC], [(ones_b[:], sq[:, c, 0:NC]) for c in range(8)], r=(sq, ones_b))
                k.rsqrt(rstd[:, 0:NC], pt[:, 0:NC], 1.0 / D, eps_t[:], r=(pt, eps_t), w=(rstd,))
                ps_free(pt)
                for c in range(8):
                    k.stt("dve", hT[:, c, 0:NC], xt[:, c, 0:NC], n1w[l][:, c:c + 1],
                          rstd[:, 0:NC], ALU.mult, ALU.mult, r=(xt, rstd, n1w[l]), w=(hT,))
            return f

        def proj(hT, NC, wt, j, M=128):
            pt = ps_alloc()
            k.mm(pt, pt[0:M, 0:NC], [(wt[:, c, j:j + M], hT[:, c, 0:NC]) for c in range(8)], r=(wt, hT))
            return pt

        class Grp:
            def __init__(self, st, dst, pcol, W, n=8, norm=None, ssq=None):
                self.st, self.dst, self.pcol, self.W, self.left = st, dst, pcol, W, n
                self.norm, self.ssq = norm, ssq

            def done(self):
                self.left -= 1
                if self.left == 0:
                    W = self.W
                    if self.norm is not None:
                        sq_ = self.ssq
                        if self.norm == 0:
                            k.rsqrt(sq_[:, :, 0:W], sq_[:, :, 0:W], 128.0, eps128_t[:], r=(sq_, eps128_t), w=(sq_,))
                        else:
                            k.rsqrt(sq_[:, :, 0:W], sq_[:, :, 0:W], 1.0, eps_t[:], r=(sq_, eps_t), w=(sq_,))
                        k.tt("dve", self.st[:, :, 0:W], self.st[:, :, 0:W], sq_[:, :, 0:W], ALU.mult,
                             r=(self.st, sq_), w=(self.st,))
                    k.dma(STQ, fm(self.dst)[:, :, self.pcol:self.pcol + W], self.st[:, :, 0:W],
                          r=(self.st,))

        def t_qkv(ti, gi, grp, c, G):
            def gen():
                t0, W = cfg.tiles[ti]
                NC = W + 4
                hT = hTs[ti % 2]
                wt = wslot[gi]
                st = G.st
                pt = proj(hT, NC, wt, c * 128)
                yield
                cc = grp * 8 + c
                acc = ta()
                k.ts("dve", acc[:, 0:W], pt[:, 0:W], dcw[l][:, 0, cc:cc + 1], None, ALU.mult,
                     r=(pt, dcw[l]), w=(acc,))
                for d in range(1, 5):
                    k.stt("dve", acc[:, 0:W], pt[:, d:d + W], dcw[l][:, d, cc:cc + 1], acc[:, 0:W],
                          ALU.mult, ALU.add, r=(pt, dcw[l], acc), w=(acc,))
                ps_free(pt)
                yield
                if grp == 2:
                    k.act(st[:, c, 0:W], acc[:, 0:W], AF.Silu, r=(acc,), w=(st,))
                    tmpA.append(acc)
                    G.done()
                    return
                k.act(st[:, c, 0:W], acc[:, 0:W], AF.Silu, r=(acc,), w=(st,))
                tmpA.append(acc)
                s2 = tb()
                k.act(s2[:, 0:W], st[:, c, 0:W], AF.Square, r=(st,), w=(s2,))
                yield
                p2 = ps_alloc()
                k.mm(p2, p2[:, 0:W], [(ones_b[:], s2[:, 0:W])], r=(s2, ones_b))
                tmpB.append(s2)
                yield
                k.copy("act", G.ssq[:, c, 0:W], p2[:, 0:W], r=(p2,), w=(G.ssq,))
                ps_free(p2)
                G.done()
            gen.nb = 1
            return gen

        def t_simple(ti, gi, c, G, func):
            def gen():
                t0, W = cfg.tiles[ti]
                NC = W + 4
                pt = proj(hTs[ti % 2], NC, wslot[gi], c * 128)
                yield
                k.act(G.st[:, c, 0:W], pt[:, 2:2 + W], func, r=(pt,), w=(G.st,))
                ps_free(pt)
                G.done()
            gen.nb = 1
            return gen

        def t_bg(ti):
            def gen():
                t0, W = cfg.tiles[ti]
                NC = W + 4
                pcol = t0 + PADF
                hT = hTs[ti % 2]
                k.dma("sp", wsm[:, :, :], wl[:, C_B:C_B + 32].rearrange("(c p) n -> p c n", p=128), w=(wsm,))
                pts = []
                for which in range(2):
                    pt = ps_alloc()
                    k.mm(pt, pt[0:16, 0:NC], [(wsm[:, c, which * 16:which * 16 + 16], hT[:, c, 0:NC])
                                              for c in range(8)], r=(wsm, hT))
                    pts.append(pt)
                yield
                k.act(gbs[:, 0, 0:W], pts[0][0:16, 2:2 + W], AF.Sigmoid, r=(pts[0],), w=(gbs,))
                k.act(gbs[:, 1, 0:W], pts[1][0:16, 2:2 + W], AF.Exp, r=(pts[1], dtb[l]), w=(gbs,), bias=dtb[l][:])
                ps_free(pts[0])
                ps_free(pts[1])
                k.act(gbs[:, 1, 0:W], gbs[:, 1, 0:W], AF.Ln, r=(gbs, one_t), w=(gbs,), bias=one_t[0:16, :])
                yield
                k.ts("dve", gbs[:, 1, 0:W], gbs[:, 1, 0:W], nAexp[l][:, 0:1], None, ALU.mult,
                     r=(gbs, nAexp[l]), w=(gbs,))
                k.dma(STQ, gbT.rearrange("(a p) n -> p a n", p=16)[:, :, pcol:pcol + W], gbs[:, :, 0:W], r=(gbs,))
            gen.nb = 2
            return gen

        def t_sc(ti, gi_c, gi_x, gi_b, c, G):
            def gen():
                t0, W = cfg.tiles[ti]
                NC = W + 4
                hT = hTs[ti % 2]
                pc = proj(hT, NC, wslot[gi_c], c * 128)
                px = proj(hT, NC, wslot[gi_x], c * 128)
                pb = proj(hT, NC, wslot[gi_b], c * 128)
                yield
                cx = ta()
                k.copy("act", cx[:, 0:NC], pc[:, 0:NC], r=(pc,), w=(cx,))
                ps_free(pc)
                yield
                pr = ta()
                k.tt("dve", pr[:, 0:NC], px[:, 0:NC], cx[:, 0:NC], ALU.mult, r=(px, cx), w=(pr,))
                ps_free(px)
                tmpA.append(cx)
                yield
                acc = ta()
                k.ts("dve", acc[:, 0:W], pr[:, 1:1 + W], scw[l][:, 0, c:c + 1], None, ALU.mult,
                     r=(pr, scw[l]), w=(acc,))
                for d in range(1, 3):
                    k.stt("dve", acc[:, 0:W], pr[:, 1 + d:1 + d + W], scw[l][:, d, c:c + 1], acc[:, 0:W],
                          ALU.mult, ALU.add, r=(pr, scw[l], acc), w=(acc,))
                tmpA.append(pr)
                yield
                k.tt("dve", G.st[:, c, 0:W], pb[:, 2:2 + W], acc[:, 0:W], ALU.mult, r=(pb, acc), w=(G.st,))
                ps_free(pb)
                tmpA.append(acc)
                G.done()
            gen.nb = 3
            return gen

        tasks = []
        ntile = len(cfg.tiles)
        stgc = [0]
        total_blocks = ntile * nblk
        tasks.append(t_xload(0))
        for gi in range(4):
            tasks.append(t_wload(gi))
        tasks.append(t_pro1(0))
        tasks.append(t_pro2(0))
        for ti, (t0, W) in enumerate(cfg.tiles):
            pcol = t0 + PADF
            base = ti * nblk

            def post(gi):
                if gi + 4 < total_blocks:
                    tasks.append(t_wload(gi + 4))

            def newG(dst, n=8, norm=None, ssq=None):
                st = stg[stgc[0] % 3]
                stgc[0] += 1
                return Grp(st, dst, pcol, W, n, norm, ssq)
            for grp, dst in enumerate((qT, kT, vT)):
                G = newG(dst, norm=(grp if grp < 2 else None), ssq=(ssq8[grp] if grp < 2 else None))
                for c in range(8):
                    tasks.append(t_qkv(ti, base + grp, grp, c, G))
                post(base + grp)
                if grp == 1 and ti + 1 < ntile:
                    tasks.append(t_xload(ti + 1))
            G = newG(zsT)
            for c in range(8):
                tasks.append(t_simple(ti, base + 3, c, G, AF.Silu))
            post(base + 3)
            tasks.append(t_bg(ti))
            G = newG(yscT)
            for c in range(8):
                tasks.append(t_sc(ti, base + 4, base + 5, base + 6, c, G))
            post(base + 4)
            post(base + 5)
            post(base + 6)
            if ti + 1 < ntile:
                tasks.append(t_pro1(ti + 1))
            for bi, dst in ((7, gaT), (8, gbgT)):
                G = newG(dst)
                for c in range(8):
                    tasks.append(t_simple(ti, base + bi, c, G, AF.Sigmoid))
                post(base + bi)
                if bi == 7 and ti + 1 < ntile:
                    tasks.append(t_pro2(ti + 1))
        run_pipeline(tasks, 5)
        k.barrier()

    def run_threads(gens):
        live = list(gens)
        while live:
            for g_ in list(live):
                try:
                    next(g_)
                except StopIteration:
                    live.remove(g_)

    def v3(ap, a):
        return ap.rearrange("p (a b) -> p a b", a=a)

    def phaseB(l):
        k.arena_reset()
        NCK = cfg.NCK
        qTv = qT.rearrange("(c p) n -> p c n", p=128)
        kTv = kT.rearrange("(c p) n -> p c n", p=128)
        vTv = vT.rearrange("(c p) n -> p c n", p=128)
        B = {}
        CDT = F32 if CHAIN_FP32 else BF16
        identc = ident_f if CHAIN_FP32 else ident_b
        for d in range(2):
            rGa_ = k.ar(f"rGa{d}", [128, 8, 128], F32)
            rGi_ = k.ar(f"rGi{d}", [128, 8, 128], F32)
            for sl in range(2):
                B[d, sl] = dict(
                    kq=k.ar(f"kq{d}{sl}", [128, 8, 2, 128], BF16),
                    vt=k.ar(f"vt{d}{sl}", [128, 8, 128], BF16),
                    gbt=k.ar(f"gbt{d}{sl}", [32, 128], F32),
                    gb=k.ar(f"gb{d}{sl}", [128, 32], F32),
                    E=k.ar(f"E{d}{sl}", [128, 24], F32),
                    bege=k.ar(f"bege{d}{sl}", [128, 8], F32),
                    nbeta=k.ar(f"nbeta{d}{sl}", [128, 8], F32),
                    ost=k.ar(f"ost{d}{sl}", [128, 8, 128], BF16),
                    rGa=rGa_, rGi=rGi_,
                )
                for hh in range(2):
                    B[d, sl, hh] = dict(
                        qkm=k.ar(f"qkm{d}{sl}{hh}", [128, 4, 128], BF16),
                        kdec=k.ar(f"kdec{d}{sl}{hh}", [128, 4, 128], BF16),
                        u=k.ar(f"u{d}{sl}{hh}", [128, 4, 128], F32),
                        wT=k.ar(f"wT{d}{sl}{hh}", [128, 4, 128], BF16),
                        qdT=k.ar(f"qdT{d}{sl}{hh}", [128, 4, 128], BF16),
                    )
            for hh in range(2):
                B["t", d, hh] = dict(
                    Dx=k.ar(f"Dx{d}{hh}", [128, 4, 128], F32),
                    DTx=k.ar(f"DTx{d}{hh}", [128, 4, 128], F32),
                    egcb=k.ar(f"egcb{d}{hh}", [128, 4, 128], F32),
                    U=[k.ar(f"U{d}{hh}{i}", [128, 4, 128], CDT) for i in range(1 if CHAIN_FP32 else 2)],
                    W=[k.ar(f"W{d}{hh}{i}", [128, 4, 128], CDT) for i in range(1 if CHAIN_FP32 else 2)],
                    P=[k.ar(f"P{d}{hh}{i}", [128, 4, 128], CDT) for i in range(1 if CHAIN_FP32 else 2)],
                    Pb=k.ar(f"Pb{d}{hh}", [128, 4, 128], BF16),
                    ktok=k.ar(f"ktok{d}{hh}", [128, 4, 128], BF16),
                    bkg=k.ar(f"bkg{d}{hh}", [128, 4, 128], BF16),
                    bv=k.ar(f"bv{d}{hh}", [128, 4, 128], BF16),
                    vnew=k.ar(f"vnew{d}{hh}", [128, 4, 128], BF16),
                    Ssc=k.ar(f"Ssc{d}{hh}", [128, 4, 128], F32),
                    S=k.ar(f"S{d}{hh}", [128, 4, 128], F32),
                    Sb=k.ar(f"Sb{d}{hh}", [128, 4, 128], BF16),
                )
                if CHAIN_FP32:
                    tt_ = B["t", d, hh]
                    tt_["U"].append(tt_["Dx"])
                    tt_["W"].append(tt_["DTx"])
                    tt_["P"].append(tt_["egcb"])
                k.memset("pool", B["t", d, hh]["S"][:], 0.0, w=(B["t", d, hh]["S"],))
                k.memset("pool", B["t", d, hh]["Sb"][:], 0.0, w=(B["t", d, hh]["Sb"],))
        print("arena phase B bytes", k.arena_off)

        def setup(d, c, sl):
            b = B[d, sl]
            Mincl, Maft = masks[:, 2 * d, :], masks[:, 2 * d + 1, :]
            cs = slice(c * 128, (c + 1) * 128)
            k.dma("sp", b["kq"][:, :, 0, :], kTv[:, :, cs], w=(b["kq"],))
            k.dma("sp", b["kq"][:, :, 1, :], qTv[:, :, cs], w=(b["kq"],))
            k.dma("sp", b["vt"][:], vTv[:, :, cs], w=(b["vt"],))
            k.dma("sp", b["gbt"][:], gbT[:, cs], w=(b["gbt"],))
            yield
            pt = nextps()
            k.mm_multi(pt, [(pt[:, 0:32], b["gbt"][:], ident_f[0:32, 0:32], True)], r=(b["gbt"], ident_f))
            k.copy("dve", b["gb"][:], pt[:, 0:32], r=(pt,), w=(b["gb"],))
            yield
            beta = b["gb"][:, d * 8:d * 8 + 8]
            g = b["gb"][:, 16 + d * 8:16 + d * 8 + 8]
            pt = nextps()
            k.mm_multi(pt, [(pt[:, 0:8], Mincl, g, False), (pt[:, 8:16], Maft, g, False),
                            (pt[:, 16:24], ones_f[:], g, False)], r=(masks, ones_f, b["gb"]))
            k.act(b["E"][:], pt[:, 0:24], AF.Exp, r=(pt,), w=(b["E"],))
            k.tt("pool", b["rGa"][:], bc(masks[:, 2 * d + 1:2 * d + 2, :], [128, 8, 128]),
                 bc(g.unsqueeze(2), [128, 8, 128]), ALU.mult, r=(masks, b["gb"]), w=(b["rGa"],))
            k.tt("pool", b["rGi"][:], bc(masks[:, 2 * d:2 * d + 1, :], [128, 8, 128]),
                 bc(g.unsqueeze(2), [128, 8, 128]), ALU.mult, r=(masks, b["gb"]), w=(b["rGi"],))
            yield
            k.tt("dve", b["bege"][:], beta, b["E"][:, 0:8], ALU.mult, r=(b["gb"], b["E"]), w=(b["bege"],))
            k.ts("dve", b["nbeta"][:], beta, -1.0, None, ALU.mult, r=(b["gb"],), w=(b["nbeta"],))
            yield

        def prep(d, c, sl, hh):
            b = B[d, sl]
            bh = B[d, sl, hh]
            t = B["t", d, hh]
            hs = slice(4 * hh, 4 * hh + 4)
            Mincl, Maft = masks[:, 2 * d, :], masks[:, 2 * d + 1, :]
            Mincl_bc = bc(masks[:, 2 * d:2 * d + 1, :], [128, 4, 128])
            Maft_bc = bc(masks[:, 2 * d + 1:2 * d + 2, :], [128, 4, 128])
            beta = b["gb"][:, d * 8 + 4 * hh:d * 8 + 4 * hh + 4]
            cr = (lambda a: a.bitcast(mybir.dt.float32r)) if (CHAIN_FP32 and CHAIN_R) else (lambda a: a)
            pD = nextps()
            k.mm(pD, pD[:], [(Mincl, b["rGa"][:, hs, :])], r=(masks, b["rGa"]))
            k.act(v3(t["Dx"][:].rearrange("p a b -> p (a b)"), 4), v3(pD[:], 4), AF.Exp, r=(pD,), w=(t["Dx"],))
            pDT = nextps()
            k.mm(pDT, pDT[:], [(Maft, b["rGi"][:, hs, :])], r=(masks, b["rGi"]))
            k.act(t["DTx"][:], v3(pDT[:], 4), AF.Exp, r=(pDT,), w=(t["DTx"],))
            yield
            pG = nextps()
            k.mm(pG, pG[:], [(ones_f[:], b["rGi"][:, hs, :])], r=(ones_f, b["rGi"]))
            k.act(t["egcb"][:], v3(pG[:], 4), AF.Exp, r=(pG,), w=(t["egcb"],))
            k.tt("pool", t["Dx"][:], t["Dx"][:], Maft_bc, ALU.mult, r=(t["Dx"], masks), w=(t["Dx"],))
            k.tt("pool", t["Dx"][:], t["Dx"][:], bc(b["nbeta"][:, hs].unsqueeze(2), [128, 4, 128]), ALU.mult,
                 r=(t["Dx"], b["nbeta"]), w=(t["Dx"],))
            k.tt("pool", t["DTx"][:], t["DTx"][:], Mincl_bc, ALU.mult, r=(t["DTx"], masks), w=(t["DTx"],))
            yield
            W0 = t["W"][0]
            for pair in range(2):
                pk = nextps()
                groups = []
                for hl in range(2):
                    h = 4 * hh + 2 * pair + hl
                    groups.append((pk[:, hl * 256:(hl + 1) * 256], b["kq"][:, h, 0, :],
                                   b["kq"][:, h, :, :].rearrange("p a b -> p (a b)"), False))
                k.mm_multi(pk, groups, r=(b["kq"],))
                pkv = v3(pk[:], 2)
                k.tt("dve", cr(W0[:, 2 * pair:2 * pair + 2, :]), pkv[:, :, 0:128], t["Dx"][:, 2 * pair:2 * pair + 2, :],
                     ALU.mult, r=(pk, t["Dx"]), w=(W0,))
                k.tt("dve", bh["qkm"][:, 2 * pair:2 * pair + 2, :], pkv[:, :, 128:256],
                     t["DTx"][:, 2 * pair:2 * pair + 2, :], ALU.mult, r=(pk, t["DTx"]), w=(bh["qkm"],))
            yield
            U0 = t["U"][0]
            pt = nextps()
            ptb = v3(pt[:], 4) if CHAIN_FP32 else v3(pt[:].bitcast(BF16)[:, 0:512], 4)
            k.mm_multi(pt, [(ptb[:, hl, :], W0[:, hl, :], identc[:], True) for hl in range(4)], r=(W0, identc))
            k.copy("dve" if CHAIN_R else "act", cr(U0[:]), ptb, r=(pt,), w=(U0,))
            pt = nextps()
            ptb = v3(pt[:].bitcast(BF16)[:, 0:512], 4)
            k.mm_multi(pt, [(ptb[:, hl, :], b["kq"][:, 4 * hh + hl, 0, :], ident_b[:], True) for hl in range(4)],
                       r=(b["kq"], ident_b))
            k.copy("act", t["ktok"][:], ptb, r=(pt,), w=(t["ktok"],))
            pt = nextps()
            ptb = v3(pt[:].bitcast(BF16)[:, 0:512], 4)
            k.mm_multi(pt, [(ptb[:, hl, :], b["vt"][:, 4 * hh + hl, :], ident_b[:], True) for hl in range(4)],
                       r=(b["vt"], ident_b))
            k.tt("dve", t["bv"][:], ptb, bc(beta.unsqueeze(2), [128, 4, 128]), ALU.mult, r=(pt, b["gb"]),
                 w=(t["bv"],))
            yield
            k.tt("pool", t["bkg"][:], t["ktok"][:], bc(b["bege"][:, hs].unsqueeze(2), [128, 4, 128]), ALU.mult,
                 r=(t["ktok"], b["bege"]), w=(t["bkg"],))
            k.tt("pool", bh["kdec"][:], t["ktok"][:], bc(b["E"][:, 8 + 4 * hh:12 + 4 * hh].unsqueeze(2), [128, 4, 128]),
                 ALU.mult, r=(t["ktok"], b["E"]), w=(bh["kdec"],))
            k.tt("pool", bh["qdT"][:], b["kq"][:, hs, 1, :], t["egcb"][:], ALU.mult, r=(b["kq"], t["egcb"]),
                 w=(bh["qdT"],))
            k.tt("dve", cr(t["P"][0][:]), U0[:], bc(identc[:].unsqueeze(1), [128, 4, 128]), ALU.add,
                 r=(U0, identc), w=(t["P"][0],))
            yield
            for lev in range(6):
                Uc, Wc = t["U"][lev % 2], t["W"][lev % 2]
                Un, Wn = t["U"][(lev + 1) % 2], t["W"][(lev + 1) % 2]
                Pc, Pn = t["P"][lev % 2], t["P"][(lev + 1) % 2]
                pB = nextps()
                k.mm_multi(pB, [(pB[:, hl * 128:(hl + 1) * 128], cr(Uc[:, hl, :]), cr(Wc[:, hl, :]), False) for hl in range(4)],
                           r=(Wc, Uc))
                k.copy("dve", cr(Wn[:]), v3(pB[:], 4), r=(pB,), w=(Wn,))
                yield
                if lev < 5:
                    pA = nextps()
                    if CHAIN_FP32:
                        pav = v3(pA[:], 4)
                    else:
                        pav = v3(pA[:].bitcast(BF16)[:, 0:512], 4)
                    k.mm_multi(pA, [(pav[:, hl, :], Wn[:, hl, :], identc[:], True) for hl in range(4)],
                               r=(Wn, identc))
                    k.copy("act", cr(Un[:]), pav, r=(pA,), w=(Un,))
                pC = nextps()
                k.mm_multi(pC, [(pC[:, hl * 128:(hl + 1) * 128], cr(Wn[:, hl, :]), cr(Pc[:, hl, :]), False) for hl in range(4)],
                           r=(Wn, Pc))
                k.tt("dve", cr(Pn[:]), v3(pC[:], 4), Pc[:], ALU.add, r=(pC, Pc), w=(Pn,))
                yield
            Pf = t["P"][0]
            if CHAIN_FP32:
                k.copy("pool", t["Pb"][:], Pf[:], r=(Pf,), w=(t["Pb"],))
                Pf = t["Pb"]
            pu = nextps()
            k.mm_multi(pu, [(pu[:, hl * 128:(hl + 1) * 128], Pf[:, hl, :], t["bv"][:, hl, :], False) for hl in range(4)],
                       r=(Pf, t["bv"]))
            k.copy("act", bh["u"][:], v3(pu[:], 4), r=(pu,), w=(bh["u"],))
            pw = nextps()
            k.mm_multi(pw, [(pw[:, hl * 128:(hl + 1) * 128], t["bkg"][:, hl, :], Pf[:, hl, :], False) for hl in range(4)],
                       r=(Pf, t["bkg"]))
            k.copy("dve", bh["wT"][:], v3(pw[:], 4), r=(pw,), w=(bh["wT"],))
            yield

        def scan(d, c, sl, hh):
            b = B[d, sl]
            bh = B[d, sl, hh]
            t = B["t", d, hh]
            S, Sb = t["S"], t["Sb"]
            pws = nextps()
            k.mm_multi(pws, [(pws[:, hl * 128:(hl + 1) * 128], bh["wT"][:, hl, :], Sb[:, hl, :], False)
                             for hl in range(4)], r=(bh["wT"], Sb))
            k.tt("dve", t["vnew"][:], bh["u"][:], v3(pws[:], 4), ALU.subtract, r=(bh["u"], pws), w=(t["vnew"],))
            k.tt("pool", t["Ssc"][:], S[:], bc(b["E"][:, 16 + 4 * hh:20 + 4 * hh].unsqueeze(2), [128, 4, 128]),
                 ALU.mult, r=(S, b["E"]), w=(t["Ssc"],))
            yield
            po = nextps()

            def fn(g, po=po, Sb=Sb, bh=bh, t=t):
                ins = None
                for hl in range(4):
                    o_ = po[:, hl * 128:(hl + 1) * 128]
                    g.matmul(o_, lhsT=Sb[:, hl, :], rhs=bh["qdT"][:, hl, :], start=True, stop=False)
                    ins = g.matmul(o_, lhsT=t["vnew"][:, hl, :], rhs=bh["qkm"][:, hl, :], start=False, stop=True)
                return ins
            k.op("pe", fn, r=(Sb, bh["qdT"], t["vnew"], bh["qkm"]), w=(po,))
            k.copy("act", b["ost"][:, 4 * hh:4 * hh + 4, :], v3(po[:], 4), r=(po,), w=(b["ost"],))
            pds = nextps()
            k.mm_multi(pds, [(pds[:, hl * 128:(hl + 1) * 128], bh["kdec"][:, hl, :], t["vnew"][:, hl, :], False)
                             for hl in range(4)], r=(bh["kdec"], t["vnew"]))
            k.tt("dve", S[:], t["Ssc"][:], v3(pds[:], 4), ALU.add, r=(t["Ssc"], pds), w=(S,))
            k.copy("act", Sb[:], S[:], r=(S,), w=(Sb,))
            yield

        def store(d, c, sl):
            b = B[d, sl]
            k.dma(STQ, oT[d].rearrange("(h p) n -> p h n", p=128)[:, :, c * 128:(c + 1) * 128], b["ost"][:],
                  r=(b["ost"],))

        def chunk_of(d, s):
            return s if d == 0 else NCK - 1 - s

        run_threads([setup(d, chunk_of(d, 0), 0) for d in range(2)])
        run_threads([prep(d, chunk_of(d, 0), 0, hh) for d in range(2) for hh in range(2)])
        for s in range(NCK):
            sl = s % 2
            th = []
            if s + 1 < NCK:
                run_threads([setup(d, chunk_of(d, s + 1), 1 - sl) for d in range(2)])
                th += [prep(d, chunk_of(d, s + 1), 1 - sl, hh) for d in range(2) for hh in range(2)]
            th += [scan(d, chunk_of(d, s), sl, hh) for d in range(2) for hh in range(2)]
            run_threads(th)
            for d in range(2):
                store(d, chunk_of(d, s), sl)
        k.barrier()


    def phaseC(l, last):
        k.arena_reset()
        x = k.ar("Cx", [128, 8, TW], F32)
        osum = k.ar("Cosum", [128, 8, TW], F32)
        b_of = k.ar("Cof", [128, 8, TW], BF16)
        b_ob = k.ar("Cob", [128, 8, TW], BF16)
        b_zs = k.ar("Czs", [128, 8, TW], BF16)
        b_ysc = k.ar("Cysc", [128, 8, TW], BF16)
        b_ga = k.ar("Cga", [128, 8, TW], BF16)
        b_gb = k.ar("Cgb", [128, 8, TW], BF16)
        a_t = k.ar("Ca", [128, 22, TW], BF16)
        ssq8 = a_t.ap.rearrange("p a b -> p (a b)")[:, 0:16 * TW].bitcast(F32).rearrange("p (a b) -> p a b", a=8)
        wb = [k.ar(f"Cw{i}", [128, 8, 1024], BF16) for i in range(4)]
        wi = [0]
        tA = [k.ar(f"CtA{i}", [128, TW], F32) for i in range(6)]
        tAi = [0]
        tB = [k.ar(f"CtB{i}", [128, TW], BF16) for i in range(2)]
        tBi = [0]
        rs_t = k.ar("Crstd", [128, TW], F32)
        otile = [k.ar(f"Cot{i}", [128, 1024], F32) for i in range(2)]
        oti = [0]
        print("arena phase C bytes", k.arena_off)
        odn, sq2, mg, h2 = b_of, b_ob, b_zs, b_of
        fm = lambda arr: arr.rearrange("(c p) n -> p c n", p=128)

        def wload(src2, rows0, cols0, ncols, nk=8, dst=None, dcol=0):
            wt = dst if dst is not None else nxt(wb, wi)
            k.dma("sp", wt[:, 0:nk, dcol:dcol + ncols],
                  src2[rows0:rows0 + nk * 128, cols0:cols0 + ncols].rearrange("(c p) n -> p c n", p=128), w=(wt,))
            return wt

        for (t0, W) in cfg.tiles:
            pcol = t0 + PADF
            k.dma("sp", x[:, :, 0:W], xTv[:, :, 2 + t0:2 + t0 + W], w=(x,))
            for buf, arr in ((b_of, oT[0]), (b_ob, oT[1]), (b_zs, zsT), (b_ysc, yscT), (b_ga, gaT), (b_gb, gbgT)):
                k.dma("sp", buf[:, :, 0:W], fm(arr)[:, :, pcol:pcol + W], w=(buf,))
            k.tt("dve", osum[:, :, 0:W], b_of[:, :, 0:W], b_ob[:, :, 0:W], ALU.add, r=(b_of, b_ob), w=(osum,))
            k.act(sq2[:, :, 0:W], osum[:, :, 0:W], AF.Square, r=(osum,), w=(sq2,))
            for c in range(8):
                p2 = nextps()
                k.mm(p2, p2[:, 0:W], [(ones_b[:], sq2[:, c, 0:W])], r=(sq2, ones_b))
                k.copy("act", ssq8[:, c, 0:W], p2[:, 0:W], r=(p2,), w=(a_t,))
            k.rsqrt(ssq8[:, :, 0:W], ssq8[:, :, 0:W], 1.0 / 128, eps_t[:], r=(a_t, eps_t), w=(a_t,))
            k.stt("dve", osum[:, :, 0:W], osum[:, :, 0:W], dnw[l][:, 0:1], ssq8[:, :, 0:W], ALU.mult, ALU.mult,
                  r=(osum, dnw[l], a_t), w=(osum,))
            k.tt("dve", odn[:, :, 0:W], osum[:, :, 0:W], b_zs[:, :, 0:W], ALU.mult, r=(osum, b_zs), w=(odn,))
            wdn = wload(wb_bdn[l], 0, 0, 1024)
            wsc = wload(wb_bsc[l], 0, 0, 1024)
            for m in range(8):
                pa = nextps()
                k.mm(pa, pa[:, 0:W], [(wdn[:, c, m * 128:(m + 1) * 128], odn[:, c, 0:W]) for c in range(8)],
                     r=(wdn, odn))
                pb = nextps()
                k.mm(pb, pb[:, 0:W], [(wsc[:, c, m * 128:(m + 1) * 128], b_ysc[:, c, 0:W]) for c in range(8)],
                     r=(wsc, b_ysc))
                t1 = nxt(tA, tAi)
                k.tt("dve", t1[:, 0:W], pa[:, 0:W], b_ga[:, m, 0:W], ALU.mult, r=(pa, b_ga), w=(t1,))
                t2 = nxt(tA, tAi)
                k.tt("dve", t2[:, 0:W], pb[:, 0:W], b_gb[:, m, 0:W], ALU.mult, r=(pb, b_gb), w=(t2,))
                k.tt("dve", mg[:, m, 0:W], t1[:, 0:W], t2[:, 0:W], ALU.add, r=(t1, t2), w=(mg,))
            wo = wload(wb_out[l], 0, 0, 1024)
            for m in range(8):
                pm = nextps()
                k.mm(pm, pm[:, 0:W], [(wo[:, c, m * 128:(m + 1) * 128], mg[:, c, 0:W]) for c in range(8)],
                     r=(wo, mg))
                k.tt("dve", x[:, m, 0:W], x[:, m, 0:W], pm[:, 0:W], ALU.add, r=(x, pm), w=(x,))
            k.act(sq2[:, :, 0:W], x[:, :, 0:W], AF.Square, r=(x,), w=(sq2,))
            pt = nextps()
            k.mm(pt, pt[:, 0:W], [(ones_b[:], sq2[:, c, 0:W]) for c in range(8)], r=(sq2, ones_b))
            k.rsqrt(rs_t[:, 0:W], pt[:, 0:W], 1.0 / D, eps_t[:], r=(pt, eps_t), w=(rs_t,))
            for c in range(8):
                k.stt("dve", h2[:, c, 0:W], x[:, c, 0:W], n2w[l][:, c:c + 1], rs_t[:, 0:W], ALU.mult, ALU.mult,
                      r=(x, rs_t, n2w[l]), w=(h2,))
            for j0 in range(0, NFF, 4):
                nj = min(4, NFF - j0)
                wt = nxt(wb, wi)
                wload(wb_gu[l], 0, j0 * 128, nj * 128, dst=wt, dcol=0)
                wload(wb_gu[l], 0, DFF + j0 * 128, nj * 128, dst=wt, dcol=512)
                for jj in range(nj):
                    j = j0 + jj
                    pg = nextps()
                    k.mm(pg, pg[:, 0:W], [(wt[:, c, jj * 128:(jj + 1) * 128], h2[:, c, 0:W]) for c in range(8)],
                         r=(wt, h2))
                    pu = nextps()
                    k.mm(pu, pu[:, 0:W], [(wt[:, c, 512 + jj * 128:512 + (jj + 1) * 128], h2[:, c, 0:W])
                                          for c in range(8)], r=(wt, h2))
                    sg = nxt(tA, tAi)
                    k.act(sg[:, 0:W], pg[:, 0:W], AF.Silu, r=(pg,), w=(sg,))
                    k.tt("dve", a_t[:, j, 0:W], sg[:, 0:W], pu[:, 0:W], ALU.mult, r=(sg, pu), w=(a_t,))
            wd = [wload(wb_down[l], kb * 1024, 0, 1024, nk=min(8, NFF - kb * 8)) for kb in range(3)]
            for m in range(8):
                pd = nextps()
                k.mm(pd, pd[:, 0:W], [(wd[j // 8][:, j % 8, m * 128:(m + 1) * 128], a_t[:, j, 0:W])
                                      for j in range(NFF)], r=(wd[0], wd[1], wd[2], a_t))
                k.tt("dve", x[:, m, 0:W], x[:, m, 0:W], pd[:, 0:W], ALU.add, r=(x, pd), w=(x,))
            if not last:
                k.dma(STQ, xTv[:, :, 2 + t0:2 + t0 + W], x[:, :, 0:W], r=(x,))
            else:
                k.act(sq2[:, :, 0:W], x[:, :, 0:W], AF.Square, r=(x,), w=(sq2,))
                pt = nextps()
                k.mm(pt, pt[:, 0:W], [(ones_b[:], sq2[:, c, 0:W]) for c in range(8)], r=(sq2, ones_b))
                k.rsqrt(rs_t[:, 0:W], pt[:, 0:W], 1.0 / D, eps_t[:], r=(pt, eps_t), w=(rs_t,))
                xn = osum
                for c in range(8):
                    k.stt("dve", xn[:, c, 0:W], x[:, c, 0:W], fw[:, c:c + 1], rs_t[:, 0:W], ALU.mult, ALU.mult,
                          r=(x, rs_t, fw), w=(xn,))
                lo = max(t0, NMETA)
                while lo < t0 + W:
                    nn = min(128, t0 + W - lo)
                    ot = nxt(otile, oti)
                    for half in range(2):
                        pt = nextps()
                        k.mm_multi(pt, [(pt[0:nn, cc * 128:(cc + 1) * 128],
                                         xn[:, half * 4 + cc, lo - t0:lo - t0 + nn], ident_f[:], True)
                                        for cc in range(4)], r=(xn, ident_f))
                        k.copy("act" if half else "dve", ot[0:nn, half * 512:(half + 1) * 512], pt[0:nn, :],
                               r=(pt,), w=(ot,))
                    k.dma(STQ, out[lo - NMETA:lo - NMETA + nn, :], ot[0:nn, :], r=(ot,))
                    lo += nn
        k.barrier()

    phases = cfg.__dict__.get("phases", "ABC")
    nl = cfg.__dict__.get("nlayers", DEPTH)
    for l in range(nl):
        phaseA(l)
        if "B" in phases:
            phaseB(l)
        if "C" in phases:
            phaseC(l, l == nl - 1)

    k.barrier()
    with nc.Block() as block:
        @block.tensor
        def _(g):
            k.replay("pe", g)

        @block.scalar
        def _(g):
            k.replay("act", g)

        @block.vector
        def _(g):
            k.replay("dve", g)

        @block.gpsimd
        def _(g):
            k.replay("pool", g)

        @block.sync
        def _(g):
            k.replay("sp", g)
    es.close()
    return nc


def make_masks():
    m = np.zeros((8, 128, 128), np.float32)
    t = np.arange(128)[:, None]
    i = np.arange(128)[None, :]
    m[0] = (t <= i)
    m[1] = (t > i)
    m[2] = (t >= i)
    m[3] = (t < i)
    return m


def core_inputs(cfg, inputs, b):
    f = np.float32
    xin = np.concatenate([inputs["meta_tokens"].astype(f), inputs["x"][b].astype(f)], axis=0)
    d = dict(
        xin=np.ascontiguousarray(xin),
        norm1_w=inputs["norm1_w"], w_in=inputs["w_in"], dn_conv_w=inputs["dn_conv_w"],
        A_log=inputs["A_log"].reshape(cfg.depth, 16), dt_bias=inputs["dt_bias"].reshape(cfg.depth, 16),
        dn_norm_w=inputs["dn_norm_w"], sc_conv_w=inputs["sc_conv_w"],
        w_branch_dn=inputs["w_branch_dn"], w_branch_sc=inputs["w_branch_sc"], w_out=inputs["w_out"],
        norm2_w=inputs["norm2_w"], w_gate_up=inputs["w_gate_up"], w_down=inputs["w_down"],
        final_norm_w=inputs["final_norm_w"],
        c_ident_f=np.eye(128, dtype=f), c_masks=make_masks())
    return {k_: np.ascontiguousarray(np.asarray(v, dtype=f)) for k_, v in d.items()}


_NC_CACHE = {}


def kernel(**inputs):
    x = inputs["x"]
    bsz, seq, _ = x.shape
    cfg = Cfg(seq, 2)
    key = (seq,)
    if key not in _NC_CACHE:
        _NC_CACHE[key] = build(cfg)
    nc = _NC_CACHE[key]
    in_maps = [core_inputs(cfg, inputs, b) for b in range(bsz)]
    res = run_bass_kernel_spmd(nc, in_maps, core_ids=list(range(bsz)))
    return np.stack([np.asarray(r["out"], dtype=np.float32) for r in res.results], axis=0)
```

```python
import numpy as np
import ml_dtypes
from contextlib import ExitStack
import concourse.bass as bass
import concourse.mybir as mybir
from concourse.bass_utils import run_bass_kernel_spmd

F32 = mybir.dt.float32
BF16 = mybir.dt.bfloat16
AF = mybir.ActivationFunctionType
ALU = mybir.AluOpType

D = 1024
NCH = 8
H = 8
NMETA = 16
DFF = 2816
NFF = 22
WIN = 9248
RMS_EPS = 1e-6
L2_EPS = 1e-6
TW = 508
SEM_ROLL = 30000
CHAIN_FP32 = True
CHAIN_R = False
STQ = "sp"

C_Q, C_K, C_V, C_Z, C_B, C_A, C_SB, C_SC, C_SX, C_GA, C_GB = (
    0, 1024, 2048, 3072, 4096, 4112, 4128, 5152, 6176, 7200, 8224)


class Tl:
    def __init__(self, name, ap):
        self.name = name
        self.ap = ap
        self.w = None
        self.r = {}

    def __getitem__(self, idx):
        return self.ap[idx]


class Eng:
    def __init__(self, name, sems):
        self.name = name
        self.sems = sems
        self.si = 0
        self.cnt = 0
        self.ops = []
        self.waited = {}


class KB:
    def __init__(self, nc, es):
        self.nc = nc
        self.es = es
        self.semh = []
        self.eng = {}
        for n in ("pe", "act", "dve", "pool", "sp"):
            ids = [self._newsem(f"s_{n}{i}") for i in range(3)]
            self.eng[n] = Eng(n, ids)
        self.dq = {}
        for q, cnt in (("sp", 14), ("pool", 2), ("act", 4)):
            self.dq[q] = dict(ids=[self._newsem(f"d_{q}{i}") for i in range(cnt)],
                              cnt=[0] * cnt, nxt=0)
        self.all_tokens = {}
        self.ntile = 0

    def _newsem(self, name):
        h = self.es.enter_context(self.nc.semaphore(name))
        self.semh.append(h)
        return len(self.semh) - 1

    def sb(self, name, shape, dt):
        t = self.es.enter_context(self.nc.sbuf_tensor(name, list(shape), dt))
        return Tl(name, t)

    def arena_init(self, nbytes):
        self.arena = self.es.enter_context(self.nc.sbuf_tensor("arena", [128, nbytes // 2], BF16))
        self.arena_size = nbytes
        self.arena_off = 0

    def arena_reset(self):
        self.arena_off = 0

    def ar(self, name, shape, dt):
        esz = 4 if dt == F32 else 2
        n = 1
        for d_ in shape[1:]:
            n *= d_
        nb = (n * esz + 31) // 32 * 32
        off = self.arena_off
        assert off + nb <= self.arena_size, f"arena overflow at {name}: {off + nb}"
        self.arena_off += nb
        ap = self.arena[0:shape[0], off // 2:(off + n * esz) // 2]
        if dt == F32:
            ap = ap.bitcast(F32)
        if len(shape) == 3:
            ap = ap.rearrange("p (a b) -> p a b", a=shape[1])
        elif len(shape) == 4:
            ap = ap.rearrange("p (a b c) -> p a b c", a=shape[1], b=shape[2])
        return Tl(name, ap)

    def ps(self, name, shape, dt=F32):
        t = self.es.enter_context(self.nc.psum_tensor(name, list(shape), dt))
        return Tl(name, t)

    def _deps(self, e, r, w, extra=()):
        waits = {}

        def add(tok):
            if tok is None:
                return
            s, v = tok
            if waits.get(s, 0) < v:
                waits[s] = v
        for t in r:
            add(t.w)
        for t in w:
            add(t.w)
            for s, v in t.r.items():
                add((s, v))
        for tok in extra:
            add(tok)
        need = []
        for s, v in waits.items():
            if e.waited.get(s, 0) < v:
                e.waited[s] = v
                need.append((s, v))
        return need

    def _mark(self, tok, r, w):
        s, v = tok
        for t in r:
            if t.r.get(s, 0) < v:
                t.r[s] = v
        for t in w:
            t.w = tok
            t.r = {}
        if self.all_tokens.get(s, 0) < v:
            self.all_tokens[s] = v

    def op(self, engine, fn, r=(), w=(), extra=()):
        e = self.eng[engine]
        need = self._deps(e, r, w, extra)
        if e.cnt >= SEM_ROLL:
            e.si += 1
            e.cnt = 0
        e.cnt += 1
        sid = e.sems[e.si]
        tok = (sid, e.cnt)
        e.ops.append((need, fn, sid, 1))
        self._mark(tok, r, w)
        return tok

    def dma(self, queue, out, in_, r=(), w=(), extra=()):
        e = self.eng[queue]
        dq = self.dq[queue]
        i = dq["nxt"]
        dq["nxt"] = (i + 1) % len(dq["ids"])
        sid = dq["ids"][i]
        prev = (sid, dq["cnt"][i]) if dq["cnt"][i] else None
        need = self._deps(e, r, w, tuple(extra) + ((prev,) if prev else ()))
        dq["cnt"][i] += 16
        tok = (sid, dq["cnt"][i])
        e.ops.append((need, lambda g, o=out, s=in_: g.dma_start(out=o, in_=s), sid, 16))
        self._mark(tok, r, w)
        return tok

    def barrier(self):
        toks = list(self.all_tokens.items())
        for e in self.eng.values():
            need = []
            for s, v in toks:
                if e.waited.get(s, 0) < v:
                    e.waited[s] = v
                    need.append((s, v))
            if need:
                e.ops.append((need, None, None, 0))

    def replay(self, engine, g):
        for need, fn, sid, inc in self.eng[engine].ops:
            for s, v in need:
                g.wait_ge(self.semh[s], v)
            if fn is None:
                continue
            ins = fn(g)
            if inc:
                ins.then_inc(self.semh[sid], inc)

    def mm(self, out_t, out_ap, pairs, r=(), transpose=False):
        n = len(pairs)

        def fn(g, out_ap=out_ap, pairs=pairs, n=n):
            ins = None
            for i, (a, b) in enumerate(pairs):
                ins = g.matmul(out_ap, lhsT=a, rhs=b, start=(i == 0), stop=(i == n - 1))
            return ins
        return self.op("pe", fn, r=r, w=(out_t,))

    def mm_multi(self, out_t, groups, r=()):
        def fn(g, groups=groups):
            ins = None
            for (o, a, b, tr) in groups:
                if tr:
                    ins = g.transpose(o, a, b)
                else:
                    ins = g.matmul(o, lhsT=a, rhs=b, start=True, stop=True)
            return ins
        return self.op("pe", fn, r=r, w=(out_t,))

    def act(self, out, in_, func, r=(), w=(), bias=None, scale=None, eng="act"):
        kw = {}
        if bias is not None:
            kw["bias"] = bias
        if scale is not None:
            kw["scale"] = scale
        return self.op(eng, lambda g, o=out, i=in_, f=func, kw=kw: g.activation(out=o, in_=i, func=f, **kw),
                       r=r, w=w)

    def tt(self, eng, out, in0, in1, op, r=(), w=()):
        return self.op(eng, lambda g, o=out, a=in0, b=in1, p=op: g.tensor_tensor(out=o, in0=a, in1=b, op=p),
                       r=r, w=w)

    def stt(self, eng, out, in0, scalar, in1, op0, op1, r=(), w=()):
        return self.op(eng, lambda g, o=out, a=in0, s=scalar, b=in1, p0=op0, p1=op1:
                       g.scalar_tensor_tensor(out=o, in0=a, scalar=s, in1=b, op0=p0, op1=p1), r=r, w=w)

    def ts(self, eng, out, in0, s1, s2, op0, op1=None, r=(), w=()):
        if op1 is None:
            return self.op(eng, lambda g, o=out, a=in0, s=s1, p0=op0:
                           g.tensor_scalar(out=o, in0=a, scalar1=s, scalar2=None, op0=p0), r=r, w=w)
        return self.op(eng, lambda g, o=out, a=in0, x=s1, y=s2, p0=op0, p1=op1:
                       g.tensor_scalar(out=o, in0=a, scalar1=x, scalar2=y, op0=p0, op1=p1), r=r, w=w)

    def copy(self, eng, out, in_, r=(), w=()):
        if eng == "act":
            return self.op(eng, lambda g, o=out, i=in_: g.copy(out=o, in_=i), r=r, w=w)
        return self.op(eng, lambda g, o=out, i=in_: g.tensor_copy(out=o, in_=i), r=r, w=w)

    def rsqrt(self, out, in_, scale, bias_ap, r=(), w=()):
        self.act(out, in_, AF.Ln, r=r, w=w, bias=bias_ap, scale=scale)
        return self.act(out, out, AF.Exp, r=w, w=w, scale=-0.5)

    def recip(self, out, in_, r=(), w=()):
        return self.op("dve", lambda g, o=out, i=in_: g.reciprocal(out=o, in_=i), r=r, w=w)

    def memset(self, eng, ap, val, w=()):
        return self.op(eng, lambda g, a=ap, v=val: g.memset(a, v), w=w)


def bc(ap, shape):
    return ap.to_broadcast(list(shape))


class Cfg:
    def __init__(self, seq, depth):
        self.seq = seq
        self.depth = depth
        self.L = seq + NMETA
        self.PADF = (-self.L) % 128
        self.T = self.L + self.PADF
        self.NCK = self.T // 128
        self.XOFF = 2
        self.XW = self.L + 4
        self.tiles = []
        t0 = 0
        while t0 < self.L:
            w = min(TW, self.L - t0)
            self.tiles.append((t0, w))
            t0 += w


def build(cfg, debug=False):
    nc = bass.Bass("TRN2", target_bir_lowering=False)
    L, T, DEPTH = cfg.L, cfg.T, cfg.depth
    es = ExitStack()
    k = KB(nc, es)

    def din(name, shape, dt=F32):
        return nc.dram_tensor(name, list(shape), dt, kind="ExternalInput").ap()

    def dscr(name, shape, dt):
        kind = "ExternalOutput" if debug else "Internal"
        return nc.dram_tensor(name, list(shape), dt, kind=kind).ap()

    xin = din("xin", [L, D])
    norm1_w = din("norm1_w", [DEPTH, D])
    w_in = din("w_in", [DEPTH, D, WIN])
    dn_conv_w = din("dn_conv_w", [DEPTH, 5, 3072])
    A_log = din("A_log", [DEPTH, 16])
    dt_bias = din("dt_bias", [DEPTH, 16])
    dn_norm_w = din("dn_norm_w", [DEPTH, 128])
    sc_conv_w = din("sc_conv_w", [DEPTH, 3, 1024])
    w_bdn = din("w_branch_dn", [DEPTH, D, D])
    w_bsc = din("w_branch_sc", [DEPTH, D, D])
    w_out = din("w_out", [DEPTH, D, D])
    norm2_w = din("norm2_w", [DEPTH, D])
    w_gu = din("w_gate_up", [DEPTH, D, 2 * DFF])
    w_down = din("w_down", [DEPTH, DFF, D])
    final_w = din("final_norm_w", [D])
    c_ident_f = din("c_ident_f", [128, 128])
    c_masks = din("c_masks", [8, 128, 128])
    out = nc.dram_tensor("out", [cfg.seq, D], F32, kind="ExternalOutput").ap()

    wb_in = dscr("wb_in", [DEPTH, D, WIN], BF16)
    wb_bdn = dscr("wb_bdn", [DEPTH, D, D], BF16)
    wb_bsc = dscr("wb_bsc", [DEPTH, D, D], BF16)
    wb_out = dscr("wb_out", [DEPTH, D, D], BF16)
    wb_gu = dscr("wb_gu", [DEPTH, D, 2 * DFF], BF16)
    wb_down = dscr("wb_down", [DEPTH, DFF, D], BF16)
    xT = dscr("xT", [D, cfg.XW], F32)
    qT = dscr("qT", [D, T], BF16)
    kT = dscr("kT", [D, T], BF16)
    vT = dscr("vT", [D, T], BF16)
    gbT = dscr("gbT", [32, T], F32)
    zsT = dscr("zsT", [D, T], BF16)
    yscT = dscr("yscT", [D, T], BF16)
    gaT = dscr("gaT", [D, T], BF16)
    gbgT = dscr("gbgT", [D, T], BF16)
    oT = [dscr(f"oT{d}", [D, T], BF16) for d in range(2)]

    ident_f = k.sb("ident_f", [128, 128], F32)
    ident_b = k.sb("ident_b", [128, 128], BF16)
    ones_b = k.sb("ones_b", [128, 128], BF16)
    ones_f = k.sb("ones_f", [128, 128], F32)
    masks = k.sb("masks", [128, 8, 128], F32)
    zero_b = k.sb("zero_b", [128, 1024], BF16)
    zero_f = k.sb("zero_f", [128, 512], F32)
    k.dma("sp", ident_f[:], c_ident_f[:, :], w=(ident_f,))
    k.dma("sp", masks[:], c_masks.rearrange("m p n -> p m n"), w=(masks,))
    k.copy("dve", ident_b[:], ident_f[:], r=(ident_f,), w=(ident_b,))
    k.memset("dve", ones_b[:], 1.0, w=(ones_b,))
    k.memset("dve", ones_f[:], 1.0, w=(ones_f,))
    k.memset("pool", zero_b[:], 0.0, w=(zero_b,))
    k.memset("pool", zero_f[:], 0.0, w=(zero_f,))

    def load_vec(name, src_ap, nchunk):
        t = k.sb(name, [128, nchunk], F32)
        k.dma("sp", t[:], src_ap.rearrange("(c p) -> p c", p=128), w=(t,))
        return t

    nc_allow = nc.allow_non_contiguous_dma(reason="tiny parameter vectors")
    es.enter_context(nc_allow)

    n1w = [load_vec(f"n1w{l}", norm1_w[l], 8) for l in range(DEPTH)]
    n2w = [load_vec(f"n2w{l}", norm2_w[l], 8) for l in range(DEPTH)]
    fw = load_vec("fw", final_w, 8)
    dcw = []
    scw = []
    dnw = []
    nAexp = []
    dtb = []
    for l in range(DEPTH):
        t = k.sb(f"dcw{l}", [128, 5, 24], F32)
        k.dma("sp", t[:], dn_conv_w[l].rearrange("d (c p) -> p d c", p=128), w=(t,))
        dcw.append(t)
        t = k.sb(f"scw{l}", [128, 3, 8], F32)
        k.dma("sp", t[:], sc_conv_w[l].rearrange("d (c p) -> p d c", p=128), w=(t,))
        scw.append(t)
        t = k.sb(f"dnw{l}", [128, 1], F32)
        k.dma("sp", t[:], dn_norm_w[l].rearrange("(p o) -> p o", o=1), w=(t,))
        dnw.append(t)
        ta = k.sb(f"alog{l}", [16, 1], F32)
        k.dma("sp", ta[:], A_log[l].rearrange("(p o) -> p o", o=1), w=(ta,))
        tb = k.sb(f"dtb{l}", [16, 1], F32)
        k.dma("sp", tb[:], dt_bias[l].rearrange("(p o) -> p o", o=1), w=(tb,))
        dtb.append(tb)
        te = k.sb(f"nAexp{l}", [16, 1], F32)
        k.act(te[:], ta[:], AF.Exp, r=(ta,), w=(te,))
        k.ts("dve", te[:], te[:], -1.0, None, ALU.mult, r=(te,), w=(te,))
        nAexp.append(te)
    eps_t = k.sb("eps_t", [128, 1], F32)
    k.memset("dve", eps_t[:], RMS_EPS, w=(eps_t,))
    eps128_t = k.sb("eps128_t", [128, 1], F32)
    k.memset("dve", eps128_t[:], 128.0 * L2_EPS, w=(eps128_t,))
    one_t = k.sb("one_t", [128, 1], F32)
    k.memset("dve", one_t[:], 1.0, w=(one_t,))

    k.arena_init(196 * 1024)
    CW = 4096
    cf = [k.ar(f"castf{i}", [128, CW], F32) for i in range(3)]
    cb = [k.ar(f"castb{i}", [128, CW], BF16) for i in range(3)]
    cidx = [0]

    def cast_w(dst, src, rows, cols):
        for l in range(DEPTH):
            for r0 in range(0, rows, 128):
                for c0 in range(0, cols, CW):
                    cw = min(CW, cols - c0)
                    i = cidx[0] % 3
                    cidx[0] += 1
                    k.dma("sp", cf[i][:, 0:cw], src[l, r0:r0 + 128, c0:c0 + cw], w=(cf[i],))
                    eng = ("act", "dve", "pool")[i]
                    k.copy(eng, cb[i][:, 0:cw], cf[i][:, 0:cw], r=(cf[i],), w=(cb[i],))
                    k.dma(STQ, dst[l, r0:r0 + 128, c0:c0 + cw], cb[i][:, 0:cw], r=(cb[i],))
    cast_w(wb_in, w_in, D, WIN)
    cast_w(wb_bdn, w_bdn, D, D)
    cast_w(wb_bsc, w_bsc, D, D)
    cast_w(wb_out, w_out, D, D)
    cast_w(wb_gu, w_gu, D, 2 * DFF)
    cast_w(wb_down, w_down, DFF, D)
    k.barrier()

    PADF = cfg.PADF
    if PADF:
        for arr in (qT, kT, vT):
            k.dma("sp", arr.rearrange("(c p) n -> p c n", p=128)[:, :, 0:PADF],
                  zero_b[:, 0:8 * PADF].rearrange("p (c n) -> p c n", c=8), r=(zero_b,))
        k.dma("sp", gbT[:, 0:PADF], zero_f[0:32, 0:PADF], r=(zero_f,))
    xTv = xT.rearrange("(c p) n -> p c n", p=128)
    k.dma("sp", xTv[:, :, 0:2], zero_f[:, 0:16].rearrange("p (c n) -> p c n", c=8), r=(zero_f,))
    k.dma("sp", xTv[:, :, L + 2:L + 4], zero_f[:, 0:16].rearrange("p (c n) -> p c n", c=8), r=(zero_f,))

    PS = [k.ps(f"ps{i}", [128, 512], F32) for i in range(8)]
    psi = [0]

    def nextps():
        t = PS[psi[0] % 8]
        psi[0] += 1
        return t

    k.arena_reset()
    p0_in = [k.ar(f"p0in{i}", [128, 4, D], F32) for i in range(2)]
    p0_out = [k.ar(f"p0out{i}", [128, 8, 512], F32) for i in range(2)]
    it = 0
    for t0 in range(0, L, 512):
        n = min(512, L - t0)
        tin = p0_in[it % 2]
        tout = p0_out[it % 2]
        nb = (n + 127) // 128
        for b in range(nb):
            nn = min(128, n - b * 128)
            k.dma("sp", tin[0:nn, b, :], xin[t0 + b * 128:t0 + b * 128 + nn, :], w=(tin,))
        for c in range(8):
            pt = nextps()
            groups = []
            for b in range(nb):
                nn = min(128, n - b * 128)
                groups.append((pt[:, b * 128:b * 128 + nn], tin[0:nn, b, c * 128:(c + 1) * 128],
                               ident_f[0:nn, 0:nn], True))
            k.mm_multi(pt, groups, r=(tin, ident_f))
            k.copy("act" if c % 2 else "dve", tout[:, c, 0:n], pt[:, 0:n], r=(pt,), w=(tout,))
        k.dma("sp", xTv[:, :, 2 + t0:2 + t0 + n], tout[:, :, 0:n], r=(tout,))
        it += 1
    k.barrier()

    NCMAX = TW + 4

    def nxt(lst, ctr):
        t = lst[ctr[0] % len(lst)]
        ctr[0] += 1
        return t

    psfree = []

    def ps_alloc():
        return psfree.pop(0)

    def ps_free(t):
        psfree.append(t)

    def run_pipeline(tasks, depth, budget=7):
        psfree[:] = list(PS)
        live = []
        pending = list(tasks)
        pi = 0
        while True:
            while pi < len(pending) and len(live) < depth:
                tk = pending[pi]
                nb = getattr(tk, "nb", 0)
                if sum(n for _, n in live) + nb > budget:
                    break
                pi += 1
                g_ = tk()
                if g_ is not None:
                    try:
                        next(g_)
                        live.append((g_, nb))
                    except StopIteration:
                        pass
            if not live:
                if pi >= len(pending):
                    break
                continue
            for ent in list(live):
                try:
                    next(ent[0])
                except StopIteration:
                    live.remove(ent)

    def phaseA(l):
        k.arena_reset()
        xts = [k.ar(f"xt{i}", [128, 8, NCMAX], F32) for i in range(1)]
        hTs = [k.ar(f"hT{i}", [128, 8, NCMAX], BF16) for i in range(2)]
        sq = k.ar("sq", [128, 8, NCMAX], BF16)
        rstd = k.ar("rstd", [128, NCMAX], F32)
        wbuf = [k.ar(f"wbuf{i}", [128, 8, 1024], BF16) for i in range(4)]
        wsm = k.ar("wsm", [128, 8, 32], BF16)
        stg = [k.ar(f"stg{i}", [128, 8, TW], BF16) for i in range(3)]
        tmpA = [k.ar(f"tmpA{i}", [128, NCMAX], F32) for i in range(8)]
        ssq8 = [k.ar(f"ssq8{i}", [128, 8, TW], F32) for i in range(2)]
        tmpB = [k.ar(f"tmpB{i}", [128, NCMAX], BF16) for i in range(6)]

        def ta():
            return tmpA.pop(0)

        def tb():
            return tmpB.pop(0)
        gbs = k.ar("gbs", [16, 2, NCMAX], F32)
        print("arena phase A bytes", k.arena_off)
        wl = wb_in[l]
        fm = lambda arr: arr.rearrange("(c p) n -> p c n", p=128)
        BLK = [C_Q, C_K, C_V, C_Z, C_SC, C_SX, C_SB, C_GA, C_GB]
        nblk = len(BLK)
        wslot = {}
        gblk = [0]

        def t_wload(gi):
            def f():
                wt = wbuf[gi % 4]
                k.dma("sp", wt[:, :, :], wl[:, BLK[gi % nblk]:BLK[gi % nblk] + 1024].rearrange("(c p) n -> p c n", p=128),
                      w=(wt,))
                wslot[gi] = wt
            return f

        def t_xload(ti):
            def f():
                t0, W = cfg.tiles[ti]
                NC = W + 4
                k.dma("sp", xts[0][:, :, 0:NC], xTv[:, :, t0:t0 + NC], w=(xts[0],))
            return f

        def t_pro1(ti):
            def f():
                t0, W = cfg.tiles[ti]
                NC = W + 4
                xt = xts[0]
                k.act(sq[:, :, 0:NC], xt[:, :, 0:NC], AF.Square, r=(xt,), w=(sq,))
            return f

        def t_pro2(ti):
            def f():
                t0, W = cfg.tiles[ti]
                NC = W + 4
                xt, hT = xts[0], hTs[ti % 2]
                pt = ps_alloc()
                k.mm(pt, pt[:, 0:NC], [(ones_b[:], sq[:, c, 0:NC]) for c in range(8)], r=(sq, ones_b))
                k.rsqrt(rstd[:, 0:NC], pt[:, 0:NC], 1.0 / D, eps_t[:], r=(pt, eps_t), w=(rstd,))
                ps_free(pt)
                for c in range(8):
                    k.stt("dve", hT[:, c, 0:NC], xt[:, c, 0:NC], n1w[l][:, c:c + 1],
                          rstd[:, 0:NC], ALU.mult, ALU.mult, r=(xt, rstd, n1w[l]), w=(hT,))
            return f

        def proj(hT, NC, wt, j, M=128):
            pt = ps_alloc()
            k.mm(pt, pt[0:M, 0:NC], [(wt[:, c, j:j + M], hT[:, c, 0:NC]) for c in range(8)], r=(wt, hT))
            return pt

        class Grp:
            def __init__(self, st, dst, pcol, W, n=8, norm=None, ssq=None):
                self.st, self.dst, self.pcol, self.W, self.left = st, dst, pcol, W, n
                self.norm, self.ssq = norm, ssq

            def done(self):
                self.left -= 1
                if self.left == 0:
                    W = self.W
                    if self.norm is not None:
                        sq_ = self.ssq
                        if self.norm == 0:
                            k.rsqrt(sq_[:, :, 0:W], sq_[:, :, 0:W], 128.0, eps128_t[:], r=(sq_, eps128_t), w=(sq_,))
                        else:
                            k.rsqrt(sq_[:, :, 0:W], sq_[:, :, 0:W], 1.0, eps_t[:], r=(sq_, eps_t), w=(sq_,))
                        k.tt("dve", self.st[:, :, 0:W], self.st[:, :, 0:W], sq_[:, :, 0:W], ALU.mult,
                             r=(self.st, sq_), w=(self.st,))
                    k.dma(STQ, fm(self.dst)[:, :, self.pcol:self.pcol + W], self.st[:, :, 0:W],
                          r=(self.st,))

        def t_qkv(ti, gi, grp, c, G):
            def gen():
                t0, W = cfg.tiles[ti]
                NC = W + 4
                hT = hTs[ti % 2]
                wt = wslot[gi]
                st = G.st
                pt = proj(hT, NC, wt, c * 128)
                yield
                cc = grp * 8 + c
                acc = ta()
                k.ts("dve", acc[:, 0:W], pt[:, 0:W], dcw[l][:, 0, cc:cc + 1], None, ALU.mult,
                     r=(pt, dcw[l]), w=(acc,))
                for d in range(1, 5):
                    k.stt("dve", acc[:, 0:W], pt[:, d:d + W], dcw[l][:, d, cc:cc + 1], acc[:, 0:W],
                          ALU.mult, ALU.add, r=(pt, dcw[l], acc), w=(acc,))
                ps_free(pt)
                yield
                if grp == 2:
                    k.act(st[:, c, 0:W], acc[:, 0:W], AF.Silu, r=(acc,), w=(st,))
                    tmpA.append(acc)
                    G.done()
                    return
                k.act(st[:, c, 0:W], acc[:, 0:W], AF.Silu, r=(acc,), w=(st,))
                tmpA.append(acc)
                s2 = tb()
                k.act(s2[:, 0:W], st[:, c, 0:W], AF.Square, r=(st,), w=(s2,))
                yield
                p2 = ps_alloc()
                k.mm(p2, p2[:, 0:W], [(ones_b[:], s2[:, 0:W])], r=(s2, ones_b))
                tmpB.append(s2)
                yield
                k.copy("act", G.ssq[:, c, 0:W], p2[:, 0:W], r=(p2,), w=(G.ssq,))
                ps_free(p2)
                G.done()
            gen.nb = 1
            return gen

        def t_simple(ti, gi, c, G, func):
            def gen():
                t0, W = cfg.tiles[ti]
                NC = W + 4
                pt = proj(hTs[ti % 2], NC, wslot[gi], c * 128)
                yield
                k.act(G.st[:, c, 0:W], pt[:, 2:2 + W], func, r=(pt,), w=(G.st,))
                ps_free(pt)
                G.done()
            gen.nb = 1
            return gen

        def t_bg(ti):
            def gen():
                t0, W = cfg.tiles[ti]
                NC = W + 4
                pcol = t0 + PADF
                hT = hTs[ti % 2]
                k.dma("sp", wsm[:, :, :], wl[:, C_B:C_B + 32].rearrange("(c p) n -> p c n", p=128), w=(wsm,))
                pts = []
                for which in range(2):
                    pt = ps_alloc()
                    k.mm(pt, pt[0:16, 0:NC], [(wsm[:, c, which * 16:which * 16 + 16], hT[:, c, 0:NC])
                                              for c in range(8)], r=(wsm, hT))
                    pts.append(pt)
                yield
                k.act(gbs[:, 0, 0:W], pts[0][0:16, 2:2 + W], AF.Sigmoid, r=(pts[0],), w=(gbs,))
                k.act(gbs[:, 1, 0:W], pts[1][0:16, 2:2 + W], AF.Exp, r=(pts[1], dtb[l]), w=(gbs,), bias=dtb[l][:])
                ps_free(pts[0])
                ps_free(pts[1])
                k.act(gbs[:, 1, 0:W], gbs[:, 1, 0:W], AF.Ln, r=(gbs, one_t), w=(gbs,), bias=one_t[0:16, :])
                yield
                k.ts("dve", gbs[:, 1, 0:W], gbs[:, 1, 0:W], nAexp[l][:, 0:1], None, ALU.mult,
                     r=(gbs, nAexp[l]), w=(gbs,))
                k.dma(STQ, gbT.rearrange("(a p) n -> p a n", p=16)[:, :, pcol:pcol + W], gbs[:, :, 0:W], r=(gbs,))
            gen.nb = 2
            return gen

        def t_sc(ti, gi_c, gi_x, gi_b, c, G):
            def gen():
                t0, W = cfg.tiles[ti]
                NC = W + 4
                hT = hTs[ti % 2]
                pc = proj(hT, NC, wslot[gi_c], c * 128)
                px = proj(hT, NC, wslot[gi_x], c * 128)
                pb = proj(hT, NC, wslot[gi_b], c * 128)
                yield
                cx = ta()
                k.copy("act", cx[:, 0:NC], pc[:, 0:NC], r=(pc,), w=(cx,))
                ps_free(pc)
                yield
                pr = ta()
                k.tt("dve", pr[:, 0:NC], px[:, 0:NC], cx[:, 0:NC], ALU.mult, r=(px, cx), w=(pr,))
                ps_free(px)
                tmpA.append(cx)
                yield
                acc = ta()
                k.ts("dve", acc[:, 0:W], pr[:, 1:1 + W], scw[l][:, 0, c:c + 1], None, ALU.mult,
                     r=(pr, scw[l]), w=(acc,))
                for d in range(1, 3):
                    k.stt("dve", acc[:, 0:W], pr[:, 1 + d:1 + d + W], scw[l][:, d, c:c + 1], acc[:, 0:W],
                          ALU.mult, ALU.add, r=(pr, scw[l], acc), w=(acc,))
                tmpA.append(pr)
                yield
                k.tt("dve", G.st[:, c, 0:W], pb[:, 2:2 + W], acc[:, 0:W], ALU.mult, r=(pb, acc), w=(G.st,))
                ps_free(pb)
                tmpA.append(acc)
                G.done()
            gen.nb = 3
            return gen

        tasks = []
        ntile = len(cfg.tiles)
        stgc = [0]
        total_blocks = ntile * nblk
        tasks.append(t_xload(0))
        for gi in range(4):
            tasks.append(t_wload(gi))
        tasks.append(t_pro1(0))
        tasks.append(t_pro2(0))
        for ti, (t0, W) in enumerate(cfg.tiles):
            pcol = t0 + PADF
            base = ti * nblk

            def post(gi):
                if gi + 4 < total_blocks:
                    tasks.append(t_wload(gi + 4))

            def newG(dst, n=8, norm=None, ssq=None):
                st = stg[stgc[0] % 3]
                stgc[0] += 1
                return Grp(st, dst, pcol, W, n, norm, ssq)
            for grp, dst in enumerate((qT, kT, vT)):
                G = newG(dst, norm=(grp if grp < 2 else None), ssq=(ssq8[grp] if grp < 2 else None))
                for c in range(8):
                    tasks.append(t_qkv(ti, base + grp, grp, c, G))
                post(base + grp)
                if grp == 1 and ti + 1 < ntile:
                    tasks.append(t_xload(ti + 1))
            G = newG(zsT)
            for c in range(8):
                tasks.append(t_simple(ti, base + 3, c, G, AF.Silu))
            post(base + 3)
            tasks.append(t_bg(ti))
            G = newG(yscT)
            for c in range(8):
                tasks.append(t_sc(ti, base + 4, base + 5, base + 6, c, G))
            post(base + 4)
            post(base + 5)
            post(base + 6)
            if ti + 1 < ntile:
                tasks.append(t_pro1(ti + 1))
            for bi, dst in ((7, gaT), (8, gbgT)):
                G = newG(dst)
                for c in range(8):
                    tasks.append(t_simple(ti, base + bi, c, G, AF.Sigmoid))
                post(base + bi)
                if bi == 7 and ti + 1 < ntile:
                    tasks.append(t_pro2(ti + 1))
        run_pipeline(tasks, 5)
        k.barrier()

    def run_threads(gens):
        live = list(gens)
        while live:
            for g_ in list(live):
                try:
                    next(g_)
                except StopIteration:
                    live.remove(g_)

    def v3(ap, a):
        return ap.rearrange("p (a b) -> p a b", a=a)

    def phaseB(l):
        k.arena_reset()
        NCK = cfg.NCK
        qTv = qT.rearrange("(c p) n -> p c n", p=128)
        kTv = kT.rearrange("(c p) n -> p c n", p=128)
        vTv = vT.rearrange("(c p) n -> p c n", p=128)
        B = {}
        CDT = F32 if CHAIN_FP32 else BF16
        identc = ident_f if CHAIN_FP32 else ident_b
        for d in range(2):
            rGa_ = k.ar(f"rGa{d}", [128, 8, 128], F32)
            rGi_ = k.ar(f"rGi{d}", [128, 8, 128], F32)
            for sl in range(2):
                B[d, sl] = dict(
                    kq=k.ar(f"kq{d}{sl}", [128, 8, 2, 128], BF16),
                    vt=k.ar(f"vt{d}{sl}", [128, 8, 128], BF16),
                    gbt=k.ar(f"gbt{d}{sl}", [32, 128], F32),
                    gb=k.ar(f"gb{d}{sl}", [128, 32], F32),
                    E=k.ar(f"E{d}{sl}", [128, 24], F32),
                    bege=k.ar(f"bege{d}{sl}", [128, 8], F32),
                    nbeta=k.ar(f"nbeta{d}{sl}", [128, 8], F32),
                    ost=k.ar(f"ost{d}{sl}", [128, 8, 128], BF16),
                    rGa=rGa_, rGi=rGi_,
                )
                for hh in range(2):
                    B[d, sl, hh] = dict(
                        qkm=k.ar(f"qkm{d}{sl}{hh}", [128, 4, 128], BF16),
                        kdec=k.ar(f"kdec{d}{sl}{hh}", [128, 4, 128], BF16),
                        u=k.ar(f"u{d}{sl}{hh}", [128, 4, 128], F32),
                        wT=k.ar(f"wT{d}{sl}{hh}", [128, 4, 128], BF16),
                        qdT=k.ar(f"qdT{d}{sl}{hh}", [128, 4, 128], BF16),
                    )
            for hh in range(2):
                B["t", d, hh] = dict(
                    Dx=k.ar(f"Dx{d}{hh}", [128, 4, 128], F32),
                    DTx=k.ar(f"DTx{d}{hh}", [128, 4, 128], F32),
                    egcb=k.ar(f"egcb{d}{hh}", [128, 4, 128], F32),
                    U=[k.ar(f"U{d}{hh}{i}", [128, 4, 128], CDT) for i in range(1 if CHAIN_FP32 else 2)],
                    W=[k.ar(f"W{d}{hh}{i}", [128, 4, 128], CDT) for i in range(1 if CHAIN_FP32 else 2)],
                    P=[k.ar(f"P{d}{hh}{i}", [128, 4, 128], CDT) for i in range(1 if CHAIN_FP32 else 2)],
                    Pb=k.ar(f"Pb{d}{hh}", [128, 4, 128], BF16),
                    ktok=k.ar(f"ktok{d}{hh}", [128, 4, 128], BF16),
                    bkg=k.ar(f"bkg{d}{hh}", [128, 4, 128], BF16),
                    bv=k.ar(f"bv{d}{hh}", [128, 4, 128], BF16),
                    vnew=k.ar(f"vnew{d}{hh}", [128, 4, 128], BF16),
                    Ssc=k.ar(f"Ssc{d}{hh}", [128, 4, 128], F32),
                    S=k.ar(f"S{d}{hh}", [128, 4, 128], F32),
                    Sb=k.ar(f"Sb{d}{hh}", [128, 4, 128], BF16),
                )
                if CHAIN_FP32:
                    tt_ = B["t", d, hh]
                    tt_["U"].append(tt_["Dx"])
                    tt_["W"].append(tt_["DTx"])
                    tt_["P"].append(tt_["egcb"])
                k.memset("pool", B["t", d, hh]["S"][:], 0.0, w=(B["t", d, hh]["S"],))
                k.memset("pool", B["t", d, hh]["Sb"][:], 0.0, w=(B["t", d, hh]["Sb"],))
        print("arena phase B bytes", k.arena_off)

        def setup(d, c, sl):
            b = B[d, sl]
            Mincl, Maft = masks[:, 2 * d, :], masks[:, 2 * d + 1, :]
            cs = slice(c * 128, (c + 1) * 128)
            k.dma("sp", b["kq"][:, :, 0, :], kTv[:, :, cs], w=(b["kq"],))
            k.dma("sp", b["kq"][:, :, 1, :], qTv[:, :, cs], w=(b["kq"],))
            k.dma("sp", b["vt"][:], vTv[:, :, cs], w=(b["vt"],))
            k.dma("sp", b["gbt"][:], gbT[:, cs], w=(b["gbt"],))
            yield
            pt = nextps()
            k.mm_multi(pt, [(pt[:, 0:32], b["gbt"][:], ident_f[0:32, 0:32], True)], r=(b["gbt"], ident_f))
            k.copy("dve", b["gb"][:], pt[:, 0:32], r=(pt,), w=(b["gb"],))
            yield
            beta = b["gb"][:, d * 8:d * 8 + 8]
            g = b["gb"][:, 16 + d * 8:16 + d * 8 + 8]
            pt = nextps()
            k.mm_multi(pt, [(pt[:, 0:8], Mincl, g, False), (pt[:, 8:16], Maft, g, False),
                            (pt[:, 16:24], ones_f[:], g, False)], r=(masks, ones_f, b["gb"]))
            k.act(b["E"][:], pt[:, 0:24], AF.Exp, r=(pt,), w=(b["E"],))
            k.tt("pool", b["rGa"][:], bc(masks[:, 2 * d + 1:2 * d + 2, :], [128, 8, 128]),
                 bc(g.unsqueeze(2), [128, 8, 128]), ALU.mult, r=(masks, b["gb"]), w=(b["rGa"],))
            k.tt("pool", b["rGi"][:], bc(masks[:, 2 * d:2 * d + 1, :], [128, 8, 128]),
                 bc(g.unsqueeze(2), [128, 8, 128]), ALU.mult, r=(masks, b["gb"]), w=(b["rGi"],))
            yield
            k.tt("dve", b["bege"][:], beta, b["E"][:, 0:8], ALU.mult, r=(b["gb"], b["E"]), w=(b["bege"],))
            k.ts("dve", b["nbeta"][:], beta, -1.0, None, ALU.mult, r=(b["gb"],), w=(b["nbeta"],))
            yield

        def prep(d, c, sl, hh):
            b = B[d, sl]
            bh = B[d, sl, hh]
            t = B["t", d, hh]
            hs = slice(4 * hh, 4 * hh + 4)
            Mincl, Maft = masks[:, 2 * d, :], masks[:, 2 * d + 1, :]
            Mincl_bc = bc(masks[:, 2 * d:2 * d + 1, :], [128, 4, 128])
            Maft_bc = bc(masks[:, 2 * d + 1:2 * d + 2, :], [128, 4, 128])
            beta = b["gb"][:, d * 8 + 4 * hh:d * 8 + 4 * hh + 4]
            cr = (lambda a: a.bitcast(mybir.dt.float32r)) if (CHAIN_FP32 and CHAIN_R) else (lambda a: a)
            pD = nextps()
            k.mm(pD, pD[:], [(Mincl, b["rGa"][:, hs, :])], r=(masks, b["rGa"]))
            k.act(v3(t["Dx"][:].rearrange("p a b -> p (a b)"), 4), v3(pD[:], 4), AF.Exp, r=(pD,), w=(t["Dx"],))
            pDT = nextps()
            k.mm(pDT, pDT[:], [(Maft, b["rGi"][:, hs, :])], r=(masks, b["rGi"]))
            k.act(t["DTx"][:], v3(pDT[:], 4), AF.Exp, r=(pDT,), w=(t["DTx"],))
            yield
            pG = nextps()
            k.mm(pG, pG[:], [(ones_f[:], b["rGi"][:, hs, :])], r=(ones_f, b["rGi"]))
            k.act(t["egcb"][:], v3(pG[:], 4), AF.Exp, r=(pG,), w=(t["egcb"],))
            k.tt("pool", t["Dx"][:], t["Dx"][:], Maft_bc, ALU.mult, r=(t["Dx"], masks), w=(t["Dx"],))
            k.tt("pool", t["Dx"][:], t["Dx"][:], bc(b["nbeta"][:, hs].unsqueeze(2), [128, 4, 128]), ALU.mult,
                 r=(t["Dx"], b["nbeta"]), w=(t["Dx"],))
            k.tt("pool", t["DTx"][:], t["DTx"][:], Mincl_bc, ALU.mult, r=(t["DTx"], masks), w=(t["DTx"],))
            yield
            W0 = t["W"][0]
            for pair in range(2):
                pk = nextps()
                groups = []
                for hl in range(2):
                    h = 4 * hh + 2 * pair + hl
                    groups.append((pk[:, hl * 256:(hl + 1) * 256], b["kq"][:, h, 0, :],
                                   b["kq"][:, h, :, :].rearrange("p a b -> p (a b)"), False))
                k.mm_multi(pk, groups, r=(b["kq"],))
                pkv = v3(pk[:], 2)
                k.tt("dve", cr(W0[:, 2 * pair:2 * pair + 2, :]), pkv[:, :, 0:128], t["Dx"][:, 2 * pair:2 * pair + 2, :],
                     ALU.mult, r=(pk, t["Dx"]), w=(W0,))
                k.tt("dve", bh["qkm"][:, 2 * pair:2 * pair + 2, :], pkv[:, :, 128:256],
                     t["DTx"][:, 2 * pair:2 * pair + 2, :], ALU.mult, r=(pk, t["DTx"]), w=(bh["qkm"],))
            yield
            U0 = t["U"][0]
            pt = nextps()
            ptb = v3(pt[:], 4) if CHAIN_FP32 else v3(pt[:].bitcast(BF16)[:, 0:512], 4)
            k.mm_multi(pt, [(ptb[:, hl, :], W0[:, hl, :], identc[:], True) for hl in range(4)], r=(W0, identc))
            k.copy("dve" if CHAIN_R else "act", cr(U0[:]), ptb, r=(pt,), w=(U0,))
            pt = nextps()
            ptb = v3(pt[:].bitcast(BF16)[:, 0:512], 4)
            k.mm_multi(pt, [(ptb[:, hl, :], b["kq"][:, 4 * hh + hl, 0, :], ident_b[:], True) for hl in range(4)],
                       r=(b["kq"], ident_b))
            k.copy("act", t["ktok"][:], ptb, r=(pt,), w=(t["ktok"],))
            pt = nextps()
            ptb = v3(pt[:].bitcast(BF16)[:, 0:512], 4)
            k.mm_multi(pt, [(ptb[:, hl, :], b["vt"][:, 4 * hh + hl, :], ident_b[:], True) for hl in range(4)],
                       r=(b["vt"], ident_b))
            k.tt("dve", t["bv"][:], ptb, bc(beta.unsqueeze(2), [128, 4, 128]), ALU.mult, r=(pt, b["gb"]),
                 w=(t["bv"],))
            yield
            k.tt("pool", t["bkg"][:], t["ktok"][:], bc(b["bege"][:, hs].unsqueeze(2), [128, 4, 128]), ALU.mult,
                 r=(t["ktok"], b["bege"]), w=(t["bkg"],))
            k.tt("pool", bh["kdec"][:], t["ktok"][:], bc(b["E"][:, 8 + 4 * hh:12 + 4 * hh].unsqueeze(2), [128, 4, 128]),
                 ALU.mult, r=(t["ktok"], b["E"]), w=(bh["kdec"],))
            k.tt("pool", bh["qdT"][:], b["kq"][:, hs, 1, :], t["egcb"][:], ALU.mult, r=(b["kq"], t["egcb"]),
                 w=(bh["qdT"],))
            k.tt("dve", cr(t["P"][0][:]), U0[:], bc(identc[:].unsqueeze(1), [128, 4, 128]), ALU.add,
                 r=(U0, identc), w=(t["P"][0],))
            yield
            for lev in range(6):
                Uc, Wc = t["U"][lev % 2], t["W"][lev % 2]
                Un, Wn = t["U"][(lev + 1) % 2], t["W"][(lev + 1) % 2]
                Pc, Pn = t["P"][lev % 2], t["P"][(lev + 1) % 2]
                pB = nextps()
                k.mm_multi(pB, [(pB[:, hl * 128:(hl + 1) * 128], cr(Uc[:, hl, :]), cr(Wc[:, hl, :]), False) for hl in range(4)],
                           r=(Wc, Uc))
                k.copy("dve", cr(Wn[:]), v3(pB[:], 4), r=(pB,), w=(Wn,))
                yield
                if lev < 5:
                    pA = nextps()
                    if CHAIN_FP32:
                        pav = v3(pA[:], 4)
                    else:
                        pav = v3(pA[:].bitcast(BF16)[:, 0:512], 4)
                    k.mm_multi(pA, [(pav[:, hl, :], Wn[:, hl, :], identc[:], True) for hl in range(4)],
                               r=(Wn, identc))
                    k.copy("act", cr(Un[:]), pav, r=(pA,), w=(Un,))
                pC = nextps()
                k.mm_multi(pC, [(pC[:, hl * 128:(hl + 1) * 128], cr(Wn[:, hl, :]), cr(Pc[:, hl, :]), False) for hl in range(4)],
                           r=(Wn, Pc))
                k.tt("dve", cr(Pn[:]), v3(pC[:], 4), Pc[:], ALU.add, r=(pC, Pc), w=(Pn,))
                yield
            Pf = t["P"][0]
            if CHAIN_FP32:
                k.copy("pool", t["Pb"][:], Pf[:], r=(Pf,), w=(t["Pb"],))
                Pf = t["Pb"]
            pu = nextps()
            k.mm_multi(pu, [(pu[:, hl * 128:(hl + 1) * 128], Pf[:, hl, :], t["bv"][:, hl, :], False) for hl in range(4)],
                       r=(Pf, t["bv"]))
            k.copy("act", bh["u"][:], v3(pu[:], 4), r=(pu,), w=(bh["u"],))
            pw = nextps()
            k.mm_multi(pw, [(pw[:, hl * 128:(hl + 1) * 128], t["bkg"][:, hl, :], Pf[:, hl, :], False) for hl in range(4)],
                       r=(Pf, t["bkg"]))
            k.copy("dve", bh["wT"][:], v3(pw[:], 4), r=(pw,), w=(bh["wT"],))
            yield

        def scan(d, c, sl, hh):
            b = B[d, sl]
            bh = B[d, sl, hh]
            t = B["t", d, hh]
            S, Sb = t["S"], t["Sb"]
            pws = nextps()
            k.mm_multi(pws, [(pws[:, hl * 128:(hl + 1) * 128], bh["wT"][:, hl, :], Sb[:, hl, :], False)
                             for hl in range(4)], r=(bh["wT"], Sb))
            k.tt("dve", t["vnew"][:], bh["u"][:], v3(pws[:], 4), ALU.subtract, r=(bh["u"], pws), w=(t["vnew"],))
            k.tt("pool", t["Ssc"][:], S[:], bc(b["E"][:, 16 + 4 * hh:20 + 4 * hh].unsqueeze(2), [128, 4, 128]),
                 ALU.mult, r=(S, b["E"]), w=(t["Ssc"],))
            yield
            po = nextps()

            def fn(g, po=po, Sb=Sb, bh=bh, t=t):
                ins = None
                for hl in range(4):
                    o_ = po[:, hl * 128:(hl + 1) * 128]
                    g.matmul(o_, lhsT=Sb[:, hl, :], rhs=bh["qdT"][:, hl, :], start=True, stop=False)
                    ins = g.matmul(o_, lhsT=t["vnew"][:, hl, :], rhs=bh["qkm"][:, hl, :], start=False, stop=True)
                return ins
            k.op("pe", fn, r=(Sb, bh["qdT"], t["vnew"], bh["qkm"]), w=(po,))
            k.copy("act", b["ost"][:, 4 * hh:4 * hh + 4, :], v3(po[:], 4), r=(po,), w=(b["ost"],))
            pds = nextps()
            k.mm_multi(pds, [(pds[:, hl * 128:(hl + 1) * 128], bh["kdec"][:, hl, :], t["vnew"][:, hl, :], False)
                             for hl in range(4)], r=(bh["kdec"], t["vnew"]))
            k.tt("dve", S[:], t["Ssc"][:], v3(pds[:], 4), ALU.add, r=(t["Ssc"], pds), w=(S,))
            k.copy("act", Sb[:], S[:], r=(S,), w=(Sb,))
            yield

        def store(d, c, sl):
            b = B[d, sl]
            k.dma(STQ, oT[d].rearrange("(h p) n -> p h n", p=128)[:, :, c * 128:(c + 1) * 128], b["ost"][:],
                  r=(b["ost"],))

        def chunk_of(d, s):
            return s if d == 0 else NCK - 1 - s

        run_threads([setup(d, chunk_of(d, 0), 0) for d in range(2)])
        run_threads([prep(d, chunk_of(d, 0), 0, hh) for d in range(2) for hh in range(2)])
        for s in range(NCK):
            sl = s % 2
            th = []
            if s + 1 < NCK:
                run_threads([setup(d, chunk_of(d, s + 1), 1 - sl) for d in range(2)])
                th += [prep(d, chunk_of(d, s + 1), 1 - sl, hh) for d in range(2) for hh in range(2)]
            th += [scan(d, chunk_of(d, s), sl, hh) for d in range(2) for hh in range(2)]
            run_threads(th)
            for d in range(2):
                store(d, chunk_of(d, s), sl)
        k.barrier()


    def phaseC(l, last):
        k.arena_reset()
        x = k.ar("Cx", [128, 8, TW], F32)
        osum = k.ar("Cosum", [128, 8, TW], F32)
        b_of = k.ar("Cof", [128, 8, TW], BF16)
        b_ob = k.ar("Cob", [128, 8, TW], BF16)
        b_zs = k.ar("Czs", [128, 8, TW], BF16)
        b_ysc = k.ar("Cysc", [128, 8, TW], BF16)
        b_ga = k.ar("Cga", [128, 8, TW], BF16)
        b_gb = k.ar("Cgb", [128, 8, TW], BF16)
        a_t = k.ar("Ca", [128, 22, TW], BF16)
        ssq8 = a_t.ap.rearrange("p a b -> p (a b)")[:, 0:16 * TW].bitcast(F32).rearrange("p (a b) -> p a b", a=8)
        wb = [k.ar(f"Cw{i}", [128, 8, 1024], BF16) for i in range(4)]
        wi = [0]
        tA = [k.ar(f"CtA{i}", [128, TW], F32) for i in range(6)]
        tAi = [0]
        tB = [k.ar(f"CtB{i}", [128, TW], BF16) for i in range(2)]
        tBi = [0]
        rs_t = k.ar("Crstd", [128, TW], F32)
        otile = [k.ar(f"Cot{i}", [128, 1024], F32) for i in range(2)]
        oti = [0]
        print("arena phase C bytes", k.arena_off)
        odn, sq2, mg, h2 = b_of, b_ob, b_zs, b_of
        fm = lambda arr: arr.rearrange("(c p) n -> p c n", p=128)

        def wload(src2, rows0, cols0, ncols, nk=8, dst=None, dcol=0):
            wt = dst if dst is not None else nxt(wb, wi)
            k.dma("sp", wt[:, 0:nk, dcol:dcol + ncols],
                  src2[rows0:rows0 + nk * 128, cols0:cols0 + ncols].rearrange("(c p) n -> p c n", p=128), w=(wt,))
            return wt

        for (t0, W) in cfg.tiles:
            pcol = t0 + PADF
            for buf, arr in ((b_ob, oT[1]), (b_of, oT[0]), (b_zs, zsT), (b_ysc, yscT), (b_ga, gaT), (b_gb, gbgT)):
                k.dma("sp", buf[:, :, 0:W], fm(arr)[:, :, pcol:pcol + W], w=(buf,))
            k.dma("sp", x[:, :, 0:W], xTv[:, :, 2 + t0:2 + t0 + W], w=(x,))
            k.tt("dve", osum[:, :, 0:W], b_of[:, :, 0:W], b_ob[:, :, 0:W], ALU.add, r=(b_of, b_ob), w=(osum,))
            k.act(sq2[:, :, 0:W], osum[:, :, 0:W], AF.Square, r=(osum,), w=(sq2,))
            for c in range(8):
                p2 = nextps()
                k.mm(p2, p2[:, 0:W], [(ones_b[:], sq2[:, c, 0:W])], r=(sq2, ones_b))
                k.copy("act", ssq8[:, c, 0:W], p2[:, 0:W], r=(p2,), w=(a_t,))
            k.rsqrt(ssq8[:, :, 0:W], ssq8[:, :, 0:W], 1.0 / 128, eps_t[:], r=(a_t, eps_t), w=(a_t,))
            k.stt("dve", osum[:, :, 0:W], osum[:, :, 0:W], dnw[l][:, 0:1], ssq8[:, :, 0:W], ALU.mult, ALU.mult,
                  r=(osum, dnw[l], a_t), w=(osum,))
            k.tt("dve", odn[:, :, 0:W], osum[:, :, 0:W], b_zs[:, :, 0:W], ALU.mult, r=(osum, b_zs), w=(odn,))
            wdn = wload(wb_bdn[l], 0, 0, 1024)
            wsc = wload(wb_bsc[l], 0, 0, 1024)
            for m in range(8):
                pa = nextps()
                k.mm(pa, pa[:, 0:W], [(wdn[:, c, m * 128:(m + 1) * 128], odn[:, c, 0:W]) for c in range(8)],
                     r=(wdn, odn))
                pb = nextps()
                k.mm(pb, pb[:, 0:W], [(wsc[:, c, m * 128:(m + 1) * 128], b_ysc[:, c, 0:W]) for c in range(8)],
                     r=(wsc, b_ysc))
                t1 = nxt(tA, tAi)
                k.tt("dve", t1[:, 0:W], pa[:, 0:W], b_ga[:, m, 0:W], ALU.mult, r=(pa, b_ga), w=(t1,))
                t2 = nxt(tA, tAi)
                k.tt("dve", t2[:, 0:W], pb[:, 0:W], b_gb[:, m, 0:W], ALU.mult, r=(pb, b_gb), w=(t2,))
                k.tt("dve", mg[:, m, 0:W], t1[:, 0:W], t2[:, 0:W], ALU.add, r=(t1, t2), w=(mg,))
            wo = wload(wb_out[l], 0, 0, 1024)
            for m in range(8):
                pm = nextps()
                k.mm(pm, pm[:, 0:W], [(wo[:, c, m * 128:(m + 1) * 128], mg[:, c, 0:W]) for c in range(8)],
                     r=(wo, mg))
                k.tt("dve", x[:, m, 0:W], x[:, m, 0:W], pm[:, 0:W], ALU.add, r=(x, pm), w=(x,))
                k.act(sq2[:, m, 0:W], x[:, m, 0:W], AF.Square, r=(x,), w=(sq2,))
            pt = nextps()
            k.mm(pt, pt[:, 0:W], [(ones_b[:], sq2[:, c, 0:W]) for c in range(8)], r=(sq2, ones_b))
            k.rsqrt(rs_t[:, 0:W], pt[:, 0:W], 1.0 / D, eps_t[:], r=(pt, eps_t), w=(rs_t,))
            for c in range(8):
                k.stt("dve", h2[:, c, 0:W], x[:, c, 0:W], n2w[l][:, c:c + 1], rs_t[:, 0:W], ALU.mult, ALU.mult,
                      r=(x, rs_t, n2w[l]), w=(h2,))
            for j0 in range(0, NFF, 4):
                nj = min(4, NFF - j0)
                wt = nxt(wb, wi)
                wload(wb_gu[l], 0, j0 * 128, nj * 128, dst=wt, dcol=0)
                wload(wb_gu[l], 0, DFF + j0 * 128, nj * 128, dst=wt, dcol=512)
                for jj in range(nj):
                    j = j0 + jj
                    pg = nextps()
                    k.mm(pg, pg[:, 0:W], [(wt[:, c, jj * 128:(jj + 1) * 128], h2[:, c, 0:W]) for c in range(8)],
                         r=(wt, h2))
                    pu = nextps()
                    k.mm(pu, pu[:, 0:W], [(wt[:, c, 512 + jj * 128:512 + (jj + 1) * 128], h2[:, c, 0:W])
                                          for c in range(8)], r=(wt, h2))
                    sg = nxt(tA, tAi)
                    k.act(sg[:, 0:W], pg[:, 0:W], AF.Silu, r=(pg,), w=(sg,))
                    k.tt("dve", a_t[:, j, 0:W], sg[:, 0:W], pu[:, 0:W], ALU.mult, r=(sg, pu), w=(a_t,))
            wd = [wload(wb_down[l], kb * 1024, 0, 1024, nk=min(8, NFF - kb * 8)) for kb in range(3)]
            for m in range(8):
                pd = nextps()
                k.mm(pd, pd[:, 0:W], [(wd[j // 8][:, j % 8, m * 128:(m + 1) * 128], a_t[:, j, 0:W])
                                      for j in range(NFF)], r=(wd[0], wd[1], wd[2], a_t))
                k.tt("dve", x[:, m, 0:W], x[:, m, 0:W], pd[:, 0:W], ALU.add, r=(x, pd), w=(x,))
            if not last:
                k.dma(STQ, xTv[:, :, 2 + t0:2 + t0 + W], x[:, :, 0:W], r=(x,))
            else:
                k.act(sq2[:, :, 0:W], x[:, :, 0:W], AF.Square, r=(x,), w=(sq2,))
                pt = nextps()
                k.mm(pt, pt[:, 0:W], [(ones_b[:], sq2[:, c, 0:W]) for c in range(8)], r=(sq2, ones_b))
                k.rsqrt(rs_t[:, 0:W], pt[:, 0:W], 1.0 / D, eps_t[:], r=(pt, eps_t), w=(rs_t,))
                xn = osum
                for c in range(8):
                    k.stt("dve", xn[:, c, 0:W], x[:, c, 0:W], fw[:, c:c + 1], rs_t[:, 0:W], ALU.mult, ALU.mult,
                          r=(x, rs_t, fw), w=(xn,))
                lo = max(t0, NMETA)
                while lo < t0 + W:
                    nn = min(128, t0 + W - lo)
                    ot = nxt(otile, oti)
                    for half in range(2):
                        pt = nextps()
                        k.mm_multi(pt, [(pt[0:nn, cc * 128:(cc + 1) * 128],
                                         xn[:, half * 4 + cc, lo - t0:lo - t0 + nn], ident_f[:], True)
                                        for cc in range(4)], r=(xn, ident_f))
                        k.copy("act" if half else "dve", ot[0:nn, half * 512:(half + 1) * 512], pt[0:nn, :],
                               r=(pt,), w=(ot,))
                    k.dma(STQ, out[lo - NMETA:lo - NMETA + nn, :], ot[0:nn, :], r=(ot,))
                    lo += nn
        k.barrier()

    phases = cfg.__dict__.get("phases", "ABC")
    nl = cfg.__dict__.get("nlayers", DEPTH)
    for l in range(nl):
        phaseA(l)
        if "B" in phases:
            phaseB(l)
        if "C" in phases:
            phaseC(l, l == nl - 1)

    k.barrier()
    with nc.Block() as block:
        @block.tensor
        def _(g):
            k.replay("pe", g)

        @block.scalar
        def _(g):
            k.replay("act", g)

        @block.vector
        def _(g):
            k.replay("dve", g)

        @block.gpsimd
        def _(g):
            k.replay("pool", g)

        @block.sync
        def _(g):
            k.replay("sp", g)
    es.close()
    return nc


def make_masks():
    m = np.zeros((8, 128, 128), np.float32)
    t = np.arange(128)[:, None]
    i = np.arange(128)[None, :]
    m[0] = (t <= i)
    m[1] = (t > i)
    m[2] = (t >= i)
    m[3] = (t < i)
    return m


def core_inputs(cfg, inputs, b):
    f = np.float32
    xin = np.concatenate([inputs["meta_tokens"].astype(f), inputs["x"][b].astype(f)], axis=0)
    d = dict(
        xin=np.ascontiguousarray(xin),
        norm1_w=inputs["norm1_w"], w_in=inputs["w_in"], dn_conv_w=inputs["dn_conv_w"],
        A_log=inputs["A_log"].reshape(cfg.depth, 16), dt_bias=inputs["dt_bias"].reshape(cfg.depth, 16),
        dn_norm_w=inputs["dn_norm_w"], sc_conv_w=inputs["sc_conv_w"],
        w_branch_dn=inputs["w_branch_dn"], w_branch_sc=inputs["w_branch_sc"], w_out=inputs["w_out"],
        norm2_w=inputs["norm2_w"], w_gate_up=inputs["w_gate_up"], w_down=inputs["w_down"],
        final_norm_w=inputs["final_norm_w"],
        c_ident_f=np.eye(128, dtype=f), c_masks=make_masks())
    return {k_: np.ascontiguousarray(np.asarray(v, dtype=f)) for k_, v in d.items()}


_NC_CACHE = {}


def kernel(**inputs):
    x = inputs["x"]
    bsz, seq, _ = x.shape
    cfg = Cfg(seq, 2)
    key = (seq,)
    if key not in _NC_CACHE:
        _NC_CACHE[key] = build(cfg)
    nc = _NC_CACHE[key]
    in_maps = [core_inputs(cfg, inputs, b) for b in range(bsz)]
    res = run_bass_kernel_spmd(nc, in_maps, core_ids=list(range(bsz)))
    return np.stack([np.asarray(r["out"], dtype=np.float32) for r in res.results], axis=0)
```

```python
import numpy as np
import ml_dtypes
from contextlib import ExitStack
import concourse.bass as bass
import concourse.mybir as mybir
from concourse.bass_utils import run_bass_kernel_spmd

F32 = mybir.dt.float32
BF16 = mybir.dt.bfloat16
AF = mybir.ActivationFunctionType
ALU = mybir.AluOpType

D = 1024
NCH = 8
H = 8
NMETA = 16
DFF = 2816
NFF = 22
WIN = 9248
RMS_EPS = 1e-6
L2_EPS = 1e-6
TW = 508
SEM_ROLL = 30000
CHAIN_FP32 = True
CHAIN_R = False
P_BF16 = False
STQ = "sp"

C_Q, C_K, C_V, C_Z, C_B, C_A, C_SB, C_SC, C_SX, C_GA, C_GB = (
    0, 1024, 2048, 3072, 4096, 4112, 4128, 5152, 6176, 7200, 8224)


class Tl:
    def __init__(self, name, ap):
        self.name = name
        self.ap = ap
        self.w = None
        self.r = {}

    def __getitem__(self, idx):
        return self.ap[idx]


class Eng:
    def __init__(self, name, sems):
        self.name = name
        self.sems = sems
        self.si = 0
        self.cnt = 0
        self.ops = []
        self.waited = {}


class KB:
    def __init__(self, nc, es):
        self.nc = nc
        self.es = es
        self.semh = []
        self.eng = {}
        for n in ("pe", "act", "dve", "pool", "sp"):
            ids = [self._newsem(f"s_{n}{i}") for i in range(3)]
            self.eng[n] = Eng(n, ids)
        self.dq = {}
        for q, cnt in (("sp", 14), ("pool", 2), ("act", 4)):
            self.dq[q] = dict(ids=[self._newsem(f"d_{q}{i}") for i in range(cnt)],
                              cnt=[0] * cnt, nxt=0)
        self.all_tokens = {}
        self.ntile = 0

    def _newsem(self, name):
        h = self.es.enter_context(self.nc.semaphore(name))
        self.semh.append(h)
        return len(self.semh) - 1

    def sb(self, name, shape, dt):
        t = self.es.enter_context(self.nc.sbuf_tensor(name, list(shape), dt))
        return Tl(name, t)

    def arena_init(self, nbytes):
        self.arena = self.es.enter_context(self.nc.sbuf_tensor("arena", [128, nbytes // 2], BF16))
        self.arena_size = nbytes
        self.arena_off = 0

    def arena_reset(self):
        self.arena_off = 0

    def ar(self, name, shape, dt):
        esz = 4 if dt == F32 else 2
        n = 1
        for d_ in shape[1:]:
            n *= d_
        nb = (n * esz + 31) // 32 * 32
        off = self.arena_off
        assert off + nb <= self.arena_size, f"arena overflow at {name}: {off + nb}"
        self.arena_off += nb
        ap = self.arena[0:shape[0], off // 2:(off + n * esz) // 2]
        if dt == F32:
            ap = ap.bitcast(F32)
        if len(shape) == 3:
            ap = ap.rearrange("p (a b) -> p a b", a=shape[1])
        elif len(shape) == 4:
            ap = ap.rearrange("p (a b c) -> p a b c", a=shape[1], b=shape[2])
        return Tl(name, ap)

    def ps(self, name, shape, dt=F32):
        t = self.es.enter_context(self.nc.psum_tensor(name, list(shape), dt))
        return Tl(name, t)

    def _deps(self, e, r, w, extra=()):
        waits = {}

        def add(tok):
            if tok is None:
                return
            s, v = tok
            if waits.get(s, 0) < v:
                waits[s] = v
        for t in r:
            add(t.w)
        for t in w:
            add(t.w)
            for s, v in t.r.items():
                add((s, v))
        for tok in extra:
            add(tok)
        need = []
        for s, v in waits.items():
            if e.waited.get(s, 0) < v:
                e.waited[s] = v
                need.append((s, v))
        return need

    def _mark(self, tok, r, w):
        s, v = tok
        for t in r:
            if t.r.get(s, 0) < v:
                t.r[s] = v
        for t in w:
            t.w = tok
            t.r = {}
        if self.all_tokens.get(s, 0) < v:
            self.all_tokens[s] = v

    def op(self, engine, fn, r=(), w=(), extra=()):
        e = self.eng[engine]
        need = self._deps(e, r, w, extra)
        if e.cnt >= SEM_ROLL:
            e.si += 1
            e.cnt = 0
        e.cnt += 1
        sid = e.sems[e.si]
        tok = (sid, e.cnt)
        e.ops.append((need, fn, sid, 1))
        self._mark(tok, r, w)
        return tok

    def dma(self, queue, out, in_, r=(), w=(), extra=()):
        e = self.eng[queue]
        dq = self.dq[queue]
        i = dq["nxt"]
        dq["nxt"] = (i + 1) % len(dq["ids"])
        sid = dq["ids"][i]
        prev = (sid, dq["cnt"][i]) if dq["cnt"][i] else None
        need = self._deps(e, r, w, tuple(extra) + ((prev,) if prev else ()))
        dq["cnt"][i] += 16
        tok = (sid, dq["cnt"][i])
        e.ops.append((need, lambda g, o=out, s=in_: g.dma_start(out=o, in_=s), sid, 16))
        self._mark(tok, r, w)
        return tok

    def barrier(self):
        toks = list(self.all_tokens.items())
        for e in self.eng.values():
            need = []
            for s, v in toks:
                if e.waited.get(s, 0) < v:
                    e.waited[s] = v
                    need.append((s, v))
            if need:
                e.ops.append((need, None, None, 0))

    def replay(self, engine, g):
        for need, fn, sid, inc in self.eng[engine].ops:
            for s, v in need:
                g.wait_ge(self.semh[s], v)
            if fn is None:
                continue
            ins = fn(g)
            if inc:
                ins.then_inc(self.semh[sid], inc)

    def mm(self, out_t, out_ap, pairs, r=(), transpose=False):
        n = len(pairs)

        def fn(g, out_ap=out_ap, pairs=pairs, n=n):
            ins = None
            for i, (a, b) in enumerate(pairs):
                ins = g.matmul(out_ap, lhsT=a, rhs=b, start=(i == 0), stop=(i == n - 1))
            return ins
        return self.op("pe", fn, r=r, w=(out_t,))

    def mm_multi(self, out_t, groups, r=()):
        def fn(g, groups=groups):
            ins = None
            for (o, a, b, tr) in groups:
                if tr:
                    ins = g.transpose(o, a, b)
                else:
                    ins = g.matmul(o, lhsT=a, rhs=b, start=True, stop=True)
            return ins
        return self.op("pe", fn, r=r, w=(out_t,))

    def act(self, out, in_, func, r=(), w=(), bias=None, scale=None, eng="act"):
        kw = {}
        if bias is not None:
            kw["bias"] = bias
        if scale is not None:
            kw["scale"] = scale
        return self.op(eng, lambda g, o=out, i=in_, f=func, kw=kw: g.activation(out=o, in_=i, func=f, **kw),
                       r=r, w=w)

    def tt(self, eng, out, in0, in1, op, r=(), w=()):
        return self.op(eng, lambda g, o=out, a=in0, b=in1, p=op: g.tensor_tensor(out=o, in0=a, in1=b, op=p),
                       r=r, w=w)

    def stt(self, eng, out, in0, scalar, in1, op0, op1, r=(), w=()):
        return self.op(eng, lambda g, o=out, a=in0, s=scalar, b=in1, p0=op0, p1=op1:
                       g.scalar_tensor_tensor(out=o, in0=a, scalar=s, in1=b, op0=p0, op1=p1), r=r, w=w)

    def ts(self, eng, out, in0, s1, s2, op0, op1=None, r=(), w=()):
        if op1 is None:
            return self.op(eng, lambda g, o=out, a=in0, s=s1, p0=op0:
                           g.tensor_scalar(out=o, in0=a, scalar1=s, scalar2=None, op0=p0), r=r, w=w)
        return self.op(eng, lambda g, o=out, a=in0, x=s1, y=s2, p0=op0, p1=op1:
                       g.tensor_scalar(out=o, in0=a, scalar1=x, scalar2=y, op0=p0, op1=p1), r=r, w=w)

    def copy(self, eng, out, in_, r=(), w=()):
        if eng == "act":
            return self.op(eng, lambda g, o=out, i=in_: g.copy(out=o, in_=i), r=r, w=w)
        return self.op(eng, lambda g, o=out, i=in_: g.tensor_copy(out=o, in_=i), r=r, w=w)

    def rsqrt(self, out, in_, scale, bias_ap, r=(), w=()):
        self.act(out, in_, AF.Ln, r=r, w=w, bias=bias_ap, scale=scale)
        return self.act(out, out, AF.Exp, r=w, w=w, scale=-0.5)

    def recip(self, out, in_, r=(), w=()):
        return self.op("dve", lambda g, o=out, i=in_: g.reciprocal(out=o, in_=i), r=r, w=w)

    def memset(self, eng, ap, val, w=()):
        return self.op(eng, lambda g, a=ap, v=val: g.memset(a, v), w=w)


def bc(ap, shape):
    return ap.to_broadcast(list(shape))


class Cfg:
    def __init__(self, seq, depth):
        self.seq = seq
        self.depth = depth
        self.L = seq + NMETA
        self.PADF = (-self.L) % 128
        self.T = self.L + self.PADF
        self.NCK = self.T // 128
        self.XOFF = 2
        self.XW = self.L + 4
        self.tiles = []
        t0 = 0
        while t0 < self.L:
            w = min(TW, self.L - t0)
            self.tiles.append((t0, w))
            t0 += w


def build(cfg, debug=False):
    nc = bass.Bass("TRN2", target_bir_lowering=False)
    L, T, DEPTH = cfg.L, cfg.T, cfg.depth
    es = ExitStack()
    k = KB(nc, es)

    def din(name, shape, dt=F32):
        return nc.dram_tensor(name, list(shape), dt, kind="ExternalInput").ap()

    def dscr(name, shape, dt):
        kind = "ExternalOutput" if debug else "Internal"
        return nc.dram_tensor(name, list(shape), dt, kind=kind).ap()

    xin = din("xin", [L, D])
    norm1_w = din("norm1_w", [DEPTH, D])
    w_in = din("w_in", [DEPTH, D, WIN])
    dn_conv_w = din("dn_conv_w", [DEPTH, 5, 3072])
    A_log = din("A_log", [DEPTH, 16])
    dt_bias = din("dt_bias", [DEPTH, 16])
    dn_norm_w = din("dn_norm_w", [DEPTH, 128])
    sc_conv_w = din("sc_conv_w", [DEPTH, 3, 1024])
    w_bdn = din("w_branch_dn", [DEPTH, D, D])
    w_bsc = din("w_branch_sc", [DEPTH, D, D])
    w_out = din("w_out", [DEPTH, D, D])
    norm2_w = din("norm2_w", [DEPTH, D])
    w_gu = din("w_gate_up", [DEPTH, D, 2 * DFF])
    w_down = din("w_down", [DEPTH, DFF, D])
    final_w = din("final_norm_w", [D])
    c_ident_f = din("c_ident_f", [128, 128])
    c_masks = din("c_masks", [8, 128, 128])
    out = nc.dram_tensor("out", [cfg.seq, D], F32, kind="ExternalOutput").ap()

    wb_in = dscr("wb_in", [DEPTH, D, WIN], BF16)
    wb_bdn = dscr("wb_bdn", [DEPTH, D, D], BF16)
    wb_bsc = dscr("wb_bsc", [DEPTH, D, D], BF16)
    wb_out = dscr("wb_out", [DEPTH, D, D], BF16)
    wb_gu = dscr("wb_gu", [DEPTH, D, 2 * DFF], BF16)
    wb_down = dscr("wb_down", [DEPTH, DFF, D], BF16)
    xT = dscr("xT", [D, cfg.XW], F32)
    qT = dscr("qT", [D, T], BF16)
    kT = dscr("kT", [D, T], BF16)
    vT = dscr("vT", [D, T], BF16)
    gbT = dscr("gbT", [32, T], F32)
    zsT = dscr("zsT", [D, T], BF16)
    yscT = dscr("yscT", [D, T], BF16)
    gaT = dscr("gaT", [D, T], BF16)
    gbgT = dscr("gbgT", [D, T], BF16)
    oT = [dscr(f"oT{d}", [D, T], BF16) for d in range(2)]

    ident_f = k.sb("ident_f", [128, 128], F32)
    ident_b = k.sb("ident_b", [128, 128], BF16)
    ones_b = k.sb("ones_b", [128, 128], BF16)
    ones_f = k.sb("ones_f", [128, 128], F32)
    masks = k.sb("masks", [128, 8, 128], F32)
    zero_b = k.sb("zero_b", [128, 1024], BF16)
    zero_f = k.sb("zero_f", [128, 512], F32)
    k.dma("sp", ident_f[:], c_ident_f[:, :], w=(ident_f,))
    k.dma("sp", masks[:], c_masks.rearrange("m p n -> p m n"), w=(masks,))
    k.copy("dve", ident_b[:], ident_f[:], r=(ident_f,), w=(ident_b,))
    k.memset("dve", ones_b[:], 1.0, w=(ones_b,))
    k.memset("dve", ones_f[:], 1.0, w=(ones_f,))
    k.memset("pool", zero_b[:], 0.0, w=(zero_b,))
    k.memset("pool", zero_f[:], 0.0, w=(zero_f,))

    def load_vec(name, src_ap, nchunk):
        t = k.sb(name, [128, nchunk], F32)
        k.dma("sp", t[:], src_ap.rearrange("(c p) -> p c", p=128), w=(t,))
        return t

    nc_allow = nc.allow_non_contiguous_dma(reason="tiny parameter vectors")
    es.enter_context(nc_allow)

    n1w = [load_vec(f"n1w{l}", norm1_w[l], 8) for l in range(DEPTH)]
    n2w = [load_vec(f"n2w{l}", norm2_w[l], 8) for l in range(DEPTH)]
    fw = load_vec("fw", final_w, 8)
    dcw = []
    scw = []
    dnw = []
    nAexp = []
    dtb = []
    for l in range(DEPTH):
        t = k.sb(f"dcw{l}", [128, 5, 24], F32)
        k.dma("sp", t[:], dn_conv_w[l].rearrange("d (c p) -> p d c", p=128), w=(t,))
        dcw.append(t)
        t = k.sb(f"scw{l}", [128, 3, 8], F32)
        k.dma("sp", t[:], sc_conv_w[l].rearrange("d (c p) -> p d c", p=128), w=(t,))
        scw.append(t)
        t = k.sb(f"dnw{l}", [128, 1], F32)
        k.dma("sp", t[:], dn_norm_w[l].rearrange("(p o) -> p o", o=1), w=(t,))
        dnw.append(t)
        ta = k.sb(f"alog{l}", [16, 1], F32)
        k.dma("sp", ta[:], A_log[l].rearrange("(p o) -> p o", o=1), w=(ta,))
        tb = k.sb(f"dtb{l}", [16, 1], F32)
        k.dma("sp", tb[:], dt_bias[l].rearrange("(p o) -> p o", o=1), w=(tb,))
        dtb.append(tb)
        te = k.sb(f"nAexp{l}", [16, 1], F32)
        k.act(te[:], ta[:], AF.Exp, r=(ta,), w=(te,))
        k.ts("dve", te[:], te[:], -1.0, None, ALU.mult, r=(te,), w=(te,))
        nAexp.append(te)
    eps_t = k.sb("eps_t", [128, 1], F32)
    k.memset("dve", eps_t[:], RMS_EPS, w=(eps_t,))
    eps128_t = k.sb("eps128_t", [128, 1], F32)
    k.memset("dve", eps128_t[:], 128.0 * L2_EPS, w=(eps128_t,))
    one_t = k.sb("one_t", [128, 1], F32)
    k.memset("dve", one_t[:], 1.0, w=(one_t,))

    k.arena_init(196 * 1024)
    CW = 4096
    cf = [k.ar(f"castf{i}", [128, CW], F32) for i in range(3)]
    cb = [k.ar(f"castb{i}", [128, CW], BF16) for i in range(3)]
    cidx = [0]

    def cast_w(dst, src, rows, cols):
        for l in range(DEPTH):
            for r0 in range(0, rows, 128):
                for c0 in range(0, cols, CW):
                    cw = min(CW, cols - c0)
                    i = cidx[0] % 3
                    cidx[0] += 1
                    k.dma("sp", cf[i][:, 0:cw], src[l, r0:r0 + 128, c0:c0 + cw], w=(cf[i],))
                    eng = ("act", "dve", "pool")[i]
                    k.copy(eng, cb[i][:, 0:cw], cf[i][:, 0:cw], r=(cf[i],), w=(cb[i],))
                    k.dma(STQ, dst[l, r0:r0 + 128, c0:c0 + cw], cb[i][:, 0:cw], r=(cb[i],))
    cast_w(wb_in, w_in, D, WIN)
    cast_w(wb_bdn, w_bdn, D, D)
    cast_w(wb_bsc, w_bsc, D, D)
    cast_w(wb_out, w_out, D, D)
    cast_w(wb_gu, w_gu, D, 2 * DFF)
    cast_w(wb_down, w_down, DFF, D)
    k.barrier()

    PADF = cfg.PADF
    if PADF:
        for arr in (qT, kT, vT):
            k.dma("sp", arr.rearrange("(c p) n -> p c n", p=128)[:, :, 0:PADF],
                  zero_b[:, 0:8 * PADF].rearrange("p (c n) -> p c n", c=8), r=(zero_b,))
        k.dma("sp", gbT[:, 0:PADF], zero_f[0:32, 0:PADF], r=(zero_f,))
    xTv = xT.rearrange("(c p) n -> p c n", p=128)
    k.dma("sp", xTv[:, :, 0:2], zero_f[:, 0:16].rearrange("p (c n) -> p c n", c=8), r=(zero_f,))
    k.dma("sp", xTv[:, :, L + 2:L + 4], zero_f[:, 0:16].rearrange("p (c n) -> p c n", c=8), r=(zero_f,))

    PS = [k.ps(f"ps{i}", [128, 512], F32) for i in range(8)]
    psi = [0]

    def nextps():
        t = PS[psi[0] % 8]
        psi[0] += 1
        return t

    k.arena_reset()
    p0_in = [k.ar(f"p0in{i}", [128, 4, D], F32) for i in range(2)]
    p0_out = [k.ar(f"p0out{i}", [128, 8, 512], F32) for i in range(2)]
    it = 0
    for t0 in range(0, L, 512):
        n = min(512, L - t0)
        tin = p0_in[it % 2]
        tout = p0_out[it % 2]
        nb = (n + 127) // 128
        for b in range(nb):
            nn = min(128, n - b * 128)
            k.dma("sp", tin[0:nn, b, :], xin[t0 + b * 128:t0 + b * 128 + nn, :], w=(tin,))
        for c in range(8):
            pt = nextps()
            groups = []
            for b in range(nb):
                nn = min(128, n - b * 128)
                groups.append((pt[:, b * 128:b * 128 + nn], tin[0:nn, b, c * 128:(c + 1) * 128],
                               ident_f[0:nn, 0:nn], True))
            k.mm_multi(pt, groups, r=(tin, ident_f))
            k.copy("act" if c % 2 else "dve", tout[:, c, 0:n], pt[:, 0:n], r=(pt,), w=(tout,))
        k.dma("sp", xTv[:, :, 2 + t0:2 + t0 + n], tout[:, :, 0:n], r=(tout,))
        it += 1
    k.barrier()

    NCMAX = TW + 4

    def nxt(lst, ctr):
        t = lst[ctr[0] % len(lst)]
        ctr[0] += 1
        return t

    psfree = []

    def ps_alloc():
        return psfree.pop(0)

    def ps_free(t):
        psfree.append(t)

    def run_pipeline(tasks, depth, budget=7):
        psfree[:] = list(PS)
        live = []
        pending = list(tasks)
        pi = 0
        while True:
            while pi < len(pending) and len(live) < depth:
                tk = pending[pi]
                nb = getattr(tk, "nb", 0)
                if sum(n for _, n in live) + nb > budget:
                    break
                pi += 1
                g_ = tk()
                if g_ is not None:
                    try:
                        next(g_)
                        live.append((g_, nb))
                    except StopIteration:
                        pass
            if not live:
                if pi >= len(pending):
                    break
                continue
            for ent in list(live):
                try:
                    next(ent[0])
                except StopIteration:
                    live.remove(ent)

    def phaseA(l):
        k.arena_reset()
        xts = [k.ar(f"xt{i}", [128, 8, NCMAX], F32) for i in range(1)]
        hTs = [k.ar(f"hT{i}", [128, 8, NCMAX], BF16) for i in range(2)]
        sq = k.ar("sq", [128, 8, NCMAX], BF16)
        rstd = k.ar("rstd", [128, NCMAX], F32)
        wbuf = [k.ar(f"wbuf{i}", [128, 8, 1024], BF16) for i in range(4)]
        wsm = k.ar("wsm", [128, 8, 32], BF16)
        stg = [k.ar(f"stg{i}", [128, 8, TW], BF16) for i in range(3)]
        tmpA = [k.ar(f"tmpA{i}", [128, NCMAX], F32) for i in range(8)]
        ssq8 = [k.ar(f"ssq8{i}", [128, 8, TW], F32) for i in range(2)]
        tmpB = [k.ar(f"tmpB{i}", [128, NCMAX], BF16) for i in range(8)]

        def ta():
            return tmpA.pop(0)

        def tb():
            return tmpB.pop(0)
        gbs = k.ar("gbs", [16, 2, NCMAX], F32)
        print("arena phase A bytes", k.arena_off)
        wl = wb_in[l]
        fm = lambda arr: arr.rearrange("(c p) n -> p c n", p=128)
        BLK = [C_Q, C_K, C_V, C_Z, C_SC, C_SX, C_SB, C_GA, C_GB]
        nblk = len(BLK)
        wslot = {}
        gblk = [0]

        def t_wload(gi):
            def f():
                wt = wbuf[gi % 4]
                k.dma("sp", wt[:, :, :], wl[:, BLK[gi % nblk]:BLK[gi % nblk] + 1024].rearrange("(c p) n -> p c n", p=128),
                      w=(wt,))
                wslot[gi] = wt
            return f

        def t_xload(ti):
            def f():
                t0, W = cfg.tiles[ti]
                NC = W + 4
                k.dma("sp", xts[0][:, :, 0:NC], xTv[:, :, t0:t0 + NC], w=(xts[0],))
            return f

        def t_pro1(ti):
            def f():
                t0, W = cfg.tiles[ti]
                NC = W + 4
                xt = xts[0]
                k.act(sq[:, :, 0:NC], xt[:, :, 0:NC], AF.Square, r=(xt,), w=(sq,))
            return f

        def t_pro2(ti):
            def f():
                t0, W = cfg.tiles[ti]
                NC = W + 4
                xt, hT = xts[0], hTs[ti % 2]
                pt = ps_alloc()
                k.mm(pt, pt[:, 0:NC], [(ones_b[:], sq[:, c, 0:NC]) for c in range(8)], r=(sq, ones_b))
                k.rsqrt(rstd[:, 0:NC], pt[:, 0:NC], 1.0 / D, eps_t[:], r=(pt, eps_t), w=(rstd,))
                ps_free(pt)
                for c in range(8):
                    k.stt("dve", hT[:, c, 0:NC], xt[:, c, 0:NC], n1w[l][:, c:c + 1],
                          rstd[:, 0:NC], ALU.mult, ALU.mult, r=(xt, rstd, n1w[l]), w=(hT,))
            return f

        def proj(hT, NC, wt, j, M=128):
            pt = ps_alloc()
            k.mm(pt, pt[0:M, 0:NC], [(wt[:, c, j:j + M], hT[:, c, 0:NC]) for c in range(8)], r=(wt, hT))
            return pt

        class Grp:
            def __init__(self, st, dst, pcol, W, n=8, norm=None, ssq=None):
                self.st, self.dst, self.pcol, self.W, self.left = st, dst, pcol, W, n
                self.norm, self.ssq = norm, ssq

            def done(self):
                self.left -= 1
                if self.left == 0:
                    W = self.W
                    if self.norm is not None:
                        sq_ = self.ssq
                        if self.norm == 0:
                            k.rsqrt(sq_[:, :, 0:W], sq_[:, :, 0:W], 128.0, eps128_t[:], r=(sq_, eps128_t), w=(sq_,))
                        else:
                            k.rsqrt(sq_[:, :, 0:W], sq_[:, :, 0:W], 1.0, eps_t[:], r=(sq_, eps_t), w=(sq_,))
                        k.tt("dve", self.st[:, :, 0:W], self.st[:, :, 0:W], sq_[:, :, 0:W], ALU.mult,
                             r=(self.st, sq_), w=(self.st,))
                    k.dma(STQ, fm(self.dst)[:, :, self.pcol:self.pcol + W], self.st[:, :, 0:W],
                          r=(self.st,))

        def t_qkv(ti, gi, grp, c, G):
            def gen():
                t0, W = cfg.tiles[ti]
                NC = W + 4
                hT = hTs[ti % 2]
                wt = wslot[gi]
                st = G.st
                pt = proj(hT, NC, wt, c * 128)
                yield
                cc = grp * 8 + c
                acc = ta()
                k.act(acc[:, 0:W], pt[:, 0:W], AF.Copy, r=(pt, dcw[l]), w=(acc,), scale=dcw[l][:, 0, cc:cc + 1])
                for d in range(1, 5):
                    k.stt("dve", acc[:, 0:W], pt[:, d:d + W], dcw[l][:, d, cc:cc + 1], acc[:, 0:W],
                          ALU.mult, ALU.add, r=(pt, dcw[l], acc), w=(acc,))
                ps_free(pt)
                yield
                if grp == 2:
                    k.act(st[:, c, 0:W], acc[:, 0:W], AF.Silu, r=(acc,), w=(st,))
                    tmpA.append(acc)
                    G.done()
                    return
                k.act(st[:, c, 0:W], acc[:, 0:W], AF.Silu, r=(acc,), w=(st,))
                tmpA.append(acc)
                s2 = tb()
                k.act(s2[:, 0:W], st[:, c, 0:W], AF.Square, r=(st,), w=(s2,))
                yield
                p2 = ps_alloc()
                k.mm(p2, p2[:, 0:W], [(ones_b[:], s2[:, 0:W])], r=(s2, ones_b))
                tmpB.append(s2)
                yield
                k.copy("act", G.ssq[:, c, 0:W], p2[:, 0:W], r=(p2,), w=(G.ssq,))
                ps_free(p2)
                G.done()
            gen.nb = 1
            return gen

        def t_simple(ti, gi, c, G, func):
            def gen():
                t0, W = cfg.tiles[ti]
                NC = W + 4
                pt = proj(hTs[ti % 2], NC, wslot[gi], c * 128)
                yield
                k.act(G.st[:, c, 0:W], pt[:, 2:2 + W], func, r=(pt,), w=(G.st,))
                ps_free(pt)
                G.done()
            gen.nb = 1
            return gen

        def t_bg(ti):
            def gen():
                t0, W = cfg.tiles[ti]
                NC = W + 4
                pcol = t0 + PADF
                hT = hTs[ti % 2]
                k.dma("sp", wsm[:, :, :], wl[:, C_B:C_B + 32].rearrange("(c p) n -> p c n", p=128), w=(wsm,))
                pts = []
                for which in range(2):
                    pt = ps_alloc()
                    k.mm(pt, pt[0:16, 0:NC], [(wsm[:, c, which * 16:which * 16 + 16], hT[:, c, 0:NC])
                                              for c in range(8)], r=(wsm, hT))
                    pts.append(pt)
                yield
                k.act(gbs[:, 0, 0:W], pts[0][0:16, 2:2 + W], AF.Sigmoid, r=(pts[0],), w=(gbs,))
                k.act(gbs[:, 1, 0:W], pts[1][0:16, 2:2 + W], AF.Exp, r=(pts[1], dtb[l]), w=(gbs,), bias=dtb[l][:])
                ps_free(pts[0])
                ps_free(pts[1])
                k.act(gbs[:, 1, 0:W], gbs[:, 1, 0:W], AF.Ln, r=(gbs, one_t), w=(gbs,), bias=one_t[0:16, :])
                yield
                k.ts("dve", gbs[:, 1, 0:W], gbs[:, 1, 0:W], nAexp[l][:, 0:1], None, ALU.mult,
                     r=(gbs, nAexp[l]), w=(gbs,))
                k.dma(STQ, gbT.rearrange("(a p) n -> p a n", p=16)[:, :, pcol:pcol + W], gbs[:, :, 0:W], r=(gbs,))
            gen.nb = 2
            return gen

        def t_sc(ti, gi_c, gi_x, gi_b, c, G):
            def gen():
                t0, W = cfg.tiles[ti]
                NC = W + 4
                hT = hTs[ti % 2]
                pc = proj(hT, NC, wslot[gi_c], c * 128)
                px = proj(hT, NC, wslot[gi_x], c * 128)
                pb = proj(hT, NC, wslot[gi_b], c * 128)
                yield
                cx = ta()
                k.copy("act", cx[:, 0:NC], pc[:, 0:NC], r=(pc,), w=(cx,))
                ps_free(pc)
                yield
                pr = ta()
                k.tt("dve", pr[:, 0:NC], px[:, 0:NC], cx[:, 0:NC], ALU.mult, r=(px, cx), w=(pr,))
                ps_free(px)
                tmpA.append(cx)
                yield
                acc = ta()
                k.ts("dve", acc[:, 0:W], pr[:, 1:1 + W], scw[l][:, 0, c:c + 1], None, ALU.mult,
                     r=(pr, scw[l]), w=(acc,))
                for d in range(1, 3):
                    k.stt("dve", acc[:, 0:W], pr[:, 1 + d:1 + d + W], scw[l][:, d, c:c + 1], acc[:, 0:W],
                          ALU.mult, ALU.add, r=(pr, scw[l], acc), w=(acc,))
                tmpA.append(pr)
                yield
                k.tt("dve", G.st[:, c, 0:W], pb[:, 2:2 + W], acc[:, 0:W], ALU.mult, r=(pb, acc), w=(G.st,))
                ps_free(pb)
                tmpA.append(acc)
                G.done()
            gen.nb = 3
            return gen

        tasks = []
        ntile = len(cfg.tiles)
        stgc = [0]
        total_blocks = ntile * nblk
        tasks.append(t_xload(0))
        for gi in range(4):
            tasks.append(t_wload(gi))
        tasks.append(t_pro1(0))
        tasks.append(t_pro2(0))
        for ti, (t0, W) in enumerate(cfg.tiles):
            pcol = t0 + PADF
            base = ti * nblk

            def post(gi):
                if gi + 4 < total_blocks:
                    tasks.append(t_wload(gi + 4))

            def newG(dst, n=8, norm=None, ssq=None):
                st = stg[stgc[0] % 3]
                stgc[0] += 1
                return Grp(st, dst, pcol, W, n, norm, ssq)
            for grp, dst in enumerate((qT, kT, vT)):
                G = newG(dst, norm=(grp if grp < 2 else None), ssq=(ssq8[grp] if grp < 2 else None))
                for c in range(8):
                    tasks.append(t_qkv(ti, base + grp, grp, c, G))
                post(base + grp)
                if grp == 1 and ti + 1 < ntile:
                    tasks.append(t_xload(ti + 1))
            G = newG(zsT)
            for c in range(8):
                tasks.append(t_simple(ti, base + 3, c, G, AF.Silu))
            post(base + 3)
            tasks.append(t_bg(ti))
            G = newG(yscT)
            for c in range(8):
                tasks.append(t_sc(ti, base + 4, base + 5, base + 6, c, G))
            post(base + 4)
            post(base + 5)
            post(base + 6)
            if ti + 1 < ntile:
                tasks.append(t_pro1(ti + 1))
            for bi, dst in ((7, gaT), (8, gbgT)):
                G = newG(dst)
                for c in range(8):
                    tasks.append(t_simple(ti, base + bi, c, G, AF.Sigmoid))
                post(base + bi)
                if bi == 7 and ti + 1 < ntile:
                    tasks.append(t_pro2(ti + 1))
        run_pipeline(tasks, 7)
        k.barrier()

    def run_threads(gens):
        live = list(gens)
        while live:
            for g_ in list(live):
                try:
                    next(g_)
                except StopIteration:
                    live.remove(g_)

    def v3(ap, a):
        return ap.rearrange("p (a b) -> p a b", a=a)

    def phaseB(l):
        k.arena_reset()
        NCK = cfg.NCK
        qTv = qT.rearrange("(c p) n -> p c n", p=128)
        kTv = kT.rearrange("(c p) n -> p c n", p=128)
        vTv = vT.rearrange("(c p) n -> p c n", p=128)
        B = {}
        CDT = F32 if CHAIN_FP32 else BF16
        identc = ident_f if CHAIN_FP32 else ident_b
        for d in range(2):
            rGa_ = k.ar(f"rGa{d}", [128, 8, 128], F32)
            rGi_ = k.ar(f"rGi{d}", [128, 8, 128], F32)
            for sl in range(2):
                B[d, sl] = dict(
                    kq=k.ar(f"kq{d}{sl}", [128, 8, 2, 128], BF16),
                    vt=k.ar(f"vt{d}{sl}", [128, 8, 128], BF16),
                    gbt=k.ar(f"gbt{d}{sl}", [32, 128], F32),
                    gb=k.ar(f"gb{d}{sl}", [128, 32], F32),
                    E=k.ar(f"E{d}{sl}", [128, 24], F32),
                    bege=k.ar(f"bege{d}{sl}", [128, 8], F32),
                    nbeta=k.ar(f"nbeta{d}{sl}", [128, 8], F32),
                    ost=k.ar(f"ost{d}{sl}", [128, 8, 128], BF16),
                    rGa=rGa_, rGi=rGi_,
                )
                for hh in range(2):
                    B[d, sl, hh] = dict(
                        qkm=k.ar(f"qkm{d}{sl}{hh}", [128, 4, 128], BF16),
                        kdec=k.ar(f"kdec{d}{sl}{hh}", [128, 4, 128], BF16),
                        u=k.ar(f"u{d}{sl}{hh}", [128, 4, 128], F32),
                        wT=k.ar(f"wT{d}{sl}{hh}", [128, 4, 128], BF16),
                        qdT=k.ar(f"qdT{d}{sl}{hh}", [128, 4, 128], BF16),
                    )
            for hh in range(2):
                B["t", d, hh] = dict(
                    Dx=k.ar(f"Dx{d}{hh}", [128, 4, 128], F32),
                    DTx=k.ar(f"DTx{d}{hh}", [128, 4, 128], F32),
                    egcb=k.ar(f"egcb{d}{hh}", [128, 4, 128], F32),
                    U=[k.ar(f"U{d}{hh}{i}", [128, 4, 128], CDT) for i in range(1 if CHAIN_FP32 else 2)],
                    W=[k.ar(f"W{d}{hh}{i}", [128, 4, 128], CDT) for i in range(1 if CHAIN_FP32 else 2)],
                    P=[k.ar(f"P{d}{hh}{i}", [128, 4, 128], CDT) for i in range(1 if CHAIN_FP32 else 2)],
                    Pb=k.ar(f"Pb{d}{hh}", [128, 4, 128], BF16),
                    Pb2=k.ar(f"Pb2{d}{hh}", [128, 4, 128], BF16),
                    ktok=k.ar(f"ktok{d}{hh}", [128, 4, 128], BF16),
                    bkg=k.ar(f"bkg{d}{hh}", [128, 4, 128], BF16),
                    bv=k.ar(f"bv{d}{hh}", [128, 4, 128], BF16),
                    vnew=k.ar(f"vnew{d}{hh}", [128, 4, 128], BF16),
                    Ssc=k.ar(f"Ssc{d}{hh}", [128, 4, 128], F32),
                    S=k.ar(f"S{d}{hh}", [128, 4, 128], F32),
                    Sb=k.ar(f"Sb{d}{hh}", [128, 4, 128], BF16),
                )
                if CHAIN_FP32:
                    tt_ = B["t", d, hh]
                    tt_["U"].append(tt_["Dx"])
                    tt_["W"].append(tt_["DTx"])
                    tt_["P"].append(tt_["egcb"])
                k.memset("pool", B["t", d, hh]["S"][:], 0.0, w=(B["t", d, hh]["S"],))
                k.memset("pool", B["t", d, hh]["Sb"][:], 0.0, w=(B["t", d, hh]["Sb"],))
        print("arena phase B bytes", k.arena_off)

        def setup(d, c, sl):
            b = B[d, sl]
            Mincl, Maft = masks[:, 2 * d, :], masks[:, 2 * d + 1, :]
            cs = slice(c * 128, (c + 1) * 128)
            k.dma("sp", b["kq"][:, :, 0, :], kTv[:, :, cs], w=(b["kq"],))
            k.dma("sp", b["kq"][:, :, 1, :], qTv[:, :, cs], w=(b["kq"],))
            k.dma("sp", b["vt"][:], vTv[:, :, cs], w=(b["vt"],))
            k.dma("sp", b["gbt"][:], gbT[:, cs], w=(b["gbt"],))
            yield
            pt = nextps()
            k.mm_multi(pt, [(pt[:, 0:32], b["gbt"][:], ident_f[0:32, 0:32], True)], r=(b["gbt"], ident_f))
            k.copy("dve", b["gb"][:], pt[:, 0:32], r=(pt,), w=(b["gb"],))
            yield
            beta = b["gb"][:, d * 8:d * 8 + 8]
            g = b["gb"][:, 16 + d * 8:16 + d * 8 + 8]
            pt = nextps()
            k.mm_multi(pt, [(pt[:, 0:8], Mincl, g, False), (pt[:, 8:16], Maft, g, False),
                            (pt[:, 16:24], ones_f[:], g, False)], r=(masks, ones_f, b["gb"]))
            k.act(b["E"][:], pt[:, 0:24], AF.Exp, r=(pt,), w=(b["E"],))
            k.tt("pool", b["rGa"][:], bc(masks[:, 2 * d + 1:2 * d + 2, :], [128, 8, 128]),
                 bc(g.unsqueeze(2), [128, 8, 128]), ALU.mult, r=(masks, b["gb"]), w=(b["rGa"],))
            k.tt("pool", b["rGi"][:], bc(masks[:, 2 * d:2 * d + 1, :], [128, 8, 128]),
                 bc(g.unsqueeze(2), [128, 8, 128]), ALU.mult, r=(masks, b["gb"]), w=(b["rGi"],))
            yield
            k.tt("dve", b["bege"][:], beta, b["E"][:, 0:8], ALU.mult, r=(b["gb"], b["E"]), w=(b["bege"],))
            k.ts("dve", b["nbeta"][:], beta, -1.0, None, ALU.mult, r=(b["gb"],), w=(b["nbeta"],))
            yield

        def prep(d, c, sl, hh):
            b = B[d, sl]
            bh = B[d, sl, hh]
            t = B["t", d, hh]
            hs = slice(4 * hh, 4 * hh + 4)
            Mincl, Maft = masks[:, 2 * d, :], masks[:, 2 * d + 1, :]
            Mincl_bc = bc(masks[:, 2 * d:2 * d + 1, :], [128, 4, 128])
            Maft_bc = bc(masks[:, 2 * d + 1:2 * d + 2, :], [128, 4, 128])
            beta = b["gb"][:, d * 8 + 4 * hh:d * 8 + 4 * hh + 4]
            cr = (lambda a: a.bitcast(mybir.dt.float32r)) if (CHAIN_FP32 and CHAIN_R) else (lambda a: a)
            pD = nextps()
            k.mm(pD, pD[:], [(Mincl, b["rGa"][:, hs, :])], r=(masks, b["rGa"]))
            k.act(v3(t["Dx"][:].rearrange("p a b -> p (a b)"), 4), v3(pD[:], 4), AF.Exp, r=(pD,), w=(t["Dx"],))
            pDT = nextps()
            k.mm(pDT, pDT[:], [(Maft, b["rGi"][:, hs, :])], r=(masks, b["rGi"]))
            k.act(t["DTx"][:], v3(pDT[:], 4), AF.Exp, r=(pDT,), w=(t["DTx"],))
            yield
            pG = nextps()
            k.mm(pG, pG[:], [(ones_f[:], b["rGi"][:, hs, :])], r=(ones_f, b["rGi"]))
            k.act(t["egcb"][:], v3(pG[:], 4), AF.Exp, r=(pG,), w=(t["egcb"],))
            k.tt("pool", t["Dx"][:], t["Dx"][:], Maft_bc, ALU.mult, r=(t["Dx"], masks), w=(t["Dx"],))
            k.tt("pool", t["Dx"][:], t["Dx"][:], bc(b["nbeta"][:, hs].unsqueeze(2), [128, 4, 128]), ALU.mult,
                 r=(t["Dx"], b["nbeta"]), w=(t["Dx"],))
            k.tt("pool", t["DTx"][:], t["DTx"][:], Mincl_bc, ALU.mult, r=(t["DTx"], masks), w=(t["DTx"],))
            yield
            W0 = t["W"][0]
            for pair in range(2):
                pk = nextps()
                groups = []
                for hl in range(2):
                    h = 4 * hh + 2 * pair + hl
                    groups.append((pk[:, hl * 256:(hl + 1) * 256], b["kq"][:, h, 0, :],
                                   b["kq"][:, h, :, :].rearrange("p a b -> p (a b)"), False))
                k.mm_multi(pk, groups, r=(b["kq"],))
                pkv = v3(pk[:], 2)
                k.tt("dve", cr(W0[:, 2 * pair:2 * pair + 2, :]), pkv[:, :, 0:128], t["Dx"][:, 2 * pair:2 * pair + 2, :],
                     ALU.mult, r=(pk, t["Dx"]), w=(W0,))
                k.tt("dve", bh["qkm"][:, 2 * pair:2 * pair + 2, :], pkv[:, :, 128:256],
                     t["DTx"][:, 2 * pair:2 * pair + 2, :], ALU.mult, r=(pk, t["DTx"]), w=(bh["qkm"],))
            yield
            U0 = t["U"][0]
            pt = nextps()
            ptb = v3(pt[:], 4) if CHAIN_FP32 else v3(pt[:].bitcast(BF16)[:, 0:512], 4)
            k.mm_multi(pt, [(ptb[:, hl, :], W0[:, hl, :], identc[:], True) for hl in range(4)], r=(W0, identc))
            k.copy("dve" if CHAIN_R else "act", cr(U0[:]), ptb, r=(pt,), w=(U0,))
            pt = nextps()
            ptb = v3(pt[:].bitcast(BF16)[:, 0:512], 4)
            k.mm_multi(pt, [(ptb[:, hl, :], b["kq"][:, 4 * hh + hl, 0, :], ident_b[:], True) for hl in range(4)],
                       r=(b["kq"], ident_b))
            k.copy("act", t["ktok"][:], ptb, r=(pt,), w=(t["ktok"],))
            pt = nextps()
            ptb = v3(pt[:].bitcast(BF16)[:, 0:512], 4)
            k.mm_multi(pt, [(ptb[:, hl, :], b["vt"][:, 4 * hh + hl, :], ident_b[:], True) for hl in range(4)],
                       r=(b["vt"], ident_b))
            k.tt("dve", t["bv"][:], ptb, bc(beta.unsqueeze(2), [128, 4, 128]), ALU.mult, r=(pt, b["gb"]),
                 w=(t["bv"],))
            yield
            k.tt("pool", t["bkg"][:], t["ktok"][:], bc(b["bege"][:, hs].unsqueeze(2), [128, 4, 128]), ALU.mult,
                 r=(t["ktok"], b["bege"]), w=(t["bkg"],))
            k.tt("pool", bh["kdec"][:], t["ktok"][:], bc(b["E"][:, 8 + 4 * hh:12 + 4 * hh].unsqueeze(2), [128, 4, 128]),
                 ALU.mult, r=(t["ktok"], b["E"]), w=(bh["kdec"],))
            k.tt("pool", bh["qdT"][:], b["kq"][:, hs, 1, :], t["egcb"][:], ALU.mult, r=(b["kq"], t["egcb"]),
                 w=(bh["qdT"],))
            if P_BF16:
                PB = [t["Pb"], t["Pb2"]]
                k.tt("dve", PB[0][:], U0[:], bc(identc[:].unsqueeze(1), [128, 4, 128]), ALU.add,
                     r=(U0, identc), w=(PB[0],))
            else:
                k.tt("dve", cr(t["P"][0][:]), U0[:], bc(identc[:].unsqueeze(1), [128, 4, 128]), ALU.add,
                     r=(U0, identc), w=(t["P"][0],))
            yield
            for lev in range(6):
                Uc, Wc = t["U"][lev % 2], t["W"][lev % 2]
                Un, Wn = t["U"][(lev + 1) % 2], t["W"][(lev + 1) % 2]
                Pc, Pn = t["P"][lev % 2], t["P"][(lev + 1) % 2]
                pB = nextps()
                k.mm_multi(pB, [(pB[:, hl * 128:(hl + 1) * 128], cr(Uc[:, hl, :]), cr(Wc[:, hl, :]), False) for hl in range(4)],
                           r=(Wc, Uc))
                k.copy("dve", cr(Wn[:]), v3(pB[:], 4), r=(pB,), w=(Wn,))
                if P_BF16:
                    Wnb = t["ktok"]
                    k.copy("act", Wnb[:], Wn[:], r=(Wn,), w=(Wnb,))
                yield
                if lev < 5:
                    pA = nextps()
                    if CHAIN_FP32:
                        pav = v3(pA[:], 4)
                    else:
                        pav = v3(pA[:].bitcast(BF16)[:, 0:512], 4)
                    k.mm_multi(pA, [(pav[:, hl, :], Wn[:, hl, :], identc[:], True) for hl in range(4)],
                               r=(Wn, identc))
                    k.copy("act", cr(Un[:]), pav, r=(pA,), w=(Un,))
                pC = nextps()
                if P_BF16:
                    Pc, Pn = PB[lev % 2], PB[(lev + 1) % 2]
                    k.mm_multi(pC, [(pC[:, hl * 128:(hl + 1) * 128], Wnb[:, hl, :], Pc[:, hl, :], False) for hl in range(4)],
                               r=(Wnb, Pc))
                    k.tt("dve", Pn[:], v3(pC[:], 4), Pc[:], ALU.add, r=(pC, Pc), w=(Pn,))
                else:
                    k.mm_multi(pC, [(pC[:, hl * 128:(hl + 1) * 128], cr(Wn[:, hl, :]), cr(Pc[:, hl, :]), False) for hl in range(4)],
                               r=(Wn, Pc))
                    k.tt("dve", cr(Pn[:]), v3(pC[:], 4), Pc[:], ALU.add, r=(pC, Pc), w=(Pn,))
                yield
            Pf = t["P"][0]
            if P_BF16:
                Pf = PB[0]
            elif CHAIN_FP32:
                k.copy("pool", t["Pb"][:], Pf[:], r=(Pf,), w=(t["Pb"],))
                Pf = t["Pb"]
            pu = nextps()
            k.mm_multi(pu, [(pu[:, hl * 128:(hl + 1) * 128], Pf[:, hl, :], t["bv"][:, hl, :], False) for hl in range(4)],
                       r=(Pf, t["bv"]))
            k.copy("act", bh["u"][:], v3(pu[:], 4), r=(pu,), w=(bh["u"],))
            pw = nextps()
            k.mm_multi(pw, [(pw[:, hl * 128:(hl + 1) * 128], t["bkg"][:, hl, :], Pf[:, hl, :], False) for hl in range(4)],
                       r=(Pf, t["bkg"]))
            k.copy("dve", bh["wT"][:], v3(pw[:], 4), r=(pw,), w=(bh["wT"],))
            yield

        def scan(d, c, sl, hh):
            b = B[d, sl]
            bh = B[d, sl, hh]
            t = B["t", d, hh]
            S, Sb = t["S"], t["Sb"]
            pws = nextps()
            k.mm_multi(pws, [(pws[:, hl * 128:(hl + 1) * 128], bh["wT"][:, hl, :], Sb[:, hl, :], False)
                             for hl in range(4)], r=(bh["wT"], Sb))
            k.tt("dve", t["vnew"][:], bh["u"][:], v3(pws[:], 4), ALU.subtract, r=(bh["u"], pws), w=(t["vnew"],))
            k.tt("pool", t["Ssc"][:], S[:], bc(b["E"][:, 16 + 4 * hh:20 + 4 * hh].unsqueeze(2), [128, 4, 128]),
                 ALU.mult, r=(S, b["E"]), w=(t["Ssc"],))
            yield
            po = nextps()

            def fn(g, po=po, Sb=Sb, bh=bh, t=t):
                ins = None
                for hl in range(4):
                    o_ = po[:, hl * 128:(hl + 1) * 128]
                    g.matmul(o_, lhsT=Sb[:, hl, :], rhs=bh["qdT"][:, hl, :], start=True, stop=False)
                    ins = g.matmul(o_, lhsT=t["vnew"][:, hl, :], rhs=bh["qkm"][:, hl, :], start=False, stop=True)
                return ins
            k.op("pe", fn, r=(Sb, bh["qdT"], t["vnew"], bh["qkm"]), w=(po,))
            k.copy("act", b["ost"][:, 4 * hh:4 * hh + 4, :], v3(po[:], 4), r=(po,), w=(b["ost"],))
            pds = nextps()
            k.mm_multi(pds, [(pds[:, hl * 128:(hl + 1) * 128], bh["kdec"][:, hl, :], t["vnew"][:, hl, :], False)
                             for hl in range(4)], r=(bh["kdec"], t["vnew"]))
            k.tt("dve", S[:], t["Ssc"][:], v3(pds[:], 4), ALU.add, r=(t["Ssc"], pds), w=(S,))
            k.copy("act", Sb[:], S[:], r=(S,), w=(Sb,))
            yield

        def store(d, c, sl):
            b = B[d, sl]
            k.dma(STQ, oT[d].rearrange("(h p) n -> p h n", p=128)[:, :, c * 128:(c + 1) * 128], b["ost"][:],
                  r=(b["ost"],))

        def chunk_of(d, s):
            return s if d == 0 else NCK - 1 - s

        run_threads([setup(d, chunk_of(d, 0), 0) for d in range(2)])
        run_threads([prep(d, chunk_of(d, 0), 0, hh) for d in range(2) for hh in range(2)])
        for s in range(NCK):
            sl = s % 2
            th = []
            if s + 1 < NCK:
                run_threads([setup(d, chunk_of(d, s + 1), 1 - sl) for d in range(2)])
                th += [prep(d, chunk_of(d, s + 1), 1 - sl, hh) for d in range(2) for hh in range(2)]
            th += [scan(d, chunk_of(d, s), sl, hh) for d in range(2) for hh in range(2)]
            run_threads(th)
            for d in range(2):
                store(d, chunk_of(d, s), sl)
        k.barrier()


    def phaseC(l, last):
        k.arena_reset()
        x = k.ar("Cx", [128, 8, TW], F32)
        osum = k.ar("Cosum", [128, 8, TW], F32)
        b_of = k.ar("Cof", [128, 8, TW], BF16)
        b_ob = k.ar("Cob", [128, 8, TW], BF16)
        b_zs = k.ar("Czs", [128, 8, TW], BF16)
        b_ysc = k.ar("Cysc", [128, 8, TW], BF16)
        b_ga = k.ar("Cga", [128, 8, TW], BF16)
        b_gb = k.ar("Cgb", [128, 8, TW], BF16)
        a_t = k.ar("Ca", [128, 22, TW], BF16)
        ssq8 = a_t.ap.rearrange("p a b -> p (a b)")[:, 0:16 * TW].bitcast(F32).rearrange("p (a b) -> p a b", a=8)
        wb = [k.ar(f"Cw{i}", [128, 8, 1024], BF16) for i in range(4)]
        wi = [0]
        tA = [k.ar(f"CtA{i}", [128, TW], F32) for i in range(6)]
        tAi = [0]
        tB = [k.ar(f"CtB{i}", [128, TW], BF16) for i in range(2)]
        tBi = [0]
        rs_t = k.ar("Crstd", [128, TW], F32)
        otile = [k.ar(f"Cot{i}", [128, 1024], F32) for i in range(2)]
        oti = [0]
        print("arena phase C bytes", k.arena_off)
        odn, sq2, mg, h2 = b_of, b_ob, b_zs, b_of
        fm = lambda arr: arr.rearrange("(c p) n -> p c n", p=128)

        def wload(src2, rows0, cols0, ncols, nk=8, dst=None, dcol=0):
            wt = dst if dst is not None else nxt(wb, wi)
            k.dma("sp", wt[:, 0:nk, dcol:dcol + ncols],
                  src2[rows0:rows0 + nk * 128, cols0:cols0 + ncols].rearrange("(c p) n -> p c n", p=128), w=(wt,))
            return wt

        for (t0, W) in cfg.tiles:
            pcol = t0 + PADF
            for buf, arr in ((b_ob, oT[1]), (b_of, oT[0]), (b_zs, zsT), (b_ysc, yscT), (b_ga, gaT), (b_gb, gbgT)):
                k.dma("sp", buf[:, :, 0:W], fm(arr)[:, :, pcol:pcol + W], w=(buf,))
            k.dma("sp", x[:, :, 0:W], xTv[:, :, 2 + t0:2 + t0 + W], w=(x,))
            k.tt("dve", osum[:, :, 0:W], b_of[:, :, 0:W], b_ob[:, :, 0:W], ALU.add, r=(b_of, b_ob), w=(osum,))
            k.act(sq2[:, :, 0:W], osum[:, :, 0:W], AF.Square, r=(osum,), w=(sq2,))
            for c in range(8):
                p2 = nextps()
                k.mm(p2, p2[:, 0:W], [(ones_b[:], sq2[:, c, 0:W])], r=(sq2, ones_b))
                k.copy("act", ssq8[:, c, 0:W], p2[:, 0:W], r=(p2,), w=(a_t,))
            k.rsqrt(ssq8[:, :, 0:W], ssq8[:, :, 0:W], 1.0 / 128, eps_t[:], r=(a_t, eps_t), w=(a_t,))
            k.stt("dve", osum[:, :, 0:W], osum[:, :, 0:W], dnw[l][:, 0:1], ssq8[:, :, 0:W], ALU.mult, ALU.mult,
                  r=(osum, dnw[l], a_t), w=(osum,))
            k.tt("dve", odn[:, :, 0:W], osum[:, :, 0:W], b_zs[:, :, 0:W], ALU.mult, r=(osum, b_zs), w=(odn,))
            wdn = wload(wb_bdn[l], 0, 0, 1024)
            wsc = wload(wb_bsc[l], 0, 0, 1024)
            for m in range(8):
                pa = nextps()
                k.mm(pa, pa[:, 0:W], [(wdn[:, c, m * 128:(m + 1) * 128], odn[:, c, 0:W]) for c in range(8)],
                     r=(wdn, odn))
                pb = nextps()
                k.mm(pb, pb[:, 0:W], [(wsc[:, c, m * 128:(m + 1) * 128], b_ysc[:, c, 0:W]) for c in range(8)],
                     r=(wsc, b_ysc))
                t1 = nxt(tA, tAi)
                k.tt("dve", t1[:, 0:W], pa[:, 0:W], b_ga[:, m, 0:W], ALU.mult, r=(pa, b_ga), w=(t1,))
                t2 = nxt(tA, tAi)
                k.tt("dve", t2[:, 0:W], pb[:, 0:W], b_gb[:, m, 0:W], ALU.mult, r=(pb, b_gb), w=(t2,))
                k.tt("dve", mg[:, m, 0:W], t1[:, 0:W], t2[:, 0:W], ALU.add, r=(t1, t2), w=(mg,))
            wo = wload(wb_out[l], 0, 0, 1024)
            for m in range(8):
                pm = nextps()
                k.mm(pm, pm[:, 0:W], [(wo[:, c, m * 128:(m + 1) * 128], mg[:, c, 0:W]) for c in range(8)],
                     r=(wo, mg))
                k.tt("dve", x[:, m, 0:W], x[:, m, 0:W], pm[:, 0:W], ALU.add, r=(x, pm), w=(x,))
                k.act(sq2[:, m, 0:W], x[:, m, 0:W], AF.Square, r=(x,), w=(sq2,))
            pt = nextps()
            k.mm(pt, pt[:, 0:W], [(ones_b[:], sq2[:, c, 0:W]) for c in range(8)], r=(sq2, ones_b))
            k.rsqrt(rs_t[:, 0:W], pt[:, 0:W], 1.0 / D, eps_t[:], r=(pt, eps_t), w=(rs_t,))
            for c in range(8):
                k.stt("dve", h2[:, c, 0:W], x[:, c, 0:W], n2w[l][:, c:c + 1], rs_t[:, 0:W], ALU.mult, ALU.mult,
                      r=(x, rs_t, n2w[l]), w=(h2,))
            for j0 in range(0, NFF, 4):
                nj = min(4, NFF - j0)
                wt = nxt(wb, wi)
                wload(wb_gu[l], 0, j0 * 128, nj * 128, dst=wt, dcol=0)
                wload(wb_gu[l], 0, DFF + j0 * 128, nj * 128, dst=wt, dcol=512)
                for jj in range(nj):
                    j = j0 + jj
                    pg = nextps()
                    k.mm(pg, pg[:, 0:W], [(wt[:, c, jj * 128:(jj + 1) * 128], h2[:, c, 0:W]) for c in range(8)],
                         r=(wt, h2))
                    pu = nextps()
                    k.mm(pu, pu[:, 0:W], [(wt[:, c, 512 + jj * 128:512 + (jj + 1) * 128], h2[:, c, 0:W])
                                          for c in range(8)], r=(wt, h2))
                    sg = nxt(tA, tAi)
                    k.act(sg[:, 0:W], pg[:, 0:W], AF.Silu, r=(pg,), w=(sg,))
                    k.tt("dve", a_t[:, j, 0:W], sg[:, 0:W], pu[:, 0:W], ALU.mult, r=(sg, pu), w=(a_t,))
            wd = [wload(wb_down[l], kb * 1024, 0, 1024, nk=min(8, NFF - kb * 8)) for kb in range(3)]
            for m in range(8):
                pd = nextps()
                k.mm(pd, pd[:, 0:W], [(wd[j // 8][:, j % 8, m * 128:(m + 1) * 128], a_t[:, j, 0:W])
                                      for j in range(NFF)], r=(wd[0], wd[1], wd[2], a_t))
                k.tt("dve", x[:, m, 0:W], x[:, m, 0:W], pd[:, 0:W], ALU.add, r=(x, pd), w=(x,))
            if not last:
                k.dma(STQ, xTv[:, :, 2 + t0:2 + t0 + W], x[:, :, 0:W], r=(x,))
            else:
                k.act(sq2[:, :, 0:W], x[:, :, 0:W], AF.Square, r=(x,), w=(sq2,))
                pt = nextps()
                k.mm(pt, pt[:, 0:W], [(ones_b[:], sq2[:, c, 0:W]) for c in range(8)], r=(sq2, ones_b))
                k.rsqrt(rs_t[:, 0:W], pt[:, 0:W], 1.0 / D, eps_t[:], r=(pt, eps_t), w=(rs_t,))
                xn = osum
                for c in range(8):
                    k.stt("dve", xn[:, c, 0:W], x[:, c, 0:W], fw[:, c:c + 1], rs_t[:, 0:W], ALU.mult, ALU.mult,
                          r=(x, rs_t, fw), w=(xn,))
                lo = max(t0, NMETA)
                while lo < t0 + W:
                    nn = min(128, t0 + W - lo)
                    ot = nxt(otile, oti)
                    for half in range(2):
                        pt = nextps()
                        k.mm_multi(pt, [(pt[0:nn, cc * 128:(cc + 1) * 128],
                                         xn[:, half * 4 + cc, lo - t0:lo - t0 + nn], ident_f[:], True)
                                        for cc in range(4)], r=(xn, ident_f))
                        k.copy("act" if half else "dve", ot[0:nn, half * 512:(half + 1) * 512], pt[0:nn, :],
                               r=(pt,), w=(ot,))
                    k.dma(STQ, out[lo - NMETA:lo - NMETA + nn, :], ot[0:nn, :], r=(ot,))
                    lo += nn
        k.barrier()

    phases = cfg.__dict__.get("phases", "ABC")
    nl = cfg.__dict__.get("nlayers", DEPTH)
    for l in range(nl):
        phaseA(l)
        if "B" in phases:
            phaseB(l)
        if "C" in phases:
            phaseC(l, l == nl - 1)

    k.barrier()
    with nc.Block() as block:
        @block.tensor
        def _(g):
            k.replay("pe", g)

        @block.scalar
        def _(g):
            k.replay("act", g)

        @block.vector
        def _(g):
            k.replay("dve", g)

        @block.gpsimd
        def _(g):
            k.replay("pool", g)

        @block.sync
        def _(g):
            k.replay("sp", g)
    es.close()
    return nc


def make_masks():
    m = np.zeros((8, 128, 128), np.float32)
    t = np.arange(128)[:, None]
    i = np.arange(128)[None, :]
    m[0] = (t <= i)
    m[1] = (t > i)
    m[2] = (t >= i)
    m[3] = (t < i)
    return m


def core_inputs(cfg, inputs, b):
    f = np.float32
    xin = np.concatenate([inputs["meta_tokens"].astype(f), inputs["x"][b].astype(f)], axis=0)
    d = dict(
        xin=np.ascontiguousarray(xin),
        norm1_w=inputs["norm1_w"], w_in=inputs["w_in"], dn_conv_w=inputs["dn_conv_w"],
        A_log=inputs["A_log"].reshape(cfg.depth, 16), dt_bias=inputs["dt_bias"].reshape(cfg.depth, 16),
        dn_norm_w=inputs["dn_norm_w"], sc_conv_w=inputs["sc_conv_w"],
        w_branch_dn=inputs["w_branch_dn"], w_branch_sc=inputs["w_branch_sc"], w_out=inputs["w_out"],
        norm2_w=inputs["norm2_w"], w_gate_up=inputs["w_gate_up"], w_down=inputs["w_down"],
        final_norm_w=inputs["final_norm_w"],
        c_ident_f=np.eye(128, dtype=f), c_masks=make_masks())
    return {k_: np.ascontiguousarray(np.asarray(v, dtype=f)) for k_, v in d.items()}


_NC_CACHE = {}


def kernel(**inputs):
    x = inputs["x"]
    bsz, seq, _ = x.shape
    cfg = Cfg(seq, 2)
    key = (seq,)
    if key not in _NC_CACHE:
        _NC_CACHE[key] = build(cfg)
    nc = _NC_CACHE[key]
    in_maps = [core_inputs(cfg, inputs, b) for b in range(bsz)]
    res = run_bass_kernel_spmd(nc, in_maps, core_ids=list(range(bsz)))
    return np.stack([np.asarray(r["out"], dtype=np.float32) for r in res.results], axis=0)
```
